# Optimizing a Trainium2 kernel written in Bass

```python
import jax, jax.numpy as jnp
from jax import lax
import numpy as np

D_MODEL = 1024
BATCH = 8
SEQ = 4096
DEPTH = 2
DEC_BATCH = 8
DEC_SEQ = 32
PAST_LEN = 4096

CHUNK = 64
N_EVEN = (DEPTH + 1) // 2
N_ODD = DEPTH // 2
POOL_GROUPS = 4
POOL_WINDOWS = (2, 4, 8, 16)
POOL_WIDTH = D_MODEL // 2
POOL_GROUP_DIM = POOL_WIDTH // POOL_GROUPS
POOL_HIST = max(POOL_WINDOWS) - 1
ATT_WIDTH = D_MODEL // 2
ATT_HEADS = 8
ATT_HEAD_DIM = ATT_WIDTH // ATT_HEADS
LEFT_CHUNKS = 8
BAND = LEFT_CHUNKS * CHUNK
REL_CLIP = 128
ML_WIDTH = D_MODEL
ML_HEADS = 4
ML_HEAD_DIM = ML_WIDTH // ML_HEADS
EVEN_IN = 2 * POOL_WIDTH + 4 * ATT_WIDTH
ODD_IN = 5 * ML_WIDTH + 2 * ML_HEADS
RMS_EPS = 1e-6
LN_EPS = 1e-6

kernel_name = "chunk_pool_band_mlstm_stream"


def rmsnorm(x, g):
    xf = x.astype(jnp.float32)
    y = xf * lax.rsqrt(jnp.mean(xf * xf, axis=-1, keepdims=True) + RMS_EPS)
    return (y * g.astype(jnp.float32)).astype(x.dtype)


def rel_bias_lookup(table, rel):
    return table[:, jnp.clip(rel, -REL_CLIP, REL_CLIP) + REL_CLIP]


def multi_scale_pool(u_ext, n_hist, pos0, w_mix, scale):
    B, T_ext, _ = u_ext.shape
    L = T_ext - n_hist
    uf = u_ext.astype(jnp.float32).reshape(B, T_ext, POOL_GROUPS, POOL_GROUP_DIM)
    cs = jnp.concatenate([jnp.zeros_like(uf[:, :1]), jnp.cumsum(uf, axis=1)], axis=1)
    win = jnp.array(POOL_WINDOWS, jnp.int32)
    t = jnp.arange(L)
    hi = n_hist + 1 + t
    lo = jnp.maximum(hi[:, None] - win[None, :], 0)
    grp = jnp.arange(POOL_GROUPS)[None, :]
    win_sum = cs[:, hi] - cs[:, lo, grp]
    count = jnp.minimum(pos0 + t[:, None] + 1, win[None, :]).astype(jnp.float32)
    pooled = win_sum / count[None, :, :, None] - uf[:, n_hist:]
    mixed = jnp.einsum('blgc,gcd->blgd', pooled, w_mix.astype(jnp.float32))
    return (mixed.reshape(B, L, POOL_WIDTH) * scale.astype(jnp.float32)).astype(u_ext.dtype)


def _band_attention_one(q, k, v, table):
    T = q.shape[0]
    n_chunks = T // CHUNK
    band_len = (LEFT_CHUNKS + 1) * CHUNK
    pad = ((BAND, 0), (0, 0), (0, 0))
    kc = jnp.pad(k, pad).reshape(n_chunks + LEFT_CHUNKS, CHUNK, ATT_HEADS, ATT_HEAD_DIM)
    vc = jnp.pad(v, pad).reshape(n_chunks + LEFT_CHUNKS, CHUNK, ATT_HEADS, ATT_HEAD_DIM)
    idx = jnp.arange(n_chunks)[:, None] + jnp.arange(LEFT_CHUNKS + 1)[None, :]
    kb = kc[idx].reshape(n_chunks, band_len, ATT_HEADS, ATT_HEAD_DIM)
    vb = vc[idx].reshape(n_chunks, band_len, ATT_HEADS, ATT_HEAD_DIM)
    qc = q.reshape(n_chunks, CHUNK, ATT_HEADS, ATT_HEAD_DIM)
    s = jnp.einsum('cqhd,ckhd->hcqk', qc, kb).astype(jnp.float32) * (ATT_HEAD_DIM ** -0.5)
    qi = jnp.arange(CHUNK)
    kj = jnp.arange(band_len)
    bias = rel_bias_lookup(table, BAND + qi[:, None] - kj[None, :]).astype(jnp.float32)
    key_chunk = jnp.arange(n_chunks)[:, None] - LEFT_CHUNKS + kj[None, :] // CHUNK
    s = jnp.where((key_chunk >= 0)[None, :, None, :], s + bias[:, None], -jnp.inf)
    p = jax.nn.softmax(s, axis=-1)
    o = jnp.einsum('hcqk,ckhd->cqhd', p.astype(vb.dtype), vb)
    return o.reshape(T, ATT_HEADS, ATT_HEAD_DIM)


def band_attention_prompt(q, k, v, table):
    return lax.map(lambda qkv: _band_attention_one(qkv[0], qkv[1], qkv[2], table), (q, k, v))


def band_attention_sample(q, k_ext, v_ext, table):
    L = q.shape[1]
    W = k_ext.shape[1] - L
    rel = W + jnp.arange(L)[:, None] - jnp.arange(W + L)[None, :]
    bias = rel_bias_lookup(table, rel).astype(jnp.float32)
    s = jnp.einsum('bqhd,bkhd->bhqk', q, k_ext).astype(jnp.float32) * (ATT_HEAD_DIM ** -0.5) + bias[None]
    p = jax.nn.softmax(s, axis=-1)
    return jnp.einsum('bhqk,bkhd->bqhd', p.astype(v_ext.dtype), v_ext)


def even_mixer(h, w_in, w_mix, scale, table, w_out, pool_hist=None, k_hist=None, v_hist=None):
    B, L, _ = h.shape
    P, A = POOL_WIDTH, ATT_WIDTH
    proj = h @ w_in
    u, q, k, v, g_pool, g_att = jnp.split(proj, [P, P + A, P + 2 * A, P + 3 * A, 2 * P + 3 * A], axis=-1)
    q = q.reshape(B, L, ATT_HEADS, ATT_HEAD_DIM)
    k = k.reshape(B, L, ATT_HEADS, ATT_HEAD_DIM)
    v = v.reshape(B, L, ATT_HEADS, ATT_HEAD_DIM)
    if pool_hist is None:
        pool_out = multi_scale_pool(u, 0, 0, w_mix, scale)
        att = band_attention_prompt(q, k, v, table)
        keep = min(BAND, L)
        new_pool, new_k, new_v = u[:, L - POOL_HIST:], k[:, L - keep:], v[:, L - keep:]
    else:
        u_ext = jnp.concatenate([pool_hist.astype(u.dtype), u], axis=1)
        pool_out = multi_scale_pool(u_ext, POOL_HIST, PAST_LEN, w_mix, scale)
        k_ext = jnp.concatenate([k_hist.astype(k.dtype), k], axis=1)
        v_ext = jnp.concatenate([v_hist.astype(v.dtype), v], axis=1)
        att = band_attention_sample(q, k_ext, v_ext, table)
        new_pool, new_k, new_v = u_ext[:, L:], k, v
    mixed = jnp.concatenate([pool_out * jax.nn.silu(g_pool),
                             att.reshape(B, L, A) * jax.nn.silu(g_att)], axis=-1)
    return mixed @ w_out, new_pool, new_k, new_v


def mlstm_chunk(carry, xs):
    C, n, m = carry
    q, k, v, ig, lf = xs
    L = q.shape[2]
    b = jnp.cumsum(lf, axis=-1)
    causal = jnp.tril(jnp.ones((L, L), bool))
    dmat = jnp.where(causal, b[..., :, None] - b[..., None, :] + ig[..., None, :], -jnp.inf)
    inter = b + m[..., None]
    m_t = jnp.maximum(inter, jnp.max(dmat, axis=-1))
    a = jnp.exp(inter - m_t)
    s = jnp.einsum('bhtd,bhsd->bhts', q, k) * jnp.exp(dmat - m_t[..., None])
    num = a[..., None] * jnp.einsum('bhvk,bhtk->bhtv', C, q) + jnp.einsum('bhts,bhsv->bhtv', s, v)
    den = a * jnp.einsum('bhk,bhtk->bht', n, q) + jnp.sum(s, axis=-1)
    h = num / jnp.maximum(jnp.abs(den), jnp.exp(-m_t))[..., None]
    g = b[..., -1:] - b + ig
    m_new = jnp.maximum(b[..., -1] + m, jnp.max(g, axis=-1))
    decay = jnp.exp(b[..., -1] + m - m_new)
    wgt = jnp.exp(g - m_new[..., None])
    C_new = decay[..., None, None] * C + jnp.einsum('bhsv,bhsk->bhvk', v * wgt[..., None], k)
    n_new = decay[..., None] * n + jnp.einsum('bhs,bhsk->bhk', wgt, k)
    return (C_new, n_new, m_new), h


def odd_mixer(h, w_in, b_gate, gain, w_out, state=None):
    B, L, _ = h.shape
    W = ML_WIDTH
    proj = h @ w_in
    q, k, v, o, z, gates = jnp.split(proj, [W, 2 * W, 3 * W, 4 * W, 5 * W], axis=-1)

    def heads(t):
        return t.astype(jnp.float32).reshape(B, L, ML_HEADS, ML_HEAD_DIM).transpose(0, 2, 1, 3)

    q, k, v = heads(q), heads(k) * (ML_HEAD_DIM ** -0.5), heads(v)
    gates = gates.astype(jnp.float32) + b_gate.astype(jnp.float32)
    ig = gates[..., :ML_HEADS].transpose(0, 2, 1)
    lf = jax.nn.log_sigmoid(gates[..., ML_HEADS:]).transpose(0, 2, 1)
    if state is None:
        n_chunks = L // CHUNK

        def to_chunks(t):
            return jnp.moveaxis(t.reshape(t.shape[:2] + (n_chunks, CHUNK) + t.shape[3:]), 2, 0)

        init = (jnp.zeros((B, ML_HEADS, ML_HEAD_DIM, ML_HEAD_DIM), jnp.float32),
                jnp.zeros((B, ML_HEADS, ML_HEAD_DIM), jnp.float32),
                jnp.zeros((B, ML_HEADS), jnp.float32))
        xs = (to_chunks(q), to_chunks(k), to_chunks(v), to_chunks(ig), to_chunks(lf))
        (C, n, m), hc = lax.scan(mlstm_chunk, init, xs)
        hc = jnp.moveaxis(hc, 0, 2).reshape(B, ML_HEADS, L, ML_HEAD_DIM)
    else:
        C0, n0, m0 = state
        init = (C0.astype(jnp.float32), n0.astype(jnp.float32), m0.astype(jnp.float32))
        (C, n, m), hc = mlstm_chunk(init, (q, k, v, ig, lf))
    mu = jnp.mean(hc, axis=-1, keepdims=True)
    var = jnp.mean(jnp.square(hc - mu), axis=-1, keepdims=True)
    hn = (hc - mu) * lax.rsqrt(var + LN_EPS)
    hn = hn.transpose(0, 2, 1, 3).reshape(B, L, W) * gain.astype(jnp.float32)
    out = (hn * jax.nn.sigmoid(o.astype(jnp.float32)) * jax.nn.silu(z.astype(jnp.float32))).astype(h.dtype)
    return out @ w_out, C, n, m


def setup_inputs(seed: int = 0) -> dict:
    key = jax.random.key(seed)
    ks = jax.random.split(key, 20)
    nrm = jax.random.normal
    f32 = jnp.float32
    cache_rows = min(BAND, PAST_LEN)
    b_i = 0.1 * nrm(ks[16], (N_ODD, ML_HEADS), f32)
    b_f = jnp.linspace(3.0, 6.0, ML_HEADS, dtype=f32)[None, :] + 0.1 * nrm(ks[17], (N_ODD, ML_HEADS), f32)
    return {
        "x_prompt": nrm(ks[0], (BATCH, SEQ, D_MODEL), f32),
        "x_sample": nrm(ks[1], (DEC_BATCH, DEC_SEQ, D_MODEL), f32),
        "cache_pool": nrm(ks[2], (N_EVEN, DEC_BATCH, POOL_HIST, POOL_WIDTH), f32),
        "cache_k": nrm(ks[3], (N_EVEN, DEC_BATCH, cache_rows, ATT_HEADS, ATT_HEAD_DIM), f32),
        "cache_v": nrm(ks[4], (N_EVEN, DEC_BATCH, cache_rows, ATT_HEADS, ATT_HEAD_DIM), f32),
        "state_C": 0.05 * nrm(ks[5], (N_ODD, DEC_BATCH, ML_HEADS, ML_HEAD_DIM, ML_HEAD_DIM), f32),
        "state_n": 0.5 * jnp.abs(nrm(ks[6], (N_ODD, DEC_BATCH, ML_HEADS, ML_HEAD_DIM), f32)),
        "state_m": nrm(ks[7], (N_ODD, DEC_BATCH, ML_HEADS), f32),
        "norm_pre": 1.0 + 0.05 * nrm(ks[8], (DEPTH, D_MODEL), f32),
        "norm_post": 1.0 + 0.05 * nrm(ks[9], (DEPTH, D_MODEL), f32),
        "w_in_even": nrm(ks[10], (N_EVEN, D_MODEL, EVEN_IN), f32) * D_MODEL ** -0.5,
        "w_pool_mix": nrm(ks[11], (N_EVEN, POOL_GROUPS, POOL_GROUP_DIM, POOL_GROUP_DIM), f32) * POOL_GROUP_DIM ** -0.5,
        "pool_scale": 1.0 + 0.1 * nrm(ks[12], (N_EVEN, POOL_WIDTH), f32),
        "rel_bias": 0.5 * nrm(ks[13], (N_EVEN, ATT_HEADS, 2 * REL_CLIP + 1), f32),
        "w_out_even": nrm(ks[14], (N_EVEN, POOL_WIDTH + ATT_WIDTH, D_MODEL), f32) * (POOL_WIDTH + ATT_WIDTH) ** -0.5,
        "w_in_odd": nrm(ks[15], (N_ODD, D_MODEL, ODD_IN), f32) * D_MODEL ** -0.5,
        "b_gate_odd": jnp.concatenate([b_i, b_f], axis=-1),
        "mlstm_norm": 1.0 + 0.05 * nrm(ks[18], (N_ODD, ML_WIDTH), f32),
        "w_out_odd": nrm(ks[19], (N_ODD, ML_WIDTH, D_MODEL), f32) * ML_WIDTH ** -0.5,
    }


def reference(x_prompt, x_sample, cache_pool, cache_k, cache_v, state_C, state_n, state_m,
              norm_pre, norm_post, w_in_even, w_pool_mix, pool_scale, rel_bias, w_out_even,
              w_in_odd, b_gate_odd, mlstm_norm, w_out_odd):
    xp, xs = x_prompt, x_sample
    pool_p, k_p, v_p, C_p, n_p, m_p = [], [], [], [], [], []
    pool_s, k_s, v_s, C_s, n_s, m_s = [], [], [], [], [], []
    for layer in range(DEPTH):
        if layer % 2 == 0:
            e = layer // 2
            wts = (w_in_even[e], w_pool_mix[e], pool_scale[e], rel_bias[e], w_out_even[e])
            yp, pp, kp, vp = even_mixer(rmsnorm(xp, norm_pre[layer]), *wts)
            ys, ps, ks_, vs = even_mixer(rmsnorm(xs, norm_pre[layer]), *wts,
                                         pool_hist=cache_pool[e], k_hist=cache_k[e], v_hist=cache_v[e])
            pool_p.append(pp); k_p.append(kp); v_p.append(vp)
            pool_s.append(ps); k_s.append(ks_); v_s.append(vs)
        else:
            o = layer // 2
            wts = (w_in_odd[o], b_gate_odd[o], mlstm_norm[o], w_out_odd[o])
            yp, cp, np_, mp = odd_mixer(rmsnorm(xp, norm_pre[layer]), *wts)
            ys, cs_, ns, ms = odd_mixer(rmsnorm(xs, norm_pre[layer]), *wts,
                                        state=(state_C[o], state_n[o], state_m[o]))
            C_p.append(cp); n_p.append(np_); m_p.append(mp)
            C_s.append(cs_); n_s.append(ns); m_s.append(ms)
        xp = xp + rmsnorm(yp, norm_post[layer])
        xs = xs + rmsnorm(ys, norm_post[layer])
    return (xp, xs,
            jnp.stack(pool_p), jnp.stack(k_p), jnp.stack(v_p),
            jnp.stack(C_p), jnp.stack(n_p), jnp.stack(m_p),
            jnp.stack(pool_s), jnp.stack(k_s), jnp.stack(v_s),
            jnp.stack(C_s), jnp.stack(n_s), jnp.stack(m_s))
```

```python
import numpy as np
from contextlib import ExitStack
import concourse.bass as bass
import concourse.mybir as mybir
from concourse.bass_utils import run_bass_kernel_spmd

F32 = mybir.dt.float32
BF16 = mybir.dt.bfloat16
ALU = mybir.AluOpType
AF = mybir.ActivationFunctionType

D = 1024
T = 4096
NS = 32
NMT = 8
POOLW = (2, 4, 8, 16)
EPS = 1e-6


class Buf:
    __slots__ = ("name", "lw", "readers", "dsem", "excl")

    def __init__(self, name="", excl=False):
        self.name = name
        self.excl = excl
        self.lw = None
        self.readers = {}
        self.dsem = None


class Sched:
    ENG = ("pe", "act", "dve", "pool", "sp")

    def __init__(self, nc, stack):
        self.nc = nc
        self.stack = stack
        self.prog = {e: [] for e in self.ENG}
        self.sems = {}
        self.issued = {}
        self.isdma = {}
        self.seen = {e: {} for e in self.ENG}
        for e in ("pe", "act", "dve", "pool"):
            self._mksem(e, False)
        self.ndma = 0

    def _mksem(self, key, isdma):
        self.sems[key] = self.stack.enter_context(self.nc.semaphore("s_" + key))
        self.issued[key] = 0
        self.isdma[key] = isdma

    def _waits(self, eng, reads, writes):
        need = {}

        def add(k, v):
            if self.isdma[k]:
                v = self.issued[k]
            if v > need.get(k, 0):
                need[k] = v
        for b in reads:
            if b.lw is not None:
                add(*b.lw)
            if b.excl:
                for k, v in b.readers.items():
                    if k != eng:
                        add(k, v)
        for b in writes:
            if b.lw is not None:
                add(*b.lw)
            for k, v in b.readers.items():
                add(k, v)
        out = []
        for k, v in need.items():
            if k == "pe" and eng == "pe":
                continue
            if self.seen[eng].get(k, 0) >= v:
                continue
            self.seen[eng][k] = v
            out.append((k, v))
        return out

    def _record(self, ev, reads, writes):
        k, v = ev
        for b in reads:
            if b.readers.get(k, 0) < v:
                b.readers[k] = v
        for b in writes:
            b.lw = ev
            b.readers = {}

    def op(self, eng, fn, reads=(), writes=(), inc=True):
        waits = self._waits(eng, reads, writes)
        if inc:
            self.issued[eng] += 1
            ev = (eng, self.issued[eng])
        else:
            ev = (eng, self.issued[eng] + 1)
        self.prog[eng].append((waits, fn, eng if inc else None))
        self._record(ev, reads, writes)
        return ev

    def dma(self, q, out_ap, in_ap, reads=(), writes=(), owner=None, **kw):
        waits = self._waits(q, reads, writes)
        if owner is None:
            owner = (list(writes) + list(reads))[0]
        if owner.dsem is None:
            self.ndma += 1
            owner.dsem = "d%d" % self.ndma
            self._mksem(owner.dsem, True)
        semkey = owner.dsem
        self.issued[semkey] += 16
        ev = (semkey, self.issued[semkey])

        def fn(e, out_ap=out_ap, in_ap=in_ap, kw=kw):
            return e.dma_start(out=out_ap, in_=in_ap, **kw)
        self.prog[q].append((waits, fn, semkey))
        self._record(ev, reads, writes)
        return ev

    def barrier(self):
        for e in self.ENG:
            waits = []
            for k, v in self.issued.items():
                if v > self.seen[e].get(k, 0):
                    self.seen[e][k] = v
                    waits.append((k, v))
            self.prog[e].append((waits, None, None))

    def emit(self):
        nc = self.nc
        prog = self.prog
        self.prog = {e: [] for e in self.ENG}
        with nc.Block() as block:
            def run(e, items):
                for waits, fn, inck in items:
                    for k, v in waits:
                        e.wait_ge(self.sems[k], v)
                    if fn is None:
                        continue
                    ins = fn(e)
                    if inck is not None:
                        ins.then_inc(self.sems[inck], 16 if self.isdma[inck] else 1)

            @block.tensor
            def _(e):
                run(e, prog["pe"])

            @block.scalar
            def _(e):
                run(e, prog["act"])

            @block.vector
            def _(e):
                run(e, prog["dve"])

            @block.gpsimd
            def _(e):
                run(e, prog["pool"])

            @block.sync
            def _(e):
                run(e, prog["sp"])


def build_nc(stage=2):
    nc = bass.Bass("TRN2", target_bir_lowering=False)

    def din(name, shape):
        return nc.dram_tensor(name, shape, F32, kind="ExternalInput").ap()

    def dout(name, shape):
        return nc.dram_tensor(name, shape, F32, kind="ExternalOutput").ap()
    xp = din("xp", [T, D]); xs = din("xs", [NS, D])
    cpool = din("cpool", [15, 512]); ck = din("ck", [512, 512]); cv = din("cv", [512, 512])
    sC = din("sC", [4, 256, 256]); sn = din("sn", [4, 256]); sm = din("sm", [4, 1])
    gpre = din("gpre", [128, 2, 8]); gpost = din("gpost", [128, 2, D])
    w0in = din("w0in", [128, 8, 3072]); w0out = din("w0out", [128, 8, D])
    wmix = din("wmix", [128, 4, 128]); pscale = din("pscale", [128, 4])
    biasT = din("biasT", [128, 8, 5, 128]); mask5 = din("mask5", [128, 5, 128]); biasS = din("biasS", [128, 8, 5, NS])
    corr = din("corr", [128, 4, 16]); ident = din("ident", [128, 128])
    w1in = din("w1in", [128, 8, 5128]); w1out = din("w1out", [128, 8, D])
    bgate = din("bgate", [128, 8]); gml = din("gml", [128, D])
    tri = din("tri", [128, 128]); sel = din("sel", [4, 4, 128])
    yp = dout("yp", [T, D]); ys = dout("ys", [NS, D])
    o_pool_p = dout("pool_p", [15, 512]); o_k_p = dout("k_p", [512, 512]); o_v_p = dout("v_p", [512, 512])
    o_C_p = dout("C_p", [4, 256, 256]); o_n_p = dout("n_p", [4, 256]); o_m_p = dout("m_p", [4, 1])
    o_pool_s = dout("pool_s", [15, 512]); o_k_s = dout("k_s", [NS, 512]); o_v_s = dout("v_s", [NS, 512])
    o_C_s = dout("C_s", [4, 256, 256]); o_n_s = dout("n_s", [4, 256]); o_m_s = dout("m_s", [4, 1])
    x1d = nc.dram_tensor("x1d", [T, D], F32, kind="Internal").ap()
    bx1d = [Buf("x1d%d" % i) for i in range(4 * NMT)]

    with ExitStack() as st:
        S = Sched(nc, st)

        def mk(stack):
            def sb(name, shape, dt=F32):
                return stack.enter_context(nc.sbuf_tensor(name, shape, dt))

            def ps(name, shape, dt=F32):
                return stack.enter_context(nc.psum_tensor(name, shape, dt))
            return sb, ps
        sb, ps = mk(st)

        bWs = [Buf("W%d" % i) for i in range(13)]

        def wb(c0, c1):
            return bWs[c0 // 512:(c1 - 1) // 512 + 1]
        stg = sb("stg", [128, 4, 1536]); bstg = [Buf("stg%d" % i) for i in range(4)]
        identf = sb("identf", [128, 128]); bidf = Buf("idf")
        identb = sb("identb", [128, 128], BF16); bidb = Buf("idb")
        gpre_t = sb("gpre_t", [128, 2, 8]); bgpre = Buf("gpre")
        gpost_t = sb("gpost_t", [128, 1, D]); bgpost = Buf("gpost")
        x1s_t = sb("x1s_t", [NS, D]); bx1s = Buf("x1s")
        junk = sb("junk", [128, D], BF16); bjunk = Buf("junk")
        ysb = sb("ysb", [128, 2, D]); bysb = [Buf("ysb%d" % i) for i in range(2)]
        h = sb("h", [128, 2, D], BF16); bh = [Buf("h%d" % i) for i in range(2)]
        ms = sb("ms", [128, 40]); rs = sb("rs", [128, 40])
        bms = [Buf("ms%d" % i) for i in range(10)]; brs = [Buf("rs%d" % i) for i in range(10)]
        pT = ps("pT", [128, 8, 128], BF16); bpT = Buf("pT", True)
        pbb = ps("pbb", [128, 2, 512]); pb = [pbb[:, 0, :], pbb[:, 1, :]]; bpb = [Buf("pb%d" % i, True) for i in range(2)]
        pST = [ps("pST%d" % i, [128, 2, 512]) for i in range(2)]; bpST = [Buf("pST%d" % i, True) for i in range(2)]
        pPV = ps("pPV", [128, 512]); bpPV = Buf("pPV", True)
        pbi = [0]

        pbl = [(pb[0], bpb[0]), (pb[1], bpb[1])]

        def nextpb():
            i = pbi[0] % len(pbl)
            pbi[0] += 1
            return pbl[i]

        def mm(out, lhsT, rhs, start, stop, reads, writes, inc):
            S.op("pe", lambda e: e.matmul(out=out, lhsT=lhsT, rhs=rhs, start=start, stop=stop), reads, writes, inc)

        def tp(out, in_, idn, reads, writes, inc):
            S.op("pe", lambda e: e.transpose(out=out, in_=in_, identity=idn), reads, writes, inc)

        def acopy(out, in_, reads, writes):
            S.op("act", lambda e: e.copy(out=out, in_=in_), reads, writes)

        def aact(out, in_, func, reads, writes, scale=1.0, bias=None, accum=None):
            kw = {}
            if bias is not None:
                kw["bias"] = bias
            if accum is not None:
                kw["accum_out"] = accum
            S.op("act", lambda e: e.activation(out=out, in_=in_, func=func, scale=scale, **kw), reads, writes)

        def vcopy(out, in_, reads, writes, eng="dve"):
            S.op(eng, lambda e: e.tensor_copy(out=out, in_=in_), reads, writes)

        def vtt(out, in0, in1, op, reads, writes, eng="dve"):
            S.op(eng, lambda e: e.tensor_tensor(out=out, in0=in0, in1=in1, op=op), reads, writes)

        def vts(out, in0, s1, s2, op0, op1, reads, writes):
            if s2 is None:
                S.op("dve", lambda e: e.tensor_scalar(out=out, in0=in0, scalar1=s1, scalar2=None, op0=op0), reads, writes)
            else:
                S.op("dve", lambda e: e.tensor_scalar(out=out, in0=in0, scalar1=s1, scalar2=s2, op0=op0, op1=op1), reads, writes)

        def vstt(out, in0, scalar, in1, op0, op1, reads, writes):
            S.op("dve", lambda e: e.scalar_tensor_tensor(out=out, in0=in0, scalar=scalar, in1=in1, op0=op0, op1=op1), reads, writes)

        def vrecip(out, in_, reads, writes):
            S.op("dve", lambda e: e.reciprocal(out=out, in_=in_), reads, writes)

        def memset(eng, ap, val, writes):
            S.op(eng, lambda e: e.memset(ap, val), (), writes)

        S.dma("sp", identf[:], ident, writes=[bidf])
        S.dma("sp", gpre_t[:], gpre, writes=[bgpre])
        S.dma("sp", gpost_t[:, 0, :], gpost[:, 0, :], writes=[bgpost])
        vcopy(identb[:], identf[:], [bidf], [bidb])

        def load_weights(warena, src, ncols, dst_col0, layer, kscale_cols=None, chunks=None, on_pool=True):
            if chunks is None:
                chunks = [(c0, min(1024, ncols - c0)) for c0 in range(0, ncols, 1024)]
            for c0, cw in chunks:
                wr = wb(dst_col0 + c0, dst_col0 + c0 + cw)
                extra = None
                if kscale_cols is not None and kscale_cols[0] <= c0 < kscale_cols[1]:
                    extra = kscale_cols[2]
                for kc in range(8):
                    s = load_weights.si % 4
                    load_weights.si += 1
                    S.dma("sp", stg[:, s, 0:cw], src[:, kc, c0:c0 + cw], writes=[bstg[s]])
                    o = warena[:, kc, dst_col0 + c0:dst_col0 + c0 + cw]
                    i = stg[:, s, 0:cw]
                    gsrc = None if layer is None else (gpre16 if extra is not None else gpre_t[:, layer, :])
                    if on_pool:
                        if layer is None:
                            vcopy(o, i, [bstg[s]], wr, eng="pool")
                        else:
                            vtt(o, i, gsrc[:, kc:kc + 1].to_broadcast([128, cw]), ALU.mult, [bstg[s], bgpre], wr, eng="pool")
                    elif layer is None:
                        if load_weights.si % 2:
                            vcopy(o, i, [bstg[s]], wr)
                        else:
                            acopy(o, i, [bstg[s]], wr)
                    elif load_weights.si % 2:
                        vts(o, i, gsrc[:, kc:kc + 1], None, ALU.mult, None, [bstg[s], bgpre], wr)
                    else:
                        aact(o, i, AF.Copy, [bstg[s], bgpre], wr, scale=gsrc[:, kc:kc + 1])
        load_weights.si = 0
        gpre16 = sb("gpre16", [128, 8])
        vts(gpre16[:], gpre_t[:, 1, :], 1.0 / 16.0, None, ALU.mult, None, [bgpre], [bgpre])

        with ExitStack() as st0:
            sb0, _ = mk(st0)
            W0 = sb0("W0", [128, 8, 4096], BF16)
            xr = stg[:, :, 0:D]; bxr = bstg
            xq = sb0("xq", [128, 2, D]); bxq = [Buf("xq%d" % i) for i in range(2)]
            hT = sb0("hT", [128, 8, 512], BF16); bhT = Buf("hT")
            uT = sb0("uT", [128, 4, 528]); buT = Buf("uT")
            qT = sb0("qT", [128, 4, 512], BF16); bqT = Buf("qT")
            kT = sb0("kT", [128, 4, 1024], BF16); bkT = [Buf("kT%d" % i) for i in range(2)]
            gpT = sb0("gpT", [128, 4, 512], BF16); bgpT = Buf("gpT")
            Vr = sb0("Vr", [128, 8, 8, 65], BF16); bV = [Buf("V%d" % i) for i in range(8)]
            gatt = sb0("gatt", [128, 4, 512], BF16); bgatt = [Buf("gatt%d" % i) for i in range(4)]
            pooledT = sb0("pooledT", [128, 4, 512], BF16); bpooled = Buf("pooled")
            tA = sb0("tA", [128, 2, 528]); btA = [Buf("tA0"), Buf("tA1")]
            ostg = tA[:, :, 0:512]; bostg = btA
            tB = sb0("tB", [128, 2, 528]); btB = [Buf("tB0"), Buf("tB1")]
            mixedT = sb0("mixedT", [128, 8, 512], BF16); bmixed = Buf("mixedT")
            E5 = sb0("E5", [128, 8, 5, 128], BF16); bE5 = Buf("E5")
            expo2 = sb0("expo", [128, 2, 640], BF16); bexpo = [Buf("expo0"), Buf("expo1"), Buf("expo2")]
            PT2 = sb0("PT", [128, 2, 640], BF16); bPT = [Buf("PT0"), Buf("PT1"), Buf("PT2")]
            expoS = [expo2[:, 0, :], expo2[:, 1, :], stg[:, 0, 1024:1536].bitcast(BF16)[:, 0:640]]
            PTS = [PT2[:, 0, :], PT2[:, 1, :], stg[:, 1, 1024:1536].bitcast(BF16)[:, 0:640]]
            ST3 = [pST[0], pST[1], pbb]
            bST3 = [[bpST[0]], [bpST[1]], bpb]
            rc = sb0("rc", [128, 2, 4]); brc = [Buf("rc0"), Buf("rc1")]
            tmpn = sb0("tmpn", [128, 2, 256]); btmpn = [Buf("tn0"), Buf("tn1")]
            attg = sb0("attg", [128, 512], BF16); battg = Buf("attg")
            wmixb = sb0("wmixb", [128, 4, 128], BF16); bwmix = Buf("wmix")
            pscale_t = sb0("pscale_t", [128, 4]); bpscale = Buf("pscale")
            corr_t = sb0("corr_t", [128, 4, 16]); bcorr = Buf("corr")
            hTs = sb0("hTs", [128, 8, NS], BF16); bhTs = Buf("hTs")
            uTs = sb0("uTs", [128, 4, 48]); buTs = Buf("uTs")
            qTs = sb0("qTs", [128, 4, NS], BF16); bqTs = Buf("qTs")
            kTs = sb0("kTs", [128, 4, NS], BF16); bkTs = Buf("kTs")
            cstage = stg[:, 0:2, :].rearrange("p a b -> p (a b)")[:, 0:2048].rearrange("p (b f) -> p b f", b=4)

            wmixf = tB[:, 0, 0:512].rearrange("p (g d) -> p g d", g=4)
            S.dma("sp", wmixf, wmix, writes=[btB[0]])
            vcopy(wmixb[:], wmixf, [btB[0]], [bwmix])
            mask_f = tA[:].rearrange("p a b -> p (a b)")[:, 0:640]
            bmask = btA
            S.dma("sp", pscale_t[:], pscale, writes=[bpscale])
            S.dma("sp", corr_t[:], corr, writes=[bcorr])
            S.dma("sp", mask_f, mask5.rearrange("p d q -> p (d q)"), writes=bmask)
            for hh in range(8):
                s = hh % 4
                S.dma("sp", stg[:, s, 0:640], biasT[:, hh, :, :].rearrange("p d q -> p (d q)"), writes=[bstg[s]])
                aact(stg[:, s, 0:640], stg[:, s, 0:640], AF.Exp, [bstg[s]], [bstg[s]])
                vtt(E5[:, hh, :, :].rearrange("p d q -> p (d q)"), stg[:, s, 0:640], mask_f, ALU.mult,
                    [bstg[s]] + bmask, [bE5])
            memset("pool", Vr[:, :, :, 64:65], 1.0, bV)
            memset("pool", uT[:, :, 0:16], 0.0, [buT])

            def norm_stats(x_ap, nt, col, bx, bm_):
                aact(junk[0:nt, :], x_ap, AF.Square, [bx], [bjunk, bm_], scale=1.0 / 32.0, accum=ms[0:nt, col:col + 1])

            def norm_rstd(nt, c0, c1, bm_, br_):
                aact(rs[0:nt, c0:c1], ms[0:nt, c0:c1], AF.Ln, [bm_], [br_], bias=EPS)
                aact(rs[0:nt, c0:c1], rs[0:nt, c0:c1], AF.Exp, [br_], [br_], scale=-0.5)

            def make_hT(x_ap, nt, col, bx, br_, hslot, hT_out, bhT_out):
                hb = h[0:nt, hslot, :]
                vts(hb, x_ap, rs[0:nt, col:col + 1], None, ALU.mult, None, [bx, br_], [bh[hslot]])
                for kc in range(8):
                    tp(pT[:, kc, 0:nt], hb[:, kc * 128:(kc + 1) * 128], identb[0:nt, 0:nt], [bh[hslot], bidb], [bpT], kc == 7)
                acopy(hT_out, pT[:, :, 0:nt], [bpT], [bhT_out])

            def pool_branch(uTt, buT_, L, first, pooled_ap, bpooled_, gp_ap, bgp_, mixed_out, bmixed_):
                pool_sums(uTt, buT_, L, first, pooled_ap, bpooled_)
                pool_mix(L, pooled_ap, bpooled_, gp_ap, bgp_, mixed_out, bmixed_)

            def pool_sums(uTt, buT_, L, first, pooled_ap, bpooled_):
                for g, w in enumerate(POOLW):
                    tbufs = (tA, btA) if g < 2 else (tB, btB)
                    eng = "dve" if g < 2 else "pool"
                    cur = uTt[:, g, :]
                    curb = buT_
                    base = 16 - (w - 1)
                    lo_t = -(w - 1)
                    k = 1
                    step = 0
                    while k < w:
                        new_lo = lo_t + k
                        n = L - new_lo
                        c_hi = base + k
                        dst = tbufs[0][:, step % 2, 0:n]
                        dstb = tbufs[1][step % 2]
                        vtt(dst, cur[:, c_hi:c_hi + n], cur[:, c_hi - k:c_hi - k + n], ALU.add, [curb], [dstb], eng=eng)
                        cur = tbufs[0][:, step % 2, :]
                        curb = dstb
                        base = 0
                        lo_t = new_lo
                        k *= 2
                        step += 1
                    s_ap = cur[:, 0:L]
                    if first:
                        vtt(cur[:, 0:16], cur[:, 0:16], corr_t[:, g, :], ALU.mult, [curb, bcorr], [curb])
                    vstt(pooled_ap[:, g, 0:L], s_ap, 1.0 / w, uTt[:, g, 16:16 + L], ALU.mult, ALU.subtract, [curb, buT_], [bpooled_])

            def pool_mix(L, pooled_ap, bpooled_, gp_ap, bgp_, mixed_out, bmixed_):
                for g in range(4):
                    bank, bb = nextpb()
                    mm(bank[:, 0:L], wmixb[:, g, :], pooled_ap[:, g, 0:L], True, True, [bwmix, bpooled_], [bb], True)
                    vstt(mixed_out[:, g, 0:L], bank[:, 0:L], pscale_t[:, g:g + 1], gp_ap[:, g, 0:L], ALU.mult, ALU.mult,
                         [bb, bpscale, bgp_], [bmixed_])

            def post_norm_resid(nt, halves, bbs, col, x_ap, bx, layer, yslot, dst_dram, dst_sb=None, bdst=None, bdram=None):
                ya = ysb[0:nt, yslot, :]
                by = bysb[yslot]
                for hf in range(2):
                    aact(junk[0:nt, 0:512], halves[hf], AF.Square, [bbs[hf]], [bjunk, bms[8]], scale=1.0 / 32.0, accum=ms[0:nt, col + hf:col + hf + 1])
                    vcopy(ya[:, hf * 512:(hf + 1) * 512], halves[hf], [bbs[hf]], [by])
                vtt(ms[0:nt, col:col + 1], ms[0:nt, col:col + 1], ms[0:nt, col + 1:col + 2], ALU.add, [bms[8]], [bms[8]])
                aact(rs[0:nt, col:col + 1], ms[0:nt, col:col + 1], AF.Ln, [bms[8]], [brs[8]], bias=EPS)
                aact(rs[0:nt, col:col + 1], rs[0:nt, col:col + 1], AF.Exp, [brs[8]], [brs[8]], scale=-0.5)
                vstt(ya, ya, rs[0:nt, col:col + 1], gpost_t[0:nt, 0, :], ALU.mult, ALU.mult, [by, brs[8], bgpost], [by])
                if dst_sb is None:
                    vtt(ya, ya, x_ap, ALU.add, [by, bx], [by], eng="pool")
                    S.dma("sp", dst_dram, ya, reads=[by], writes=([bdram] if bdram is not None else []), owner=by)
                else:
                    vtt(dst_sb, ya, x_ap, ALU.add, [by, bx], [bdst], eng="pool")

            def pre_load(g):
                S.dma("sp", xr[:, g % 4, :], xp[g * 128:(g + 1) * 128, :], writes=[bxr[g % 4]])

            def pre_stats(m):
                for j in range(4):
                    g = 4 * m + j
                    norm_stats(xr[:, g % 4, :], 128, g, bxr[g % 4], bms[m])
                norm_rstd(128, 4 * m, 4 * m + 4, bms[m], brs[m])

            def pre_T(g):
                m, j = divmod(g, 4)
                make_hT(xr[:, g % 4, :], 128, g, bxr[g % 4], brs[m], g % 2, hT[:, :, j * 128:(j + 1) * 128], bhT)

            def proj_feat(co, N, rhsT, brhs):
                bank, bb = nextpb()
                for kc in range(8):
                    mm(bank[:, 0:N], W0[:, kc, co:co + 128], rhsT[:, kc, 0:N], kc == 0, kc == 7, wb(co, co + 128) + [brhs], [bb], kc == 7)
                return bank, bb

            def proj_tok(co, nt, lhs, blhs):
                bank, bb = nextpb()
                for kc in range(8):
                    mm(bank[0:nt, :], lhs[:, kc, :], W0[:, kc, co:co + 512], kc == 0, kc == 7, wb(co, co + 512) + [blhs], [bb], kc == 7)
                return bank, bb

            def inf_stage(m):
                slot = m % 2
                for c in range(4):
                    bank, bb = proj_feat(c * 128, 512, hT, bhT)
                    acopy(uT[:, c, 16:528], bank[:, :], [bb], [buT])
                for c in range(4):
                    bank, bb = proj_feat(512 + c * 128, 512, hT, bhT)
                    vcopy(qT[:, c, :], bank[:, :], [bb], [bqT])
                for c in range(4):
                    bank, bb = proj_feat(1024 + c * 128, 512, hT, bhT)
                    if c % 2:
                        vcopy(kT[:, c, slot * 512:(slot + 1) * 512], bank[:, :], [bb], [bkT[slot]])
                    else:
                        acopy(kT[:, c, slot * 512:(slot + 1) * 512], bank[:, :], [bb], [bkT[slot]])
                for c in range(4):
                    bank, bb = proj_feat(2048 + c * 128, 512, hT, bhT)
                    aact(gpT[:, c, :], bank[:, :], AF.Silu, [bb], [bgpT])
                pool_sums(uT, buT, 512, m == 0, pooledT, bpooled)
                pool_hist(m)

            def int_stage(m):
                for j in range(4):
                    g = 4 * m + j
                    lhs = hT[:, :, j * 128:(j + 1) * 128]
                    bank, bb = proj_tok(1536, 128, lhs, bhT)
                    chk(130)
                    vcopy(Vr[:, g % 8, :, 0:64], bank[:, :].rearrange("p (h d) -> p h d", h=8), [bb], [bV[g % 8]])
                    chk(131)
                    if m == NMT - 1:
                        acopy(ostg[:, 0, :], bank[:, :], [bb], [bostg[0]])
                        chk(1315)
                        S.dma("sp", o_v_p[j * 128:(j + 1) * 128, :], ostg[:, 0, :], reads=[bostg[0]])
                    chk(132)
                    bank, bb = proj_tok(2560, 128, lhs, bhT)
                    aact(gatt[:, j, :], bank[:, :], AF.Silu, [bb], [bgatt[j]])
                    chk(133)
                    if m == NMT - 1:
                        bank, bb = proj_tok(1024, 128, lhs, bhT)
                        acopy(ostg[:, 1, :], bank[:, :], [bb], [bostg[1]])
                        S.dma("sp", o_k_p[j * 128:(j + 1) * 128, :], ostg[:, 1, :], reads=[bostg[1]])

            def pool_stage(m):
                pool_mix(512, pooledT, bpooled, gpT, bgpT, mixedT, bmixed)

            def pool_hist(m):
                if m == NMT - 1:
                    bank, bb = nextpb()
                    for g in range(4):
                        tp(bank[0:15, g * 128:(g + 1) * 128], uT[:, g, 513:528], identf[:, :], [buT, bidf], [bb], g == 3)
                    acopy(ostg[0:15, 0, :], bank[0:15, :], [bb], [bostg[0]])
                    S.dma("sp", o_pool_p, ostg[0:15, 0, :], reads=[bostg[0]])
                else:
                    S.op("pool", lambda e: e.tensor_copy(out=uT[:, :, 1:16], in_=uT[:, :, 513:528]), [buT], [buT])

            def att_qk(m, qb, hh, buf):
                gq = 4 * m + qb
                nkb = min(gq, 4) + 1
                r0 = (hh % 2) * 64
                c = hh // 2
                for d in range(nkb):
                    gk = gq - d
                    slot = (gk // 4) % 2
                    off = slot * 512 + (gk % 4) * 128
                    mm(ST3[buf][:, d // 4, (d % 4) * 128:(d % 4) * 128 + 128], kT[r0:r0 + 64, c, off:off + 128], qT[r0:r0 + 64, c, qb * 128:(qb + 1) * 128],
                       True, True, [bkT[slot], bqT], bST3[buf], d == nkb - 1)
                n = nkb * 128
                src = ST3[buf][:].rearrange("p a b -> p (a b)")[:, 0:n]
                aact(expoS[buf][:, 0:n], src, AF.Exp, bST3[buf], [bexpo[buf]], scale=0.125)
                vtt(PTS[buf][:, 0:n], expoS[buf][:, 0:n], E5[:, hh, 0:nkb, :].rearrange("p d q -> p (d q)"), ALU.mult, [bexpo[buf], bE5], [bPT[buf]])

            def att_pv(m, qb, hh, buf):
                gq = 4 * m + qb
                nkb = min(gq, 4) + 1
                hq = hh % 4
                for d in range(nkb):
                    gk = gq - d
                    mm(pPV[:, hq * 65:hq * 65 + 65], PTS[buf][:, d * 128:(d + 1) * 128], Vr[:, gk % 8, hh, :], d == 0, d == nkb - 1,
                       [bPT[buf], bV[gk % 8]], [bpPV], d == nkb - 1)

            def att_norm(qb, hg, gatt_ap, bg_, nt=128):
                pv = pPV[0:nt, 0:260].rearrange("p (h d) -> p h d", h=4)
                r = hg % 2
                vrecip(rc[0:nt, r, :], pv[:, :, 64], [bpPV], [brc[r]])
                tn = tmpn[0:nt, r, :].rearrange("p (h d) -> p h d", h=4)
                vtt(tn, pv[:, :, 0:64], rc[0:nt, r, :].unsqueeze(2).to_broadcast([nt, 4, 64]), ALU.mult, [bpPV, brc[r]], [btmpn[r]])
                vtt(attg[0:nt, hg * 256:(hg + 1) * 256], tmpn[0:nt, r, :], gatt_ap[:, hg * 256:(hg + 1) * 256], ALU.mult, [btmpn[r], bg_], [battg])

            def att_stage(m, between=None):
                for qb in range(4):
                    if between is not None:
                        between(qb)
                    seq = list(range(8))
                    att_qk(m, qb, 0, 0)
                    att_qk(m, qb, 1, 1)
                    for hh in seq:
                        if hh + 2 < 8:
                            att_qk(m, qb, hh + 2, (hh + 2) % 3)
                        att_pv(m, qb, hh, hh % 3)
                        if hh % 4 == 3:
                            att_norm(qb, hh // 4, gatt[:, qb, :], bgatt[qb])
                    for c in range(4):
                        tp(pT[:, c, :], attg[:, c * 128:(c + 1) * 128], identb[:, :], [battg, bidb], [bpT], c == 3)
                    acopy(mixedT[:, 4:8, qb * 128:(qb + 1) * 128], pT[:, 0:4, :], [bpT], [bmixed])

            def out_stage(m):
                for j in range(4):
                    g = 4 * m + j
                    S.dma("sp", xq[:, g % 2, :], xp[g * 128:(g + 1) * 128, :], writes=[bxq[g % 2]])
                    halves = []
                    bbs = []
                    for hf in range(2):
                        bank, bb = nextpb()
                        for kc in range(8):
                            mm(bank[:, :], mixedT[:, kc, j * 128:(j + 1) * 128], W0[:, kc, 3072 + hf * 512:3072 + (hf + 1) * 512], kc == 0, kc == 7,
                               [bmixed] + wb(3072 + hf * 512, 3584 + hf * 512), [bb], kc == 7)
                        halves.append(bank[:, :])
                        bbs.append(bb)
                    post_norm_resid(128, halves, bbs, 36, xq[:, g % 2, :], bxq[g % 2], 0, g % 2, x1d[g * 128:(g + 1) * 128, :], bdram=bx1d[g])

            def chk(n):
                if stage == n:
                    raise StopIteration
            def l0_all():
              chk(10)
              for g in range(4):
                pre_load(g)
              pre_stats(0)
              for g in range(4):
                pre_T(g)
              load_weights(W0, w0in, 3072, 0, 0)
              load_weights(W0, w0out, 1024, 3072, None)
              chk(11)
              for m in range(NMT):
                inf_stage(m)
                chk(12)
                int_stage(m)
                chk(13)
                if m + 1 < NMT:
                    for g in range(4 * (m + 1), 4 * (m + 2)):
                        pre_load(g)
                    pre_stats(m + 1)
                pool_stage(m)
                chk(14)
                if m + 1 < NMT:
                    att_stage(m, between=lambda qb, m=m: pre_T(4 * (m + 1) + qb))
                else:
                    att_stage(m)
                chk(15)
                out_stage(m)
                chk(16)
              sample_l0()

            def sample_l0():
                bxs = bxr[3]
                xs_t = xr[0:NS, 3, :]
                S.dma("sp", xs_t, xs, writes=[bxs])
                norm_stats(xs_t, NS, 32, bxs, bms[9])
                norm_rstd(NS, 32, 33, bms[9], brs[9])
                make_hT(xs_t, NS, 32, bxs, brs[9], 0, hTs[:, :, :], bhTs)
                kTc = qT; bkTc = bqT
                Vc = Vr[:, 0:5, :, :]
                cstb = pooledT; bcstb = bpooled
                ES = E5[:].rearrange("p h d q -> p (h d q)")[:, 0:1280].rearrange("p (h d q) -> p h d q", h=8, d=5); bES = bE5
                S.dma("sp", stg[:, 2, 0:1280], biasS.rearrange("p h d q -> p (h d q)"), writes=[bstg[2]])
                aact(ES.rearrange("p h d q -> p (h d q)"), stg[:, 2, 0:1280], AF.Exp, [bstg[2]], [bES])
                S.dma("sp", cstage[:, :, :], ck.rearrange("(b p) f -> p b f", p=128), writes=[bstg[0], bstg[1], bexpo[2], bPT[2]])
                vcopy(cstb[:], cstage[:], [bstg[0], bstg[1]], [bcstb])
                for blk in range(4):
                    for c in range(4):
                        tp(pT[:, c, :], cstb[:, blk, c * 128:(c + 1) * 128], identb[:, :], [bcstb, bidb], [bpT], c == 3)
                    acopy(kTc[:, :, blk * 128:(blk + 1) * 128], pT[:, 0:4, :], [bpT], [bkTc])
                S.dma("sp", cstage[:, :, :], cv.rearrange("(b p) f -> p b f", p=128), writes=[bstg[0], bstg[1], bexpo[2], bPT[2]])
                vcopy(Vc[:, 0:4, :, 0:64], cstage[:].rearrange("p b (h d) -> p b h d", h=8), [bstg[0], bstg[1]], bV[0:5])
                S.dma("sp", ostg[0:15, 0, :], cpool, writes=[bostg[0]])
                bank, bb = nextpb()
                for g in range(4):
                    tp(bank[:, g * 16:g * 16 + 15], ostg[0:15, 0, g * 128:(g + 1) * 128], identf[0:15, 0:15], [bostg[0], bidf], [bb], g == 3)
                acopy(uTs[:, :, 1:16], bank[:, 0:64].rearrange("p (g t) -> p g t", g=4)[:, :, 0:15], [bb], [buTs])
                for c in range(4):
                    bank, bb = proj_feat(c * 128, NS, hTs, bhTs)
                    acopy(uTs[:, c, 16:48], bank[:, 0:NS], [bb], [buTs])
                for c in range(4):
                    bank, bb = proj_feat(512 + c * 128, NS, hTs, bhTs)
                    vcopy(qTs[:, c, :], bank[:, 0:NS], [bb], [bqTs])
                for c in range(4):
                    bank, bb = proj_feat(1024 + c * 128, NS, hTs, bhTs)
                    vcopy(kTs[:, c, :], bank[:, 0:NS], [bb], [bkTs])
                for c in range(4):
                    bank, bb = proj_feat(2048 + c * 128, NS, hTs, bhTs)
                    aact(gpT[:, c, 0:NS], bank[:, 0:NS], AF.Silu, [bb], [bgpT])
                bank, bb = proj_tok(0, NS, hTs, bhTs)
                acopy(ostg[0:NS, 0, :], bank[0:NS, :], [bb], [bostg[0]])
                S.dma("sp", o_pool_s, ostg[17:32, 0, :], reads=[bostg[0]])
                bank, bb = proj_tok(1024, NS, hTs, bhTs)
                acopy(ostg[0:NS, 1, :], bank[0:NS, :], [bb], [bostg[1]])
                S.dma("sp", o_k_s, ostg[0:NS, 1, :], reads=[bostg[1]])
                bank, bb = proj_tok(1536, NS, hTs, bhTs)
                acopy(ostg[0:NS, 0, :], bank[0:NS, :], [bb], [bostg[0]])
                S.dma("sp", o_v_s, ostg[0:NS, 0, :], reads=[bostg[0]])
                vcopy(Vc[0:NS, 4, :, 0:64], bank[0:NS, :].rearrange("p (h d) -> p h d", h=8), [bb], bV[0:5])
                bank, bb = proj_tok(2560, NS, hTs, bhTs)
                aact(gatt[0:NS, 0, :], bank[0:NS, :], AF.Silu, [bb], [bgatt[0]])
                pool_branch(uTs, buTs, NS, False, pooledT, bpooled, gpT, bgpT, mixedT, bmixed)
                for hh in range(8):
                    buf = hh % 2
                    r0 = (hh % 2) * 64
                    c = hh // 2
                    for blk in range(4):
                        mm(pST[buf][:, 0, blk * NS:(blk + 1) * NS], kTc[r0:r0 + 64, c, blk * 128:(blk + 1) * 128], qTs[r0:r0 + 64, c, :], True, True,
                           [bkTc, bqTs], [bpST[buf]], False)
                    mm(pST[buf][0:NS, 0, 4 * NS:5 * NS], kTs[r0:r0 + 64, c, :], qTs[r0:r0 + 64, c, :], True, True, [bkTs, bqTs], [bpST[buf]], True)
                    aact(expoS[buf][:, 0:4 * NS], pST[buf][:, 0, 0:4 * NS], AF.Exp, [bpST[buf]], [bexpo[buf]], scale=0.125)
                    aact(expoS[buf][0:NS, 4 * NS:5 * NS], pST[buf][0:NS, 0, 4 * NS:5 * NS], AF.Exp, [bpST[buf]], [bexpo[buf]], scale=0.125)
                    vtt(PTS[buf][:, 0:4 * NS], expoS[buf][:, 0:4 * NS], ES[:, hh, 0:4, :].rearrange("p d q -> p (d q)"), ALU.mult, [bexpo[buf], bES], [bPT[buf]])
                    vtt(PTS[buf][0:NS, 4 * NS:5 * NS], expoS[buf][0:NS, 4 * NS:5 * NS], ES[0:NS, hh, 4, :], ALU.mult, [bexpo[buf], bES], [bPT[buf]])
                    hq = hh % 4
                    for blk in range(4):
                        mm(pPV[0:NS, hq * 65:hq * 65 + 65], PTS[buf][:, blk * NS:(blk + 1) * NS], Vc[:, blk, hh, :], blk == 0, False, [bPT[buf]] + bV[0:5], [bpPV], False)
                    mm(pPV[0:NS, hq * 65:hq * 65 + 65], PTS[buf][0:NS, 4 * NS:5 * NS], Vc[0:NS, 4, hh, :], False, True, [bPT[buf]] + bV[0:5], [bpPV], True)
                    if hh % 4 == 3:
                        att_norm(0, hh // 4, gatt[0:NS, 0, :], bgatt[0], nt=NS)
                for c in range(4):
                    tp(pT[:, c, 0:NS], attg[0:NS, c * 128:(c + 1) * 128], identb[0:NS, 0:NS], [battg, bidb], [bpT], c == 3)
                acopy(mixedT[:, 4:8, 0:NS], pT[:, 0:4, 0:NS], [bpT], [bmixed])
                halves = []
                bbs = []
                for hf in range(2):
                    bank, bb = nextpb()
                    for kc in range(8):
                        mm(bank[0:NS, :], mixedT[:, kc, 0:NS], W0[:, kc, 3072 + hf * 512:3072 + (hf + 1) * 512], kc == 0, kc == 7, [bmixed] + wb(3072 + hf * 512, 3584 + hf * 512), [bb], kc == 7)
                    halves.append(bank[0:NS, :])
                    bbs.append(bb)
                post_norm_resid(NS, halves, bbs, 38, xs_t, bxs, 0, 0, None, dst_sb=x1s_t[:, :], bdst=bx1s)

            try:
                l0_all()
            except StopIteration:
                pass

            if stage == 1:
                for g in range(4 * NMT):
                    S.dma("sp", xq[:, g % 2, :], x1d[g * 128:(g + 1) * 128, :], reads=[bx1d[g]], writes=[bxq[g % 2]], owner=bxq[g % 2])
                    S.dma("sp", yp[g * 128:(g + 1) * 128, :], xq[:, g % 2, :], reads=[bxq[g % 2]])
                S.dma("sp", ys, x1s_t[:, :], reads=[bx1s])
            S.barrier()
            S.emit()


        if stage != 1:
          with ExitStack() as st1:
            sb1, _ = mk(st1)
            W1 = sb1("W1", [128, 8, 6152], BF16)
            S.dma("sp", gpost_t[:, 0, :], gpost[:, 1, :], writes=[bgpost])
            load_weights(W1, w1in, 5128, 0, 1, kscale_cols=(1024, 2048, 1.0 / 16.0),
                         chunks=[(5120, 8), (0, 1024), (1024, 1024), (2048, 1024), (3072, 1024), (4096, 1024)], on_pool=False)
            load_weights(W1, w1out, 1024, 5128, None, on_pool=False)
            xr = stg[:, :, 0:D]; bxr = bstg
            hT1a = sb1("hT1", [128, 2, 8, 128], BF16); bhT1a = [Buf("hT1_0"), Buf("hT1_1")]
            qT1 = sb1("qT1", [128, 3, 8, 128], BF16); bqT1 = [Buf("qT1_%d" % i) for i in range(3)]
            kT1 = sb1("kT1", [128, 2, 8, 128], BF16); bkT1 = [Buf("kT1a"), Buf("kT1b")]
            ktok = sb1("ktok", [128, 2, D], BF16); bktok = [Buf("ktoka"), Buf("ktokb")]
            vx = sb1("vx", [128, 3, 4, 257], BF16); bvx = [Buf("vx_%d" % i) for i in range(3)]
            wv = sb1("wv", [128, 4, 257], BF16); bwv = Buf("wv")
            St = stg[:, 3, 1024:1536].bitcast(BF16).rearrange("p (s h t) -> p s h t", s=2, h=4); bSt = [Buf("St0"), Buf("St1")]
            Cf = sb1("Cf", [128, 4, 2, 257]); bCfh = [Buf("Cf%d" % i) for i in range(4)]
            Cb = sb1("Cb", [128, 2, 4, 2, 257], BF16); bCb = [Buf("Cb0"), Buf("Cb1")]
            hc = sb1("hc", [128, D]); bhch = [Buf("hc%d" % i) for i in range(4)]
            junk2 = stg[:, 2, 1024:1280].bitcast(BF16).rearrange("p (a b) -> p a b", a=2); bjunk2 = [Buf("j2a"), Buf("j2b")]
            sg = sb1("sg", [128, 3, D], BF16); bsg = [Buf("sg_%d" % i) for i in range(3)]
            sz = sb1("sz", [128, D], BF16); bsz = Buf("sz")
            outm = stg[:, 1, 1024:1536].bitcast(BF16); boutm = Buf("outm")
            outT = stg[:, 0, 1024:1536].bitcast(BF16).rearrange("p (k t) -> p k t", k=8); boutT = Buf("outT")
            gml_t = sb1("gml_t", [128, D]); bgml = Buf("gml")
            tri_t = sb1("tri_t", [128, 128]); btri = Buf("tri")
            bg_t = sb1("bg_t", [128, 8]); bbg = Buf("bg")
            ones4 = sb1("ones4", [4, 128]); bones4 = Buf("ones4")
            gb = sb1("gb", [128, 2, 8]); bgb = [Buf("gba"), Buf("gbb")]
            g4 = sb1("g4", [128, 2, 16]); bg4 = [Buf("g4a"), Buf("g4b")]
            big4 = sb1("big4", [4, 256]); bbig4 = Buf("big4")
            cbt = sb1("cbt", [4, 2, 256]); bcbt = [Buf("cbta"), Buf("cbtb")]
            sm4 = sb1("sm4", [4, 16]); bsm4 = Buf("sm4")
            mprev = sb1("mprev", [4, 1]); bmprev = Buf("mprev")
            gsb = sb1("gsb", [128, 2, 12]); bgsb = [Buf("gsb0"), Buf("gsb1")]
            st8 = sb1("st8", [128, 4, 8]); bst8 = [Buf("st8_%d" % i) for i in range(4)]
            ctr = hc[:, 0:512].rearrange("p (a b) -> p a b", a=2); bctr = [bhch[0], bhch[1]]

            S.dma("sp", gml_t[:], gml, writes=[bgml])
            S.dma("sp", tri_t[:], tri, writes=[btri])
            S.dma("sp", bg_t[:], bgate, writes=[bbg])
            memset("pool", ones4[:], 1.0, [bones4])
            memset("pool", vx[:, :, :, 256:257], 1.0, bvx)

            bro = [Buf("ro0", True), Buf("ro1", True)]
            pbl.extend([(pST[0][:, 0, :], Buf("pq0", True)), (pST[0][:, 1, :], Buf("pq1", True))])

            def genA(nt, x_ap, bx, col, sl, s3):
                hT1 = hT1a[:, sl, :, :]
                bhT1 = bhT1a[sl]
                norm_stats(x_ap, nt, col, bx, bms[9])
                norm_rstd(nt, col, col + 1, bms[9], brs[9])
                hb = h[0:nt, sl, :]
                vts(hb, x_ap, rs[0:nt, col:col + 1], None, ALU.mult, None, [bx, brs[9]], [bh[sl]])
                yield
                for kc in range(8):
                    tp(pT[:, kc, 0:nt], hb[:, kc * 128:(kc + 1) * 128], identb[0:nt, 0:nt], [bh[sl], bidb], [bpT], kc == 7)
                acopy(hT1[:, :, 0:nt], pT[:, :, 0:nt], [bpT], [bhT1])
                yield
                bank, bb = nextpb()
                for kc in range(8):
                    mm(bank[0:nt, 0:8], hT1[:, kc, 0:nt], W1[:, kc, 5120:5128], kc == 0, kc == 7, wb(5120, 5128) + [bhT1], [bb], kc == 7)
                vtt(gb[0:nt, sl, :], bank[0:nt, 0:8], bg_t[0:nt, :], ALU.add, [bb, bbg], [bgb[sl]])
                aact(g4[0:nt, sl, 0:4], gb[0:nt, sl, 4:8], AF.Exp, [bgb[sl]], [bg4[sl]], scale=-1.0)
                aact(g4[0:nt, sl, 4:8], g4[0:nt, sl, 0:4], AF.Ln, [bg4[sl]], [bg4[sl]], bias=1.0)
                yield
                def tokproj(co):
                    bank, bb = nextpb()
                    for kc in range(8):
                        mm(bank[0:nt, :], hT1[:, kc, 0:nt], W1[:, kc, co:co + 512], kc == 0, kc == 7, wb(co, co + 512) + [bhT1], [bb], kc == 7)
                    return bank, bb
                bank, bb = tokproj(0)
                acopy(sz[0:nt, 0:512], bank[0:nt, :], [bb], [bsz])
                yield
                bank, bb = nextpb()
                mm(bank[0:nt, 0:4], tri_t[0:nt, 0:nt], g4[0:nt, sl, 4:8], True, True, [btri, bg4[sl]], [bb], True)
                vtt(g4[0:nt, sl, 8:12], gb[0:nt, sl, 0:4], bank[0:nt, 0:4], ALU.add, [bgb[sl], bb], [bg4[sl]])
                vcopy(g4[0:nt, sl, 12:16], bank[0:nt, 0:4], [bb], [bg4[sl]])
                yield
                bank, bb = tokproj(512)
                vcopy(sz[0:nt, 512:1024], bank[0:nt, :], [bb], [bsz])
                yield
                bank, bb = nextpb()
                tp(bank[0:4, 0:nt], g4[0:nt, sl, 8:12], identf[0:nt, 0:nt], [bg4[sl], bidf], [bb], False)
                tp(bank[0:4, 128:128 + nt], g4[0:nt, sl, 12:16], identf[0:nt, 0:nt], [bg4[sl], bidf], [bb], True)
                vcopy(cbt[:, sl, 0:256], bank[0:4, 0:256], [bb], [bcbt[sl]])
                for kc in range(8):
                    tp(pT[:, kc, 0:nt], sz[0:nt, kc * 128:(kc + 1) * 128], identb[0:nt, 0:nt], [bsz, bidb], [bpT], kc == 7)
                vcopy(qT1[:, s3, :, 0:nt], pT[:, :, 0:nt], [bpT], [bqT1[s3]])
                yield
                for half in range(2):
                    bank, bb = tokproj(1024 + half * 512)
                    acopy(ktok[0:nt, sl, half * 512:(half + 1) * 512], bank[0:nt, :], [bb], [bktok[sl]])
                    yield
                for half in range(2):
                    bank, bb = tokproj(2048 + half * 512)
                    vcopy(vx[0:nt, s3, half * 2:half * 2 + 2, 0:256], bank[0:nt, :].rearrange("p (h d) -> p h d", h=2), [bb], [bvx[s3]])
                    if half == 0:
                        for kc in range(8):
                            tp(pT[:, kc, 0:nt], ktok[0:nt, sl, kc * 128:(kc + 1) * 128], identb[0:nt, 0:nt], [bktok[sl], bidb], [bpT], kc == 7)
                        acopy(kT1[:, sl, :, 0:nt], pT[:, :, 0:nt], [bpT], [bkT1[sl]])
                    yield
                for half in range(2):
                    bank, bb = tokproj(3072 + half * 512)
                    aact(sg[0:nt, s3, half * 512:(half + 1) * 512], bank[0:nt, :], AF.Sigmoid, [bb], [bsg[s3]])
                    yield
                for half in range(2):
                    bank, bb = tokproj(4096 + half * 512)
                    aact(sz[0:nt, half * 512:(half + 1) * 512], bank[0:nt, :], AF.Silu, [bb], [bsz])
                    yield
                vtt(sg[0:nt, s3, :], sg[0:nt, s3, :], sz[0:nt, :], ALU.mult, [bsg[s3], bsz], [bsg[s3]], eng="pool")
                vtt(sg[0:nt, s3, :], sg[0:nt, s3, :], gml_t[0:nt, :], ALU.mult, [bsg[s3], bgml], [bsg[s3]], eng="pool")
                yield

            def genBe(nt, sl, s3):
                G = gsb[:, sl, :]
                S.op("dve", lambda e: e.tensor_reduce(out=sm4[:, 0:1], in_=cbt[:, sl, 0:nt], axis=mybir.AxisListType.X, op=ALU.max), [bcbt[sl]], [bsm4])
                vtt(sm4[:, 1:2], sm4[:, 0:1], mprev[:, :], ALU.max, [bsm4, bmprev], [bsm4])
                vts(sm4[:, 2:3], sm4[:, 1:2], -1.0, None, ALU.mult, None, [bsm4], [bsm4])
                aact(sm4[:, 3:4], mprev[:, :], AF.Exp, [bmprev, bsm4], [bsm4], bias=sm4[:, 2:3])
                aact(big4[:, 0:nt], cbt[:, sl, 0:nt], AF.Exp, [bcbt[sl], bsm4], [bbig4], bias=sm4[:, 2:3])
                aact(big4[:, 128:128 + nt], cbt[:, sl, 128:128 + nt], AF.Exp, [bcbt[sl], bsm4], [bbig4], bias=sm4[:, 2:3])
                vtt(mprev[:, :], sm4[:, 1:2], cbt[:, sl, 128 + nt - 1:128 + nt], ALU.subtract, [bsm4, bcbt[sl]], [bmprev])
                vts(sm4[:, 4:8], identf[0:4, 0:4], sm4[:, 3:4], None, ALU.mult, None, [bidf, bsm4], [bsm4])
                yield
                bank, bb = nextpb()
                mm(bank[0:nt, 0:4], big4[:, 0:nt], identf[0:4, 0:4], True, True, [bbig4, bidf], [bb], False)
                mm(bank[0:nt, 4:8], big4[:, 128:128 + nt], identf[0:4, 0:4], True, True, [bbig4, bidf], [bb], False)
                mm(bank[:, 8:12], ones4[:, :], sm4[:, 4:8], True, True, [bones4, bsm4], [bb], True)
                vcopy(G[0:nt, 0:8], bank[0:nt, 0:8], [bb], [bgsb[sl]])
                vcopy(G[:, 8:12], bank[:, 8:12], [bb], [bgsb[sl]])
                yield
                for hh in range(4):
                    aact(Cf[:, hh, :, :], Cf[:, hh, :, :], AF.Copy, [bCfh[hh], bgsb[sl]], [bCfh[hh]], scale=G[:, 8 + hh:9 + hh])
                    vcopy(Cb[:, sl, hh, :, :], Cf[:, hh, :, :], [bCfh[hh]], [bCb[sl]], eng="pool")
                S.op("dve", lambda e: e.tensor_tensor(out=wv[0:nt, :, :], in0=vx[0:nt, s3, :, :], in1=G[0:nt, 0:4].unsqueeze(2).to_broadcast([nt, 4, 257]), op=ALU.mult),
                     [bvx[s3], bgsb[sl]], [bwv])
                yield
                for hh in range(4):
                    for c in range(2):
                        bank, bb = nextpb()
                        mm(bank[:, 0:257], ktok[0:nt, sl, hh * 256 + c * 128:hh * 256 + (c + 1) * 128], wv[0:nt, hh, :], True, True, [bktok[sl], bwv], [bb], True)
                        vtt(Cf[:, hh, c, :], bank[:, 0:257], Cf[:, hh, c, :], ALU.add, [bCfh[hh], bb], [bCfh[hh]])
                    if hh % 2 == 1:
                        yield
                for hh in range(4):
                    for c in range(2):
                        mm(pPV[0:nt, hh * 128:hh * 128 + nt], kT1[:, sl, 2 * hh + c, 0:nt], qT1[:, s3, 2 * hh + c, 0:nt], c == 0, c == 1, [bkT1[sl], bqT1[s3]], [bpPV], hh == 3 and c == 1)
                for hh in range(4):
                    vstt(St[0:nt, sl, hh, 0:nt], pPV[0:nt, hh * 128:hh * 128 + nt], G[0:nt, hh:hh + 1], tri_t[0:nt, 0:nt], ALU.mult, ALU.mult, [bpPV, bgsb[sl], btri], [bSt[sl]])
                yield

            def genBl(nt, x_ap, bx, sl, s3, dst_dram):
                G = gsb[:, sl, :]
                for hh in range(4):
                    ro = pST[1][0:nt, hh % 2, 0:257]
                    for c in range(2):
                        mm(ro, qT1[:, s3, 2 * hh + c, 0:nt], Cb[:, sl, hh, c, :], c == 0, False, [bqT1[s3], bCb[sl]], [bro[hh % 2]], False)
                    mm(ro, St[0:nt, sl, hh, 0:nt], vx[0:nt, s3, hh, :], False, True, [bSt[sl], bvx[s3]], [bro[hh % 2]], True)
                    b8 = [bst8[hh]]
                    vcopy(st8[0:nt, hh, 2:3], ro[:, 256:257], [bro[hh % 2]], b8)
                    vstt(st8[0:nt, hh, 0:1], st8[0:nt, hh, 2:3], -1.0, st8[0:nt, hh, 2:3], ALU.mult, ALU.max, b8, b8)
                    vtt(st8[0:nt, hh, 0:1], st8[0:nt, hh, 0:1], G[0:nt, 4 + hh:5 + hh], ALU.max, b8 + [bgsb[sl]], b8)
                    vrecip(st8[0:nt, hh, 1:2], st8[0:nt, hh, 0:1], b8, b8)
                    aact(hc[0:nt, hh * 256:(hh + 1) * 256], ro[:, 0:256], AF.Copy, [bro[hh % 2]] + b8, [bhch[hh]] + b8, scale=st8[0:nt, hh, 1:2], accum=st8[0:nt, hh, 3:4])
                    aact(junk2[0:nt, hh % 2, :], hc[0:nt, hh * 256:(hh + 1) * 256], AF.Square, [bhch[hh]], [bjunk2[hh % 2]] + b8, accum=st8[0:nt, hh, 4:5])
                    yield
                vts(st8[0:nt, :, 5], st8[0:nt, :, 3], 1.0 / 256.0, None, ALU.mult, None, bst8, bst8)
                vtt(st8[0:nt, :, 6], st8[0:nt, :, 5], st8[0:nt, :, 5], ALU.mult, bst8, bst8)
                vts(st8[0:nt, :, 4], st8[0:nt, :, 4], 1.0 / 256.0, None, ALU.mult, None, bst8, bst8)
                vtt(st8[0:nt, :, 7], st8[0:nt, :, 4], st8[0:nt, :, 6], ALU.subtract, bst8, bst8)
                aact(st8[0:nt, :, 7], st8[0:nt, :, 7], AF.Ln, bst8, bst8, bias=EPS)
                aact(st8[0:nt, :, 7], st8[0:nt, :, 7], AF.Exp, bst8, bst8, scale=-0.5)
                for hh in range(4):
                    vts(hc[0:nt, hh * 256:(hh + 1) * 256], hc[0:nt, hh * 256:(hh + 1) * 256], st8[0:nt, hh, 5:6], st8[0:nt, hh, 7:8],
                        ALU.subtract, ALU.mult, [bhch[hh], bst8[hh]], [bhch[hh]])
                vtt(outm[0:nt, :], hc[0:nt, :], sg[0:nt, s3, :], ALU.mult, bhch + [bsg[s3]], [boutm])
                yield
                for kc in range(8):
                    tp(pT[:, kc, 0:nt], outm[0:nt, kc * 128:(kc + 1) * 128], identb[0:nt, 0:nt], [boutm, bidb], [bpT], kc == 7)
                acopy(outT[:, :, 0:nt], pT[:, :, 0:nt], [bpT], [boutT])
                yield
                halves = []
                bbs = []
                for hf in range(2):
                    bank, bb = nextpb()
                    for kc in range(8):
                        mm(bank[0:nt, :], outT[:, kc, 0:nt], W1[:, kc, 5128 + hf * 512:5128 + (hf + 1) * 512], kc == 0, kc == 7, [boutT] + wb(5128 + hf * 512, 5640 + hf * 512), [bb], kc == 7)
                    halves.append(bank[0:nt, :])
                    bbs.append(bb)
                post_norm_resid(nt, halves, bbs, 36, x_ap, bx, 1, 0, dst_dram)
                yield

            def drive(gens, order=None, nodrain=()):
                live = {i: g for i, g in enumerate(gens) if g is not None}
                for i in (order or ()):
                    if i in live:
                        try:
                            next(live[i])
                        except StopIteration:
                            del live[i]
                while [i for i in live if i not in nodrain]:
                    for i in sorted(live):
                        if i in nodrain:
                            continue
                        try:
                            next(live[i])
                        except StopIteration:
                            del live[i]

            def state_out(oC, on, om):
                stage = [hc[:, :].rearrange("p (h vb k) -> p h vb k", h=2, vb=2), ysb[:, 1, :].rearrange("p (h vb k) -> p h vb k", h=2, vb=2)]
                bst = [bhch, [bysb[1]]]
                for hh in range(4):
                    sg_ = stage[hh // 2]
                    for vb in range(2):
                        bank, bb = nextpb()
                        for c in range(2):
                            tp(bank[:, c * 128:(c + 1) * 128], Cf[:, hh, c, vb * 128:(vb + 1) * 128], identf[:, :], bCfh + [bidf], [bb], c == 1)
                        if vb == 0:
                            acopy(sg_[:, hh % 2, vb, :], bank[:, 0:256], [bb], bst[hh // 2])
                        else:
                            vcopy(sg_[:, hh % 2, vb, :], bank[:, 0:256], [bb], bst[hh // 2])
                    S.dma("sp", oC[hh].rearrange("(vb p) k -> p vb k", p=128), sg_[:, hh % 2, :, :], reads=bst[hh // 2], owner=bst[hh // 2][0])
                    for c in range(2):
                        S.dma("sp", on[hh:hh + 1, c * 128:(c + 1) * 128].rearrange("a k -> k a"), Cf[:, hh, c, 256:257], reads=bCfh, owner=bst[hh // 2][0])
                S.dma("sp", om, mprev[:, :], reads=[bmprev], owner=bysb[1])

            memset("pool", Cf[:], 0.0, bCfh)
            memset("pool", mprev[:], 0.0, [bmprev])
            NCH = 4 * NMT

            def mkA(g):
                if g >= NCH:
                    return None
                slot = g % 4
                S.dma("sp", xr[:, slot, :], x1d[g * 128:(g + 1) * 128, :], reads=[bx1d[g]], writes=[bxr[slot]], owner=bxr[slot])
                return genA(128, xr[:, slot, :], bxr[slot], 32, g % 2, g % 3)

            def mkBe(g):
                return genBe(128, g % 2, g % 3) if g < NCH else None

            def mkBl(g):
                if g < 0:
                    return None
                slot = g % 4
                return genBl(128, xr[:, slot, :], bxr[slot], g % 2, g % 3, yp[g * 128:(g + 1) * 128, :])
            drive([mkA(0)])
            A_next = mkA(1)
            if A_next is not None:
                next(A_next)
                next(A_next)
            for g in range(NCH + 1):
                A_after = mkA(g + 2)
                if g == NCH:
                    A_next = genA(NS, x1s_t[:, :], bx1s, 33, 0, 0)
                drive([mkBl(g - 1), mkBe(g), A_next, A_after],
                      order=[1, 0, 2, 0, 2, 0, 2, 1, 0, 2, 1, 3, 2, 1, 0, 2, 1, 3, 2, 1, 2, 0, 2, 2, 0, 2, 2, 2, 2], nodrain=(3,))
                A_next = A_after
            state_out(o_C_p, o_n_p, o_m_p)
            for hh in range(4):
                c0 = xr[:, hh, 0:512].rearrange("p (vb k) -> p vb k", vb=2)
                S.dma("sp", c0, sC[hh].rearrange("(vb p) k -> p vb k", p=128), writes=[bxr[hh]], owner=bxr[hh])
            for hh in range(4):
                c0 = xr[:, hh, 0:512].rearrange("p (vb k) -> p vb k", vb=2)
                for c in range(2):
                    bank, bb = nextpb()
                    for vb in range(2):
                        tp(bank[:, vb * 128:(vb + 1) * 128], c0[:, vb, c * 128:(c + 1) * 128], identf[:, :], [bxr[hh], bidf], [bb], vb == 1)
                    if c == 0:
                        acopy(Cf[:, hh, c, 0:256], bank[:, 0:256], [bb], [bCfh[hh]])
                    else:
                        vcopy(Cf[:, hh, c, 0:256], bank[:, 0:256], [bb], [bCfh[hh]])
                    S.dma("sp", Cf[:, hh, c, 256:257], sn[hh:hh + 1, c * 128:(c + 1) * 128].rearrange("a k -> k a"), writes=[bCfh[hh]], owner=bysb[1])
            S.dma("sp", mprev[:, :], sm, writes=[bmprev], owner=bysb[1])
            drive([genBe(NS, 0, 0)])
            drive([genBl(NS, x1s_t[:, :], bx1s, 0, 0, ys)])
            state_out(o_C_s, o_n_s, o_m_s)
            S.barrier()
            S.emit()

        S.barrier()
        S.emit()
    return nc


_CACHE = {}


def _host_consts(rel_bias):
    tab = np.asarray(rel_bias[0], np.float32)
    k = np.arange(128)[:, None, None]
    d = np.arange(5)[None, :, None]
    q = np.arange(128)[None, None, :]
    idx = np.clip(128 * d + q - k, -128, 128) + 128
    biasT = np.ascontiguousarray(tab[:, idx].transpose(1, 0, 2, 3))
    mask5 = np.ones((128, 5, 128), np.float32)
    mask5[64:, 0, :64] = 0.0
    mask5[:64, 4, 64:] = 0.0
    qs = np.arange(NS)[None, None, :]
    blk = np.arange(5)[None, :, None]
    kk = np.arange(128)[:, None, None]
    rel = np.where(blk < 4, 512 + qs - (128 * blk + kk), qs - kk)
    idxs = np.clip(rel, -128, 128) + 128
    biasS = np.ascontiguousarray(tab[:, idxs].transpose(1, 0, 2, 3))
    corr = np.ones((128, 4, 16), np.float32)
    for g, w in enumerate(POOLW):
        t = np.arange(16)
        corr[:, g, :] = w / np.minimum(t + 1, w)
    return biasT, mask5, biasS, corr


def _relayout_w(w):
    k, n = w.shape
    return np.ascontiguousarray(w.reshape(8, 128, n).transpose(1, 0, 2))


def kernel(x_prompt, x_sample, cache_pool, cache_k, cache_v, state_C, state_n, state_m,
           norm_pre, norm_post, w_in_even, w_pool_mix, pool_scale, rel_bias, w_out_even,
           w_in_odd, b_gate_odd, mlstm_norm, w_out_odd, _stage=2):
    f = lambda a: np.ascontiguousarray(np.asarray(a, np.float32))
    if ("nc", _stage) not in _CACHE:
        _CACHE[("nc", _stage)] = build_nc(_stage)
    nc = _CACHE[("nc", _stage)]
    biasT, mask5, biasS, corr = _host_consts(f(rel_bias))
    tri = np.triu(np.ones((128, 128), np.float32))
    sel = np.zeros((4, 4, 128), np.float32)
    for hh in range(4):
        sel[hh, hh, :] = 1.0
    shared = {
        "gpre": np.ascontiguousarray(f(norm_pre).reshape(2, 8, 128).transpose(2, 0, 1)),
        "gpost": np.ascontiguousarray(np.broadcast_to(f(norm_post)[None], (128, 2, D))),
        "w0in": _relayout_w(f(w_in_even)[0]), "w0out": _relayout_w(f(w_out_even)[0]),
        "wmix": np.ascontiguousarray(f(w_pool_mix)[0].transpose(1, 0, 2)),
        "pscale": np.ascontiguousarray(f(pool_scale)[0].reshape(4, 128).T),
        "biasT": biasT, "mask5": mask5, "biasS": biasS, "corr": corr,
        "ident": np.eye(128, dtype=np.float32),
        "w1in": _relayout_w(f(w_in_odd)[0]), "w1out": _relayout_w(f(w_out_odd)[0]),
        "bgate": np.ascontiguousarray(np.broadcast_to(f(b_gate_odd)[0][None, :], (128, 8))),
        "gml": np.ascontiguousarray(np.broadcast_to(f(mlstm_norm)[0][None], (128, D))),
        "tri": tri, "sel": sel,
    }
    in_maps = []
    for c in range(8):
        m = dict(shared)
        m.update({
            "xp": f(x_prompt[c]), "xs": f(x_sample[c]),
            "cpool": f(cache_pool[0, c]), "ck": f(cache_k[0, c]).reshape(512, 512), "cv": f(cache_v[0, c]).reshape(512, 512),
            "sC": f(state_C[0, c]), "sn": f(state_n[0, c]), "sm": f(state_m[0, c]).reshape(4, 1),
        })
        in_maps.append(m)
    res = run_bass_kernel_spmd(nc, in_maps, core_ids=list(range(8)))
    R = res.results

    def gather(name, shape):
        return np.stack([np.asarray(r[name], np.float32).reshape(shape) for r in R], 0)
    y_p = gather("yp", (T, D)); y_s = gather("ys", (NS, D))
    pool_p = gather("pool_p", (15, 512))[None]
    k_p = gather("k_p", (512, 8, 64))[None]; v_p = gather("v_p", (512, 8, 64))[None]
    C_p = gather("C_p", (4, 256, 256))[None]; n_p = gather("n_p", (4, 256))[None]; m_p = gather("m_p", (4,))[None]
    pool_s = gather("pool_s", (15, 512))[None]
    k_s = gather("k_s", (NS, 8, 64))[None]; v_s = gather("v_s", (NS, 8, 64))[None]
    C_s = gather("C_s", (4, 256, 256))[None]; n_s = gather("n_s", (4, 256))[None]; m_s = gather("m_s", (4,))[None]
    return (y_p, y_s, pool_p, k_p, v_p, C_p, n_p, m_p, pool_s, k_s, v_s, C_s, n_s, m_s)
```

```python
import numpy as np
from contextlib import ExitStack
import concourse.bass as bass
import concourse.mybir as mybir
from concourse.bass_utils import run_bass_kernel_spmd

F32 = mybir.dt.float32
BF16 = mybir.dt.bfloat16
ALU = mybir.AluOpType
AF = mybir.ActivationFunctionType

D = 1024
T = 4096
NS = 32
NMT = 8
POOLW = (2, 4, 8, 16)
EPS = 1e-6


class Buf:
    __slots__ = ("name", "lw", "readers", "dsem", "excl")

    def __init__(self, name="", excl=False):
        self.name = name
        self.excl = excl
        self.lw = None
        self.readers = {}
        self.dsem = None


class Sched:
    ENG = ("pe", "act", "dve", "pool", "sp")

    def __init__(self, nc, stack):
        self.nc = nc
        self.stack = stack
        self.prog = {e: [] for e in self.ENG}
        self.sems = {}
        self.issued = {}
        self.isdma = {}
        self.seen = {e: {} for e in self.ENG}
        for e in ("pe", "act", "dve", "pool"):
            self._mksem(e, False)
        self.ndma = 0

    def _mksem(self, key, isdma):
        self.sems[key] = self.stack.enter_context(self.nc.semaphore("s_" + key))
        self.issued[key] = 0
        self.isdma[key] = isdma

    def _waits(self, eng, reads, writes):
        need = {}

        def add(k, v):
            if self.isdma[k]:
                v = self.issued[k]
            if v > need.get(k, 0):
                need[k] = v
        for b in reads:
            if b.lw is not None:
                add(*b.lw)
            if b.excl:
                for k, v in b.readers.items():
                    if k != eng:
                        add(k, v)
        for b in writes:
            if b.lw is not None:
                add(*b.lw)
            for k, v in b.readers.items():
                add(k, v)
        out = []
        for k, v in need.items():
            if k == "pe" and eng == "pe":
                continue
            if self.seen[eng].get(k, 0) >= v:
                continue
            self.seen[eng][k] = v
            out.append((k, v))
        return out

    def _record(self, ev, reads, writes):
        k, v = ev
        for b in reads:
            if b.readers.get(k, 0) < v:
                b.readers[k] = v
        for b in writes:
            b.lw = ev
            b.readers = {}

    def op(self, eng, fn, reads=(), writes=(), inc=True):
        waits = self._waits(eng, reads, writes)
        if inc:
            self.issued[eng] += 1
            ev = (eng, self.issued[eng])
        else:
            ev = (eng, self.issued[eng] + 1)
        self.prog[eng].append((waits, fn, eng if inc else None))
        self._record(ev, reads, writes)
        return ev

    def dma(self, q, out_ap, in_ap, reads=(), writes=(), owner=None, **kw):
        waits = self._waits(q, reads, writes)
        if owner is None:
            owner = (list(writes) + list(reads))[0]
        if owner.dsem is None:
            self.ndma += 1
            owner.dsem = "d%d" % self.ndma
            self._mksem(owner.dsem, True)
        semkey = owner.dsem
        self.issued[semkey] += 16
        ev = (semkey, self.issued[semkey])

        def fn(e, out_ap=out_ap, in_ap=in_ap, kw=kw):
            return e.dma_start(out=out_ap, in_=in_ap, **kw)
        self.prog[q].append((waits, fn, semkey))
        self._record(ev, reads, writes)
        return ev

    def barrier(self):
        for e in self.ENG:
            waits = []
            for k, v in self.issued.items():
                if v > self.seen[e].get(k, 0):
                    self.seen[e][k] = v
                    waits.append((k, v))
            self.prog[e].append((waits, None, None))

    def emit(self):
        nc = self.nc
        prog = self.prog
        self.prog = {e: [] for e in self.ENG}
        with nc.Block() as block:
            def run(e, items):
                for waits, fn, inck in items:
                    for k, v in waits:
                        e.wait_ge(self.sems[k], v)
                    if fn is None:
                        continue
                    ins = fn(e)
                    if inck is not None:
                        ins.then_inc(self.sems[inck], 16 if self.isdma[inck] else 1)

            @block.tensor
            def _(e):
                run(e, prog["pe"])

            @block.scalar
            def _(e):
                run(e, prog["act"])

            @block.vector
            def _(e):
                run(e, prog["dve"])

            @block.gpsimd
            def _(e):
                run(e, prog["pool"])

            @block.sync
            def _(e):
                run(e, prog["sp"])


def build_nc(stage=2):
    nc = bass.Bass("TRN2", target_bir_lowering=False)

    def din(name, shape):
        return nc.dram_tensor(name, shape, F32, kind="ExternalInput").ap()

    def dout(name, shape):
        return nc.dram_tensor(name, shape, F32, kind="ExternalOutput").ap()
    xp = din("xp", [T, D]); xs = din("xs", [NS, D])
    cpool = din("cpool", [15, 512]); ck = din("ck", [512, 512]); cv = din("cv", [512, 512])
    sC = din("sC", [4, 256, 256]); sn = din("sn", [4, 256]); sm = din("sm", [4, 1])
    gpre = din("gpre", [128, 2, 8]); gpost = din("gpost", [128, 2, D])
    w0in = din("w0in", [128, 8, 3072]); w0out = din("w0out", [128, 8, D])
    wmix = din("wmix", [128, 4, 128]); pscale = din("pscale", [128, 4])
    biasT = din("biasT", [128, 8, 5, 128]); mask5 = din("mask5", [128, 5, 128]); biasS = din("biasS", [128, 8, 5, NS])
    corr = din("corr", [128, 4, 16]); ident = din("ident", [128, 128])
    w1in = din("w1in", [128, 8, 5128]); w1out = din("w1out", [128, 8, D])
    bgate = din("bgate", [128, 8]); gml = din("gml", [128, D])
    tri = din("tri", [128, 128]); sel = din("sel", [4, 4, 128])
    yp = dout("yp", [T, D]); ys = dout("ys", [NS, D])
    o_pool_p = dout("pool_p", [15, 512]); o_k_p = dout("k_p", [512, 512]); o_v_p = dout("v_p", [512, 512])
    o_C_p = dout("C_p", [4, 256, 256]); o_n_p = dout("n_p", [4, 256]); o_m_p = dout("m_p", [4, 1])
    o_pool_s = dout("pool_s", [15, 512]); o_k_s = dout("k_s", [NS, 512]); o_v_s = dout("v_s", [NS, 512])
    o_C_s = dout("C_s", [4, 256, 256]); o_n_s = dout("n_s", [4, 256]); o_m_s = dout("m_s", [4, 1])
    x1d = nc.dram_tensor("x1d", [T, D], F32, kind="Internal").ap()
    bx1d = [Buf("x1d%d" % i) for i in range(4 * NMT)]

    with ExitStack() as st:
        S = Sched(nc, st)

        def mk(stack):
            def sb(name, shape, dt=F32):
                return stack.enter_context(nc.sbuf_tensor(name, shape, dt))

            def ps(name, shape, dt=F32):
                return stack.enter_context(nc.psum_tensor(name, shape, dt))
            return sb, ps
        sb, ps = mk(st)

        bWs = [Buf("W%d" % i) for i in range(13)]

        def wb(c0, c1):
            return bWs[c0 // 512:(c1 - 1) // 512 + 1]
        stg = sb("stg", [128, 4, 1536]); bstg = [Buf("stg%d" % i) for i in range(4)]
        identf = sb("identf", [128, 128]); bidf = Buf("idf")
        identb = sb("identb", [128, 128], BF16); bidb = Buf("idb")
        gpre_t = sb("gpre_t", [128, 2, 8]); bgpre = Buf("gpre")
        gpost_t = sb("gpost_t", [128, 1, D]); bgpost = Buf("gpost")
        x1s_t = sb("x1s_t", [NS, D]); bx1s = Buf("x1s")
        junk = sb("junk", [128, D], BF16); bjunk = Buf("junk")
        ysb = sb("ysb", [128, 2, D]); bysb = [Buf("ysb%d" % i) for i in range(2)]
        h = sb("h", [128, 2, D], BF16); bh = [Buf("h%d" % i) for i in range(2)]
        ms = sb("ms", [128, 40]); rs = sb("rs", [128, 40])
        bms = [Buf("ms%d" % i) for i in range(10)]; brs = [Buf("rs%d" % i) for i in range(10)]
        pT = ps("pT", [128, 8, 128], BF16); bpT = Buf("pT", True)
        pbb = ps("pbb", [128, 2, 512]); pb = [pbb[:, 0, :], pbb[:, 1, :]]; bpb = [Buf("pb%d" % i, True) for i in range(2)]
        pST = [ps("pST%d" % i, [128, 2, 512]) for i in range(2)]; bpST = [Buf("pST%d" % i, True) for i in range(2)]
        pPV = ps("pPV", [128, 512]); bpPV = Buf("pPV", True)
        pbi = [0]

        pbl = [(pb[0], bpb[0]), (pb[1], bpb[1])]

        def nextpb():
            i = pbi[0] % len(pbl)
            pbi[0] += 1
            return pbl[i]

        def mm(out, lhsT, rhs, start, stop, reads, writes, inc):
            S.op("pe", lambda e: e.matmul(out=out, lhsT=lhsT, rhs=rhs, start=start, stop=stop), reads, writes, inc)

        def tp(out, in_, idn, reads, writes, inc):
            S.op("pe", lambda e: e.transpose(out=out, in_=in_, identity=idn), reads, writes, inc)

        def acopy(out, in_, reads, writes):
            S.op("act", lambda e: e.copy(out=out, in_=in_), reads, writes)

        def aact(out, in_, func, reads, writes, scale=1.0, bias=None, accum=None):
            kw = {}
            if bias is not None:
                kw["bias"] = bias
            if accum is not None:
                kw["accum_out"] = accum
            S.op("act", lambda e: e.activation(out=out, in_=in_, func=func, scale=scale, **kw), reads, writes)

        def vcopy(out, in_, reads, writes, eng="dve"):
            S.op(eng, lambda e: e.tensor_copy(out=out, in_=in_), reads, writes)

        def vtt(out, in0, in1, op, reads, writes, eng="dve"):
            S.op(eng, lambda e: e.tensor_tensor(out=out, in0=in0, in1=in1, op=op), reads, writes)

        def vts(out, in0, s1, s2, op0, op1, reads, writes):
            if s2 is None:
                S.op("dve", lambda e: e.tensor_scalar(out=out, in0=in0, scalar1=s1, scalar2=None, op0=op0), reads, writes)
            else:
                S.op("dve", lambda e: e.tensor_scalar(out=out, in0=in0, scalar1=s1, scalar2=s2, op0=op0, op1=op1), reads, writes)

        def vstt(out, in0, scalar, in1, op0, op1, reads, writes):
            S.op("dve", lambda e: e.scalar_tensor_tensor(out=out, in0=in0, scalar=scalar, in1=in1, op0=op0, op1=op1), reads, writes)

        def vrecip(out, in_, reads, writes):
            S.op("dve", lambda e: e.reciprocal(out=out, in_=in_), reads, writes)

        def memset(eng, ap, val, writes):
            S.op(eng, lambda e: e.memset(ap, val), (), writes)

        S.dma("sp", identf[:], ident, writes=[bidf])
        S.dma("sp", gpre_t[:], gpre, writes=[bgpre])
        S.dma("sp", gpost_t[:, 0, :], gpost[:, 0, :], writes=[bgpost])
        vcopy(identb[:], identf[:], [bidf], [bidb])

        def load_weights(warena, src, ncols, dst_col0, layer, kscale_cols=None, chunks=None, on_pool=True):
            if chunks is None:
                chunks = [(c0, min(1024, ncols - c0)) for c0 in range(0, ncols, 1024)]
            for c0, cw in chunks:
                wr = wb(dst_col0 + c0, dst_col0 + c0 + cw)
                extra = None
                if kscale_cols is not None and kscale_cols[0] <= c0 < kscale_cols[1]:
                    extra = kscale_cols[2]
                for kc in range(8):
                    s = load_weights.si % 4
                    load_weights.si += 1
                    S.dma("sp", stg[:, s, 0:cw], src[:, kc, c0:c0 + cw], writes=[bstg[s]])
                    o = warena[:, kc, dst_col0 + c0:dst_col0 + c0 + cw]
                    i = stg[:, s, 0:cw]
                    gsrc = None if layer is None else (gpre16 if extra is not None else gpre_t[:, layer, :])
                    if on_pool:
                        if layer is None:
                            vcopy(o, i, [bstg[s]], wr, eng="pool")
                        else:
                            vtt(o, i, gsrc[:, kc:kc + 1].to_broadcast([128, cw]), ALU.mult, [bstg[s], bgpre], wr, eng="pool")
                    elif layer is None:
                        if load_weights.si % 2:
                            vcopy(o, i, [bstg[s]], wr)
                        else:
                            acopy(o, i, [bstg[s]], wr)
                    elif load_weights.si % 2:
                        vts(o, i, gsrc[:, kc:kc + 1], None, ALU.mult, None, [bstg[s], bgpre], wr)
                    else:
                        aact(o, i, AF.Copy, [bstg[s], bgpre], wr, scale=gsrc[:, kc:kc + 1])
        load_weights.si = 0
        gpre16 = sb("gpre16", [128, 8])
        vts(gpre16[:], gpre_t[:, 1, :], 1.0 / 16.0, None, ALU.mult, None, [bgpre], [bgpre])

        with ExitStack() as st0:
            sb0, _ = mk(st0)
            W0 = sb0("W0", [128, 8, 4096], BF16)
            xr = stg[:, :, 0:D]; bxr = bstg
            xq = sb0("xq", [128, 2, D]); bxq = [Buf("xq%d" % i) for i in range(2)]
            hT = sb0("hT", [128, 8, 512], BF16); bhT = Buf("hT")
            uT = sb0("uT", [128, 4, 528]); buT = Buf("uT")
            qT = sb0("qT", [128, 4, 512], BF16); bqT = Buf("qT")
            kT = sb0("kT", [128, 4, 1024], BF16); bkT = [Buf("kT%d" % i) for i in range(2)]
            gpT = sb0("gpT", [128, 4, 512], BF16); bgpT = Buf("gpT")
            Vr = sb0("Vr", [128, 8, 8, 65], BF16); bV = [Buf("V%d" % i) for i in range(8)]
            gatt = sb0("gatt", [128, 4, 512], BF16); bgatt = [Buf("gatt%d" % i) for i in range(4)]
            pooledT = sb0("pooledT", [128, 4, 512], BF16); bpooled = Buf("pooled")
            tA = sb0("tA", [128, 2, 528]); btA = [Buf("tA0"), Buf("tA1")]
            ostg = tA[:, :, 0:512]; bostg = btA
            tB = sb0("tB", [128, 2, 528]); btB = [Buf("tB0"), Buf("tB1")]
            mixedT = sb0("mixedT", [128, 8, 512], BF16); bmixed = Buf("mixedT")
            E5 = sb0("E5", [128, 8, 5, 128], BF16); bE5 = Buf("E5")
            expo2 = sb0("expo", [128, 2, 640], BF16); bexpo = [Buf("expo0"), Buf("expo1"), Buf("expo2")]
            PT2 = sb0("PT", [128, 2, 640], BF16); bPT = [Buf("PT0"), Buf("PT1"), Buf("PT2")]
            expoS = [expo2[:, 0, :], expo2[:, 1, :], stg[:, 0, 1024:1536].bitcast(BF16)[:, 0:640]]
            PTS = [PT2[:, 0, :], PT2[:, 1, :], stg[:, 1, 1024:1536].bitcast(BF16)[:, 0:640]]
            ST3 = [pST[0], pST[1], pbb]
            bST3 = [[bpST[0]], [bpST[1]], bpb]
            rc = sb0("rc", [128, 2, 4]); brc = [Buf("rc0"), Buf("rc1")]
            tmpn = sb0("tmpn", [128, 2, 256]); btmpn = [Buf("tn0"), Buf("tn1")]
            attg = sb0("attg", [128, 512], BF16); battg = Buf("attg")
            attgs = [attg, stg[:, 2, 1024:1536].bitcast(BF16)[:, 0:512]]; battgs = [battg, Buf("attg2")]
            wmixb = sb0("wmixb", [128, 4, 128], BF16); bwmix = Buf("wmix")
            pscale_t = sb0("pscale_t", [128, 4]); bpscale = Buf("pscale")
            corr_t = sb0("corr_t", [128, 4, 16]); bcorr = Buf("corr")
            hTs = sb0("hTs", [128, 8, NS], BF16); bhTs = Buf("hTs")
            uTs = sb0("uTs", [128, 4, 48]); buTs = Buf("uTs")
            qTs = sb0("qTs", [128, 4, NS], BF16); bqTs = Buf("qTs")
            kTs = sb0("kTs", [128, 4, NS], BF16); bkTs = Buf("kTs")
            cstage = stg[:, 0:2, :].rearrange("p a b -> p (a b)")[:, 0:2048].rearrange("p (b f) -> p b f", b=4)

            wmixf = tB[:, 0, 0:512].rearrange("p (g d) -> p g d", g=4)
            S.dma("sp", wmixf, wmix, writes=[btB[0]])
            vcopy(wmixb[:], wmixf, [btB[0]], [bwmix])
            mask_f = tA[:].rearrange("p a b -> p (a b)")[:, 0:640]
            bmask = btA
            S.dma("sp", pscale_t[:], pscale, writes=[bpscale])
            S.dma("sp", corr_t[:], corr, writes=[bcorr])
            S.dma("sp", mask_f, mask5.rearrange("p d q -> p (d q)"), writes=bmask)
            for hh in range(8):
                s = hh % 4
                S.dma("sp", stg[:, s, 0:640], biasT[:, hh, :, :].rearrange("p d q -> p (d q)"), writes=[bstg[s]])
                aact(stg[:, s, 0:640], stg[:, s, 0:640], AF.Exp, [bstg[s]], [bstg[s]])
                vtt(E5[:, hh, :, :].rearrange("p d q -> p (d q)"), stg[:, s, 0:640], mask_f, ALU.mult,
                    [bstg[s]] + bmask, [bE5])
            memset("pool", Vr[:, :, :, 64:65], 1.0, bV)
            memset("pool", uT[:, :, 0:16], 0.0, [buT])

            def norm_stats(x_ap, nt, col, bx, bm_):
                aact(junk[0:nt, :], x_ap, AF.Square, [bx], [bjunk, bm_], scale=1.0 / 32.0, accum=ms[0:nt, col:col + 1])

            def norm_rstd(nt, c0, c1, bm_, br_):
                aact(rs[0:nt, c0:c1], ms[0:nt, c0:c1], AF.Ln, [bm_], [br_], bias=EPS)
                aact(rs[0:nt, c0:c1], rs[0:nt, c0:c1], AF.Exp, [br_], [br_], scale=-0.5)

            def make_hT(x_ap, nt, col, bx, br_, hslot, hT_out, bhT_out):
                hb = h[0:nt, hslot, :]
                vts(hb, x_ap, rs[0:nt, col:col + 1], None, ALU.mult, None, [bx, br_], [bh[hslot]])
                for kc in range(8):
                    tp(pT[:, kc, 0:nt], hb[:, kc * 128:(kc + 1) * 128], identb[0:nt, 0:nt], [bh[hslot], bidb], [bpT], kc == 7)
                acopy(hT_out, pT[:, :, 0:nt], [bpT], [bhT_out])

            def pool_branch(uTt, buT_, L, first, pooled_ap, bpooled_, gp_ap, bgp_, mixed_out, bmixed_):
                pool_sums(uTt, buT_, L, first, pooled_ap, bpooled_)
                pool_mix(L, pooled_ap, bpooled_, gp_ap, bgp_, mixed_out, bmixed_)

            def pool_sums(uTt, buT_, L, first, pooled_ap, bpooled_):
                for g, w in enumerate(POOLW):
                    tbufs = (tA, btA) if g < 2 else (tB, btB)
                    eng = "dve" if g < 2 else "pool"
                    cur = uTt[:, g, :]
                    curb = buT_
                    base = 16 - (w - 1)
                    lo_t = -(w - 1)
                    k = 1
                    step = 0
                    while k < w:
                        new_lo = lo_t + k
                        n = L - new_lo
                        c_hi = base + k
                        dst = tbufs[0][:, step % 2, 0:n]
                        dstb = tbufs[1][step % 2]
                        vtt(dst, cur[:, c_hi:c_hi + n], cur[:, c_hi - k:c_hi - k + n], ALU.add, [curb], [dstb], eng=eng)
                        cur = tbufs[0][:, step % 2, :]
                        curb = dstb
                        base = 0
                        lo_t = new_lo
                        k *= 2
                        step += 1
                    s_ap = cur[:, 0:L]
                    if first:
                        vtt(cur[:, 0:16], cur[:, 0:16], corr_t[:, g, :], ALU.mult, [curb, bcorr], [curb])
                    vstt(pooled_ap[:, g, 0:L], s_ap, 1.0 / w, uTt[:, g, 16:16 + L], ALU.mult, ALU.subtract, [curb, buT_], [bpooled_])

            def pool_mix(L, pooled_ap, bpooled_, gp_ap, bgp_, mixed_out, bmixed_):
                for g in range(4):
                    bank, bb = nextpb()
                    mm(bank[:, 0:L], wmixb[:, g, :], pooled_ap[:, g, 0:L], True, True, [bwmix, bpooled_], [bb], True)
                    vstt(mixed_out[:, g, 0:L], bank[:, 0:L], pscale_t[:, g:g + 1], gp_ap[:, g, 0:L], ALU.mult, ALU.mult,
                         [bb, bpscale, bgp_], [bmixed_])

            def post_norm_resid(nt, halves, bbs, col, x_ap, bx, layer, yslot, dst_dram, dst_sb=None, bdst=None, bdram=None):
                ya = ysb[0:nt, yslot, :]
                by = bysb[yslot]
                for hf in range(2):
                    aact(junk[0:nt, 0:512], halves[hf], AF.Square, [bbs[hf]], [bjunk, bms[8]], scale=1.0 / 32.0, accum=ms[0:nt, col + hf:col + hf + 1])
                    vcopy(ya[:, hf * 512:(hf + 1) * 512], halves[hf], [bbs[hf]], [by])
                vtt(ms[0:nt, col:col + 1], ms[0:nt, col:col + 1], ms[0:nt, col + 1:col + 2], ALU.add, [bms[8]], [bms[8]])
                aact(rs[0:nt, col:col + 1], ms[0:nt, col:col + 1], AF.Ln, [bms[8]], [brs[8]], bias=EPS)
                aact(rs[0:nt, col:col + 1], rs[0:nt, col:col + 1], AF.Exp, [brs[8]], [brs[8]], scale=-0.5)
                vstt(ya, ya, rs[0:nt, col:col + 1], gpost_t[0:nt, 0, :], ALU.mult, ALU.mult, [by, brs[8], bgpost], [by])
                if dst_sb is None:
                    vtt(ya, ya, x_ap, ALU.add, [by, bx], [by], eng="pool")
                    S.dma("sp", dst_dram, ya, reads=[by], writes=([bdram] if bdram is not None else []), owner=by)
                else:
                    vtt(dst_sb, ya, x_ap, ALU.add, [by, bx], [bdst], eng="pool")

            def pre_load(g):
                S.dma("sp", xr[:, g % 4, :], xp[g * 128:(g + 1) * 128, :], writes=[bxr[g % 4]])

            def pre_stats(m):
                for j in range(4):
                    g = 4 * m + j
                    norm_stats(xr[:, g % 4, :], 128, g, bxr[g % 4], bms[m])
                norm_rstd(128, 4 * m, 4 * m + 4, bms[m], brs[m])

            def pre_T(g):
                m, j = divmod(g, 4)
                make_hT(xr[:, g % 4, :], 128, g, bxr[g % 4], brs[m], g % 2, hT[:, :, j * 128:(j + 1) * 128], bhT)

            def pre_scale(g):
                m, j = divmod(g, 4)
                vts(h[:, g % 2, :], xr[:, g % 4, :], rs[:, g:g + 1], None, ALU.mult, None, [bxr[g % 4], brs[m]], [bh[g % 2]])

            def pre_tp(g):
                m, j = divmod(g, 4)
                hb = h[:, g % 2, :]
                for kc in range(8):
                    tp(pT[:, kc, :], hb[:, kc * 128:(kc + 1) * 128], identb[:, :], [bh[g % 2], bidb], [bpT], kc == 7)
                acopy(hT[:, :, j * 128:(j + 1) * 128], pT[:, :, :], [bpT], [bhT])

            def proj_feat(co, N, rhsT, brhs):
                bank, bb = nextpb()
                for kc in range(8):
                    mm(bank[:, 0:N], W0[:, kc, co:co + 128], rhsT[:, kc, 0:N], kc == 0, kc == 7, wb(co, co + 128) + [brhs], [bb], kc == 7)
                return bank, bb

            def proj_tok(co, nt, lhs, blhs):
                bank, bb = nextpb()
                for kc in range(8):
                    mm(bank[0:nt, :], lhs[:, kc, :], W0[:, kc, co:co + 512], kc == 0, kc == 7, wb(co, co + 512) + [blhs], [bb], kc == 7)
                return bank, bb

            def inf_stage(m):
                slot = m % 2
                for c in range(4):
                    bank, bb = proj_feat(c * 128, 512, hT, bhT)
                    acopy(uT[:, c, 16:528], bank[:, :], [bb], [buT])
                for c in range(4):
                    bank, bb = proj_feat(512 + c * 128, 512, hT, bhT)
                    vcopy(qT[:, c, :], bank[:, :], [bb], [bqT])
                for c in range(4):
                    bank, bb = proj_feat(1024 + c * 128, 512, hT, bhT)
                    if c % 2:
                        vcopy(kT[:, c, slot * 512:(slot + 1) * 512], bank[:, :], [bb], [bkT[slot]])
                    else:
                        acopy(kT[:, c, slot * 512:(slot + 1) * 512], bank[:, :], [bb], [bkT[slot]])
                for c in range(4):
                    bank, bb = proj_feat(2048 + c * 128, 512, hT, bhT)
                    aact(gpT[:, c, :], bank[:, :], AF.Silu, [bb], [bgpT])
                pool_sums(uT, buT, 512, m == 0, pooledT, bpooled)
                pool_hist(m)

            def int_stage(m):
                for j in range(4):
                    g = 4 * m + j
                    lhs = hT[:, :, j * 128:(j + 1) * 128]
                    bank, bb = proj_tok(1536, 128, lhs, bhT)
                    chk(130)
                    vcopy(Vr[:, g % 8, :, 0:64], bank[:, :].rearrange("p (h d) -> p h d", h=8), [bb], [bV[g % 8]])
                    chk(131)
                    if m == NMT - 1:
                        acopy(ostg[:, 0, :], bank[:, :], [bb], [bostg[0]])
                        chk(1315)
                        S.dma("sp", o_v_p[j * 128:(j + 1) * 128, :], ostg[:, 0, :], reads=[bostg[0]])
                    chk(132)
                    bank, bb = proj_tok(2560, 128, lhs, bhT)
                    aact(gatt[:, j, :], bank[:, :], AF.Silu, [bb], [bgatt[j]])
                    chk(133)
                    if m == NMT - 1:
                        bank, bb = proj_tok(1024, 128, lhs, bhT)
                        acopy(ostg[:, 1, :], bank[:, :], [bb], [bostg[1]])
                        S.dma("sp", o_k_p[j * 128:(j + 1) * 128, :], ostg[:, 1, :], reads=[bostg[1]])

            def pool_stage(m):
                pool_mix(512, pooledT, bpooled, gpT, bgpT, mixedT, bmixed)

            def pool_hist(m):
                if m == NMT - 1:
                    bank, bb = nextpb()
                    for g in range(4):
                        tp(bank[0:15, g * 128:(g + 1) * 128], uT[:, g, 513:528], identf[:, :], [buT, bidf], [bb], g == 3)
                    acopy(ostg[0:15, 0, :], bank[0:15, :], [bb], [bostg[0]])
                    S.dma("sp", o_pool_p, ostg[0:15, 0, :], reads=[bostg[0]])
                else:
                    S.op("pool", lambda e: e.tensor_copy(out=uT[:, :, 1:16], in_=uT[:, :, 513:528]), [buT], [buT])

            def att_qk(m, qb, hh, buf):
                gq = 4 * m + qb
                nkb = min(gq, 4) + 1
                r0 = (hh % 2) * 64
                c = hh // 2
                for d in range(nkb):
                    gk = gq - d
                    slot = (gk // 4) % 2
                    off = slot * 512 + (gk % 4) * 128
                    mm(ST3[buf][:, d // 4, (d % 4) * 128:(d % 4) * 128 + 128], kT[r0:r0 + 64, c, off:off + 128], qT[r0:r0 + 64, c, qb * 128:(qb + 1) * 128],
                       True, True, [bkT[slot], bqT], bST3[buf], d == nkb - 1)
                n = nkb * 128
                src = ST3[buf][:].rearrange("p a b -> p (a b)")[:, 0:n]
                aact(expoS[buf][:, 0:n], src, AF.Exp, bST3[buf], [bexpo[buf]], scale=0.125)
                vtt(PTS[buf][:, 0:n], expoS[buf][:, 0:n], E5[:, hh, 0:nkb, :].rearrange("p d q -> p (d q)"), ALU.mult, [bexpo[buf], bE5], [bPT[buf]])

            def att_pv(m, qb, hh, buf):
                gq = 4 * m + qb
                nkb = min(gq, 4) + 1
                hq = hh % 4
                for d in range(nkb):
                    gk = gq - d
                    mm(pPV[:, hq * 65:hq * 65 + 65], PTS[buf][:, d * 128:(d + 1) * 128], Vr[:, gk % 8, hh, :], d == 0, d == nkb - 1,
                       [bPT[buf], bV[gk % 8]], [bpPV], d == nkb - 1)

            def att_norm(qb, hg, gatt_ap, bg_, nt=128, asl=0):
                pv = pPV[0:nt, 0:260].rearrange("p (h d) -> p h d", h=4)
                r = hg % 2
                vrecip(rc[0:nt, r, :], pv[:, :, 64], [bpPV], [brc[r]])
                tn = tmpn[0:nt, r, :].rearrange("p (h d) -> p h d", h=4)
                vtt(tn, pv[:, :, 0:64], rc[0:nt, r, :].unsqueeze(2).to_broadcast([nt, 4, 64]), ALU.mult, [bpPV, brc[r]], [btmpn[r]])
                vtt(attgs[asl][0:nt, hg * 256:(hg + 1) * 256], tmpn[0:nt, r, :], gatt_ap[:, hg * 256:(hg + 1) * 256], ALU.mult, [btmpn[r], bg_], [battgs[asl]])

            def att_stage(m, between=None):
                def flush(qb):
                    a = attgs[qb % 2]
                    for c in range(4):
                        tp(pT[:, c, :], a[:, c * 128:(c + 1) * 128], identb[:, :], [battgs[qb % 2], bidb], [bpT], c == 3)
                    acopy(mixedT[:, 4:8, qb * 128:(qb + 1) * 128], pT[:, 0:4, :], [bpT], [bmixed])
                pending = None
                for qb in range(4):
                    if between is not None:
                        between(qb)
                    seq = list(range(8))
                    att_qk(m, qb, 0, 0)
                    att_qk(m, qb, 1, 1)
                    if pending is not None:
                        flush(pending)
                    for hh in seq:
                        if hh + 2 < 8:
                            att_qk(m, qb, hh + 2, (hh + 2) % 3)
                        att_pv(m, qb, hh, hh % 3)
                        if hh % 4 == 3:
                            att_norm(qb, hh // 4, gatt[:, qb, :], bgatt[qb], asl=qb % 2)
                    pending = qb
                flush(pending)

            def out_stage(m):
                for j in range(4):
                    g = 4 * m + j
                    S.dma("sp", xq[:, g % 2, :], xp[g * 128:(g + 1) * 128, :], writes=[bxq[g % 2]])
                    halves = []
                    bbs = []
                    for hf in range(2):
                        bank, bb = nextpb()
                        for kc in range(8):
                            mm(bank[:, :], mixedT[:, kc, j * 128:(j + 1) * 128], W0[:, kc, 3072 + hf * 512:3072 + (hf + 1) * 512], kc == 0, kc == 7,
                               [bmixed] + wb(3072 + hf * 512, 3584 + hf * 512), [bb], kc == 7)
                        halves.append(bank[:, :])
                        bbs.append(bb)
                    post_norm_resid(128, halves, bbs, 36, xq[:, g % 2, :], bxq[g % 2], 0, g % 2, x1d[g * 128:(g + 1) * 128, :], bdram=bx1d[g])

            def chk(n):
                if stage == n:
                    raise StopIteration
            def l0_all():
              chk(10)
              for g in range(4):
                pre_load(g)
              pre_stats(0)
              for g in range(4):
                pre_T(g)
              load_weights(W0, w0in, 3072, 0, 0)
              load_weights(W0, w0out, 1024, 3072, None)
              chk(11)
              for m in range(NMT):
                inf_stage(m)
                chk(12)
                int_stage(m)
                chk(13)
                if m + 1 < NMT:
                    for g in range(4 * (m + 1), 4 * (m + 2)):
                        pre_load(g)
                    pre_stats(m + 1)
                pool_stage(m)
                chk(14)
                if m + 1 < NMT:
                    def btw(qb, m=m):
                        g = 4 * (m + 1) + qb
                        pre_tp(g)
                        if qb < 3:
                            pre_scale(g + 1)
                    pre_scale(4 * (m + 1))
                    att_stage(m, between=btw)
                else:
                    att_stage(m)
                chk(15)
                out_stage(m)
                chk(16)
              sample_l0()

            def sample_l0():
                bxs = bxr[3]
                xs_t = xr[0:NS, 3, :]
                S.dma("sp", xs_t, xs, writes=[bxs])
                norm_stats(xs_t, NS, 32, bxs, bms[9])
                norm_rstd(NS, 32, 33, bms[9], brs[9])
                make_hT(xs_t, NS, 32, bxs, brs[9], 0, hTs[:, :, :], bhTs)
                kTc = qT; bkTc = bqT
                Vc = Vr[:, 0:5, :, :]
                cstb = pooledT; bcstb = bpooled
                ES = E5[:].rearrange("p h d q -> p (h d q)")[:, 0:1280].rearrange("p (h d q) -> p h d q", h=8, d=5); bES = bE5
                S.dma("sp", stg[:, 2, 0:1280], biasS.rearrange("p h d q -> p (h d q)"), writes=[bstg[2], battgs[1]])
                aact(ES.rearrange("p h d q -> p (h d q)"), stg[:, 2, 0:1280], AF.Exp, [bstg[2]], [bES])
                S.dma("sp", cstage[:, :, :], ck.rearrange("(b p) f -> p b f", p=128), writes=[bstg[0], bstg[1], bexpo[2], bPT[2]])
                vcopy(cstb[:], cstage[:], [bstg[0], bstg[1]], [bcstb])
                for blk in range(4):
                    for c in range(4):
                        tp(pT[:, c, :], cstb[:, blk, c * 128:(c + 1) * 128], identb[:, :], [bcstb, bidb], [bpT], c == 3)
                    acopy(kTc[:, :, blk * 128:(blk + 1) * 128], pT[:, 0:4, :], [bpT], [bkTc])
                S.dma("sp", cstage[:, :, :], cv.rearrange("(b p) f -> p b f", p=128), writes=[bstg[0], bstg[1], bexpo[2], bPT[2]])
                vcopy(Vc[:, 0:4, :, 0:64], cstage[:].rearrange("p b (h d) -> p b h d", h=8), [bstg[0], bstg[1]], bV[0:5])
                S.dma("sp", ostg[0:15, 0, :], cpool, writes=[bostg[0]])
                bank, bb = nextpb()
                for g in range(4):
                    tp(bank[:, g * 16:g * 16 + 15], ostg[0:15, 0, g * 128:(g + 1) * 128], identf[0:15, 0:15], [bostg[0], bidf], [bb], g == 3)
                acopy(uTs[:, :, 1:16], bank[:, 0:64].rearrange("p (g t) -> p g t", g=4)[:, :, 0:15], [bb], [buTs])
                for c in range(4):
                    bank, bb = proj_feat(c * 128, NS, hTs, bhTs)
                    acopy(uTs[:, c, 16:48], bank[:, 0:NS], [bb], [buTs])
                for c in range(4):
                    bank, bb = proj_feat(512 + c * 128, NS, hTs, bhTs)
                    vcopy(qTs[:, c, :], bank[:, 0:NS], [bb], [bqTs])
                for c in range(4):
                    bank, bb = proj_feat(1024 + c * 128, NS, hTs, bhTs)
                    vcopy(kTs[:, c, :], bank[:, 0:NS], [bb], [bkTs])
                for c in range(4):
                    bank, bb = proj_feat(2048 + c * 128, NS, hTs, bhTs)
                    aact(gpT[:, c, 0:NS], bank[:, 0:NS], AF.Silu, [bb], [bgpT])
                bank, bb = proj_tok(0, NS, hTs, bhTs)
                acopy(ostg[0:NS, 0, :], bank[0:NS, :], [bb], [bostg[0]])
                S.dma("sp", o_pool_s, ostg[17:32, 0, :], reads=[bostg[0]])
                bank, bb = proj_tok(1024, NS, hTs, bhTs)
                acopy(ostg[0:NS, 1, :], bank[0:NS, :], [bb], [bostg[1]])
                S.dma("sp", o_k_s, ostg[0:NS, 1, :], reads=[bostg[1]])
                bank, bb = proj_tok(1536, NS, hTs, bhTs)
                acopy(ostg[0:NS, 0, :], bank[0:NS, :], [bb], [bostg[0]])
                S.dma("sp", o_v_s, ostg[0:NS, 0, :], reads=[bostg[0]])
                vcopy(Vc[0:NS, 4, :, 0:64], bank[0:NS, :].rearrange("p (h d) -> p h d", h=8), [bb], bV[0:5])
                bank, bb = proj_tok(2560, NS, hTs, bhTs)
                aact(gatt[0:NS, 0, :], bank[0:NS, :], AF.Silu, [bb], [bgatt[0]])
                pool_branch(uTs, buTs, NS, False, pooledT, bpooled, gpT, bgpT, mixedT, bmixed)
                for hh in range(8):
                    buf = hh % 2
                    r0 = (hh % 2) * 64
                    c = hh // 2
                    for blk in range(4):
                        mm(pST[buf][:, 0, blk * NS:(blk + 1) * NS], kTc[r0:r0 + 64, c, blk * 128:(blk + 1) * 128], qTs[r0:r0 + 64, c, :], True, True,
                           [bkTc, bqTs], [bpST[buf]], False)
                    mm(pST[buf][0:NS, 0, 4 * NS:5 * NS], kTs[r0:r0 + 64, c, :], qTs[r0:r0 + 64, c, :], True, True, [bkTs, bqTs], [bpST[buf]], True)
                    aact(expoS[buf][:, 0:4 * NS], pST[buf][:, 0, 0:4 * NS], AF.Exp, [bpST[buf]], [bexpo[buf]], scale=0.125)
                    aact(expoS[buf][0:NS, 4 * NS:5 * NS], pST[buf][0:NS, 0, 4 * NS:5 * NS], AF.Exp, [bpST[buf]], [bexpo[buf]], scale=0.125)
                    vtt(PTS[buf][:, 0:4 * NS], expoS[buf][:, 0:4 * NS], ES[:, hh, 0:4, :].rearrange("p d q -> p (d q)"), ALU.mult, [bexpo[buf], bES], [bPT[buf]])
                    vtt(PTS[buf][0:NS, 4 * NS:5 * NS], expoS[buf][0:NS, 4 * NS:5 * NS], ES[0:NS, hh, 4, :], ALU.mult, [bexpo[buf], bES], [bPT[buf]])
                    hq = hh % 4
                    for blk in range(4):
                        mm(pPV[0:NS, hq * 65:hq * 65 + 65], PTS[buf][:, blk * NS:(blk + 1) * NS], Vc[:, blk, hh, :], blk == 0, False, [bPT[buf]] + bV[0:5], [bpPV], False)
                    mm(pPV[0:NS, hq * 65:hq * 65 + 65], PTS[buf][0:NS, 4 * NS:5 * NS], Vc[0:NS, 4, hh, :], False, True, [bPT[buf]] + bV[0:5], [bpPV], True)
                    if hh % 4 == 3:
                        att_norm(0, hh // 4, gatt[0:NS, 0, :], bgatt[0], nt=NS)
                for c in range(4):
                    tp(pT[:, c, 0:NS], attg[0:NS, c * 128:(c + 1) * 128], identb[0:NS, 0:NS], [battg, bidb], [bpT], c == 3)
                acopy(mixedT[:, 4:8, 0:NS], pT[:, 0:4, 0:NS], [bpT], [bmixed])
                halves = []
                bbs = []
                for hf in range(2):
                    bank, bb = nextpb()
                    for kc in range(8):
                        mm(bank[0:NS, :], mixedT[:, kc, 0:NS], W0[:, kc, 3072 + hf * 512:3072 + (hf + 1) * 512], kc == 0, kc == 7, [bmixed] + wb(3072 + hf * 512, 3584 + hf * 512), [bb], kc == 7)
                    halves.append(bank[0:NS, :])
                    bbs.append(bb)
                post_norm_resid(NS, halves, bbs, 38, xs_t, bxs, 0, 0, None, dst_sb=x1s_t[:, :], bdst=bx1s)

            try:
                l0_all()
            except StopIteration:
                pass

            if stage == 1:
                for g in range(4 * NMT):
                    S.dma("sp", xq[:, g % 2, :], x1d[g * 128:(g + 1) * 128, :], reads=[bx1d[g]], writes=[bxq[g % 2]], owner=bxq[g % 2])
                    S.dma("sp", yp[g * 128:(g + 1) * 128, :], xq[:, g % 2, :], reads=[bxq[g % 2]])
                S.dma("sp", ys, x1s_t[:, :], reads=[bx1s])
            S.barrier()
            S.emit()


        if stage != 1:
          with ExitStack() as st1:
            sb1, _ = mk(st1)
            W1 = sb1("W1", [128, 8, 6152], BF16)
            S.dma("sp", gpost_t[:, 0, :], gpost[:, 1, :], writes=[bgpost])
            load_weights(W1, w1in, 5128, 0, 1, kscale_cols=(1024, 2048, 1.0 / 16.0),
                         chunks=[(5120, 8), (0, 1024), (1024, 1024), (2048, 1024), (3072, 1024), (4096, 1024)], on_pool=False)
            load_weights(W1, w1out, 1024, 5128, None, on_pool=False)
            xr = stg[:, :, 0:D]; bxr = bstg
            hT1a = sb1("hT1", [128, 2, 8, 128], BF16); bhT1a = [Buf("hT1_0"), Buf("hT1_1")]
            qT1 = sb1("qT1", [128, 3, 8, 128], BF16); bqT1 = [Buf("qT1_%d" % i) for i in range(3)]
            kT1 = sb1("kT1", [128, 2, 8, 128], BF16); bkT1 = [Buf("kT1a"), Buf("kT1b")]
            ktok = sb1("ktok", [128, 2, D], BF16); bktok = [Buf("ktoka"), Buf("ktokb")]
            vx = sb1("vx", [128, 3, 4, 257], BF16); bvx = [Buf("vx_%d" % i) for i in range(3)]
            wv = sb1("wv", [128, 4, 257], BF16); bwv = Buf("wv")
            St = stg[:, 3, 1024:1536].bitcast(BF16).rearrange("p (s h t) -> p s h t", s=2, h=4); bSt = [Buf("St0"), Buf("St1")]
            Cf = sb1("Cf", [128, 4, 2, 257]); bCfh = [Buf("Cf%d" % i) for i in range(4)]
            Cb = sb1("Cb", [128, 2, 4, 2, 257], BF16); bCb = [Buf("Cb0"), Buf("Cb1")]
            hc = sb1("hc", [128, D]); bhch = [Buf("hc%d" % i) for i in range(4)]
            junk2 = stg[:, 2, 1024:1280].bitcast(BF16).rearrange("p (a b) -> p a b", a=2); bjunk2 = [Buf("j2a"), Buf("j2b")]
            sg = sb1("sg", [128, 3, D], BF16); bsg = [Buf("sg_%d" % i) for i in range(3)]
            sz = sb1("sz", [128, D], BF16); bsz = Buf("sz")
            outm = stg[:, 1, 1024:1536].bitcast(BF16); boutm = Buf("outm")
            outT = stg[:, 0, 1024:1536].bitcast(BF16).rearrange("p (k t) -> p k t", k=8); boutT = Buf("outT")
            gml_t = sb1("gml_t", [128, D]); bgml = Buf("gml")
            tri_t = sb1("tri_t", [128, 128]); btri = Buf("tri")
            bg_t = sb1("bg_t", [128, 8]); bbg = Buf("bg")
            ones4 = sb1("ones4", [4, 128]); bones4 = Buf("ones4")
            gb = sb1("gb", [128, 2, 8]); bgb = [Buf("gba"), Buf("gbb")]
            g4 = sb1("g4", [128, 2, 16]); bg4 = [Buf("g4a"), Buf("g4b")]
            big4 = sb1("big4", [4, 256]); bbig4 = Buf("big4")
            cbt = sb1("cbt", [4, 2, 256]); bcbt = [Buf("cbta"), Buf("cbtb")]
            sm4 = sb1("sm4", [4, 16]); bsm4 = Buf("sm4")
            mprev = sb1("mprev", [4, 1]); bmprev = Buf("mprev")
            gsb = sb1("gsb", [128, 2, 12]); bgsb = [Buf("gsb0"), Buf("gsb1")]
            st8 = sb1("st8", [128, 4, 8]); bst8 = [Buf("st8_%d" % i) for i in range(4)]
            ctr = hc[:, 0:512].rearrange("p (a b) -> p a b", a=2); bctr = [bhch[0], bhch[1]]

            S.dma("sp", gml_t[:], gml, writes=[bgml])
            S.dma("sp", tri_t[:], tri, writes=[btri])
            S.dma("sp", bg_t[:], bgate, writes=[bbg])
            memset("pool", ones4[:], 1.0, [bones4])
            memset("pool", vx[:, :, :, 256:257], 1.0, bvx)

            bro = [Buf("ro0", True), Buf("ro1", True)]
            pbl.extend([(pST[0][:, 0, :], Buf("pq0", True)), (pST[0][:, 1, :], Buf("pq1", True))])

            def genA(nt, x_ap, bx, col, sl, s3):
                hT1 = hT1a[:, sl, :, :]
                bhT1 = bhT1a[sl]
                norm_stats(x_ap, nt, col, bx, bms[9])
                norm_rstd(nt, col, col + 1, bms[9], brs[9])
                hb = h[0:nt, sl, :]
                vts(hb, x_ap, rs[0:nt, col:col + 1], None, ALU.mult, None, [bx, brs[9]], [bh[sl]])
                yield
                for kc in range(8):
                    tp(pT[:, kc, 0:nt], hb[:, kc * 128:(kc + 1) * 128], identb[0:nt, 0:nt], [bh[sl], bidb], [bpT], kc == 7)
                acopy(hT1[:, :, 0:nt], pT[:, :, 0:nt], [bpT], [bhT1])
                yield
                bank, bb = nextpb()
                for kc in range(8):
                    mm(bank[0:nt, 0:8], hT1[:, kc, 0:nt], W1[:, kc, 5120:5128], kc == 0, kc == 7, wb(5120, 5128) + [bhT1], [bb], kc == 7)
                vtt(gb[0:nt, sl, :], bank[0:nt, 0:8], bg_t[0:nt, :], ALU.add, [bb, bbg], [bgb[sl]])
                aact(g4[0:nt, sl, 0:4], gb[0:nt, sl, 4:8], AF.Exp, [bgb[sl]], [bg4[sl]], scale=-1.0)
                aact(g4[0:nt, sl, 4:8], g4[0:nt, sl, 0:4], AF.Ln, [bg4[sl]], [bg4[sl]], bias=1.0)
                yield
                def tokproj(co):
                    bank, bb = nextpb()
                    for kc in range(8):
                        mm(bank[0:nt, :], hT1[:, kc, 0:nt], W1[:, kc, co:co + 512], kc == 0, kc == 7, wb(co, co + 512) + [bhT1], [bb], kc == 7)
                    return bank, bb
                bank, bb = tokproj(0)
                acopy(sz[0:nt, 0:512], bank[0:nt, :], [bb], [bsz])
                yield
                bank, bb = nextpb()
                mm(bank[0:nt, 0:4], tri_t[0:nt, 0:nt], g4[0:nt, sl, 4:8], True, True, [btri, bg4[sl]], [bb], True)
                vtt(g4[0:nt, sl, 8:12], gb[0:nt, sl, 0:4], bank[0:nt, 0:4], ALU.add, [bgb[sl], bb], [bg4[sl]])
                vcopy(g4[0:nt, sl, 12:16], bank[0:nt, 0:4], [bb], [bg4[sl]])
                yield
                bank, bb = tokproj(512)
                vcopy(sz[0:nt, 512:1024], bank[0:nt, :], [bb], [bsz])
                yield
                bank, bb = nextpb()
                tp(bank[0:4, 0:nt], g4[0:nt, sl, 8:12], identf[0:nt, 0:nt], [bg4[sl], bidf], [bb], False)
                tp(bank[0:4, 128:128 + nt], g4[0:nt, sl, 12:16], identf[0:nt, 0:nt], [bg4[sl], bidf], [bb], True)
                vcopy(cbt[:, sl, 0:256], bank[0:4, 0:256], [bb], [bcbt[sl]])
                for kc in range(8):
                    tp(pT[:, kc, 0:nt], sz[0:nt, kc * 128:(kc + 1) * 128], identb[0:nt, 0:nt], [bsz, bidb], [bpT], kc == 7)
                vcopy(qT1[:, s3, :, 0:nt], pT[:, :, 0:nt], [bpT], [bqT1[s3]])
                yield
                for half in range(2):
                    bank, bb = tokproj(1024 + half * 512)
                    acopy(ktok[0:nt, sl, half * 512:(half + 1) * 512], bank[0:nt, :], [bb], [bktok[sl]])
                    yield
                for half in range(2):
                    bank, bb = tokproj(2048 + half * 512)
                    vcopy(vx[0:nt, s3, half * 2:half * 2 + 2, 0:256], bank[0:nt, :].rearrange("p (h d) -> p h d", h=2), [bb], [bvx[s3]])
                    if half == 0:
                        for kc in range(8):
                            tp(pT[:, kc, 0:nt], ktok[0:nt, sl, kc * 128:(kc + 1) * 128], identb[0:nt, 0:nt], [bktok[sl], bidb], [bpT], kc == 7)
                        acopy(kT1[:, sl, :, 0:nt], pT[:, :, 0:nt], [bpT], [bkT1[sl]])
                    yield
                for half in range(2):
                    bank, bb = tokproj(3072 + half * 512)
                    aact(sg[0:nt, s3, half * 512:(half + 1) * 512], bank[0:nt, :], AF.Sigmoid, [bb], [bsg[s3]])
                    yield
                for half in range(2):
                    bank, bb = tokproj(4096 + half * 512)
                    aact(sz[0:nt, half * 512:(half + 1) * 512], bank[0:nt, :], AF.Silu, [bb], [bsz])
                    yield
                vtt(sg[0:nt, s3, :], sg[0:nt, s3, :], sz[0:nt, :], ALU.mult, [bsg[s3], bsz], [bsg[s3]], eng="pool")
                vtt(sg[0:nt, s3, :], sg[0:nt, s3, :], gml_t[0:nt, :], ALU.mult, [bsg[s3], bgml], [bsg[s3]], eng="pool")
                yield

            def genBe(nt, sl, s3):
                G = gsb[:, sl, :]
                S.op("dve", lambda e: e.tensor_reduce(out=sm4[:, 0:1], in_=cbt[:, sl, 0:nt], axis=mybir.AxisListType.X, op=ALU.max), [bcbt[sl]], [bsm4])
                vtt(sm4[:, 1:2], sm4[:, 0:1], mprev[:, :], ALU.max, [bsm4, bmprev], [bsm4])
                vts(sm4[:, 2:3], sm4[:, 1:2], -1.0, None, ALU.mult, None, [bsm4], [bsm4])
                aact(sm4[:, 3:4], mprev[:, :], AF.Exp, [bmprev, bsm4], [bsm4], bias=sm4[:, 2:3])
                aact(big4[:, 0:nt], cbt[:, sl, 0:nt], AF.Exp, [bcbt[sl], bsm4], [bbig4], bias=sm4[:, 2:3])
                aact(big4[:, 128:128 + nt], cbt[:, sl, 128:128 + nt], AF.Exp, [bcbt[sl], bsm4], [bbig4], bias=sm4[:, 2:3])
                vtt(mprev[:, :], sm4[:, 1:2], cbt[:, sl, 128 + nt - 1:128 + nt], ALU.subtract, [bsm4, bcbt[sl]], [bmprev])
                vts(sm4[:, 4:8], identf[0:4, 0:4], sm4[:, 3:4], None, ALU.mult, None, [bidf, bsm4], [bsm4])
                yield
                bank, bb = nextpb()
                mm(bank[0:nt, 0:4], big4[:, 0:nt], identf[0:4, 0:4], True, True, [bbig4, bidf], [bb], False)
                mm(bank[0:nt, 4:8], big4[:, 128:128 + nt], identf[0:4, 0:4], True, True, [bbig4, bidf], [bb], False)
                mm(bank[:, 8:12], ones4[:, :], sm4[:, 4:8], True, True, [bones4, bsm4], [bb], True)
                vcopy(G[0:nt, 0:8], bank[0:nt, 0:8], [bb], [bgsb[sl]])
                vcopy(G[:, 8:12], bank[:, 8:12], [bb], [bgsb[sl]])
                yield
                for hh in range(4):
                    aact(Cf[:, hh, :, :], Cf[:, hh, :, :], AF.Copy, [bCfh[hh], bgsb[sl]], [bCfh[hh]], scale=G[:, 8 + hh:9 + hh])
                    vcopy(Cb[:, sl, hh, :, :], Cf[:, hh, :, :], [bCfh[hh]], [bCb[sl]], eng="pool")
                S.op("dve", lambda e: e.tensor_tensor(out=wv[0:nt, :, :], in0=vx[0:nt, s3, :, :], in1=G[0:nt, 0:4].unsqueeze(2).to_broadcast([nt, 4, 257]), op=ALU.mult),
                     [bvx[s3], bgsb[sl]], [bwv])
                yield
                for hh in range(4):
                    for c in range(2):
                        bank, bb = nextpb()
                        mm(bank[:, 0:257], ktok[0:nt, sl, hh * 256 + c * 128:hh * 256 + (c + 1) * 128], wv[0:nt, hh, :], True, True, [bktok[sl], bwv], [bb], True)
                        vtt(Cf[:, hh, c, :], bank[:, 0:257], Cf[:, hh, c, :], ALU.add, [bCfh[hh], bb], [bCfh[hh]])
                    if hh % 2 == 1:
                        yield
                for hh in range(4):
                    for c in range(2):
                        mm(pPV[0:nt, hh * 128:hh * 128 + nt], kT1[:, sl, 2 * hh + c, 0:nt], qT1[:, s3, 2 * hh + c, 0:nt], c == 0, c == 1, [bkT1[sl], bqT1[s3]], [bpPV], hh == 3 and c == 1)
                for hh in range(4):
                    vstt(St[0:nt, sl, hh, 0:nt], pPV[0:nt, hh * 128:hh * 128 + nt], G[0:nt, hh:hh + 1], tri_t[0:nt, 0:nt], ALU.mult, ALU.mult, [bpPV, bgsb[sl], btri], [bSt[sl]])
                yield

            def genBl(nt, x_ap, bx, sl, s3, dst_dram):
                G = gsb[:, sl, :]
                for hh in range(4):
                    ro = pST[1][0:nt, hh % 2, 0:257]
                    for c in range(2):
                        mm(ro, qT1[:, s3, 2 * hh + c, 0:nt], Cb[:, sl, hh, c, :], c == 0, False, [bqT1[s3], bCb[sl]], [bro[hh % 2]], False)
                    mm(ro, St[0:nt, sl, hh, 0:nt], vx[0:nt, s3, hh, :], False, True, [bSt[sl], bvx[s3]], [bro[hh % 2]], True)
                    b8 = [bst8[hh]]
                    vcopy(st8[0:nt, hh, 2:3], ro[:, 256:257], [bro[hh % 2]], b8)
                    vstt(st8[0:nt, hh, 0:1], st8[0:nt, hh, 2:3], -1.0, st8[0:nt, hh, 2:3], ALU.mult, ALU.max, b8, b8)
                    vtt(st8[0:nt, hh, 0:1], st8[0:nt, hh, 0:1], G[0:nt, 4 + hh:5 + hh], ALU.max, b8 + [bgsb[sl]], b8)
                    vrecip(st8[0:nt, hh, 1:2], st8[0:nt, hh, 0:1], b8, b8)
                    aact(hc[0:nt, hh * 256:(hh + 1) * 256], ro[:, 0:256], AF.Copy, [bro[hh % 2]] + b8, [bhch[hh]] + b8, scale=st8[0:nt, hh, 1:2], accum=st8[0:nt, hh, 3:4])
                    aact(junk2[0:nt, hh % 2, :], hc[0:nt, hh * 256:(hh + 1) * 256], AF.Square, [bhch[hh]], [bjunk2[hh % 2]] + b8, accum=st8[0:nt, hh, 4:5])
                    yield
                vts(st8[0:nt, :, 5], st8[0:nt, :, 3], 1.0 / 256.0, None, ALU.mult, None, bst8, bst8)
                vtt(st8[0:nt, :, 6], st8[0:nt, :, 5], st8[0:nt, :, 5], ALU.mult, bst8, bst8)
                vts(st8[0:nt, :, 4], st8[0:nt, :, 4], 1.0 / 256.0, None, ALU.mult, None, bst8, bst8)
                vtt(st8[0:nt, :, 7], st8[0:nt, :, 4], st8[0:nt, :, 6], ALU.subtract, bst8, bst8)
                aact(st8[0:nt, :, 7], st8[0:nt, :, 7], AF.Ln, bst8, bst8, bias=EPS)
                aact(st8[0:nt, :, 7], st8[0:nt, :, 7], AF.Exp, bst8, bst8, scale=-0.5)
                for hh in range(4):
                    vts(hc[0:nt, hh * 256:(hh + 1) * 256], hc[0:nt, hh * 256:(hh + 1) * 256], st8[0:nt, hh, 5:6], st8[0:nt, hh, 7:8],
                        ALU.subtract, ALU.mult, [bhch[hh], bst8[hh]], [bhch[hh]])
                vtt(outm[0:nt, :], hc[0:nt, :], sg[0:nt, s3, :], ALU.mult, bhch + [bsg[s3]], [boutm])
                yield
                for kc in range(8):
                    tp(pT[:, kc, 0:nt], outm[0:nt, kc * 128:(kc + 1) * 128], identb[0:nt, 0:nt], [boutm, bidb], [bpT], kc == 7)
                acopy(outT[:, :, 0:nt], pT[:, :, 0:nt], [bpT], [boutT])
                yield
                halves = []
                bbs = []
                for hf in range(2):
                    bank, bb = nextpb()
                    for kc in range(8):
                        mm(bank[0:nt, :], outT[:, kc, 0:nt], W1[:, kc, 5128 + hf * 512:5128 + (hf + 1) * 512], kc == 0, kc == 7, [boutT] + wb(5128 + hf * 512, 5640 + hf * 512), [bb], kc == 7)
                    halves.append(bank[0:nt, :])
                    bbs.append(bb)
                post_norm_resid(nt, halves, bbs, 36, x_ap, bx, 1, 0, dst_dram)
                yield

            def drive(gens, order=None, nodrain=()):
                live = {i: g for i, g in enumerate(gens) if g is not None}
                for i in (order or ()):
                    if i in live:
                        try:
                            next(live[i])
                        except StopIteration:
                            del live[i]
                while [i for i in live if i not in nodrain]:
                    for i in sorted(live):
                        if i in nodrain:
                            continue
                        try:
                            next(live[i])
                        except StopIteration:
                            del live[i]

            def state_out(oC, on, om):
                stage = [hc[:, :].rearrange("p (h vb k) -> p h vb k", h=2, vb=2), ysb[:, 1, :].rearrange("p (h vb k) -> p h vb k", h=2, vb=2)]
                bst = [bhch, [bysb[1]]]
                for hh in range(4):
                    sg_ = stage[hh // 2]
                    for vb in range(2):
                        bank, bb = nextpb()
                        for c in range(2):
                            tp(bank[:, c * 128:(c + 1) * 128], Cf[:, hh, c, vb * 128:(vb + 1) * 128], identf[:, :], bCfh + [bidf], [bb], c == 1)
                        if vb == 0:
                            acopy(sg_[:, hh % 2, vb, :], bank[:, 0:256], [bb], bst[hh // 2])
                        else:
                            vcopy(sg_[:, hh % 2, vb, :], bank[:, 0:256], [bb], bst[hh // 2])
                    S.dma("sp", oC[hh].rearrange("(vb p) k -> p vb k", p=128), sg_[:, hh % 2, :, :], reads=bst[hh // 2], owner=bst[hh // 2][0])
                    for c in range(2):
                        S.dma("sp", on[hh:hh + 1, c * 128:(c + 1) * 128].rearrange("a k -> k a"), Cf[:, hh, c, 256:257], reads=bCfh, owner=bst[hh // 2][0])
                S.dma("sp", om, mprev[:, :], reads=[bmprev], owner=bysb[1])

            memset("pool", Cf[:], 0.0, bCfh)
            memset("pool", mprev[:], 0.0, [bmprev])
            NCH = 4 * NMT

            def mkA(g):
                if g >= NCH:
                    return None
                slot = g % 4
                S.dma("sp", xr[:, slot, :], x1d[g * 128:(g + 1) * 128, :], reads=[bx1d[g]], writes=[bxr[slot]], owner=bxr[slot])
                return genA(128, xr[:, slot, :], bxr[slot], 32, g % 2, g % 3)

            def mkBe(g):
                return genBe(128, g % 2, g % 3) if g < NCH else None

            def mkBl(g):
                if g < 0:
                    return None
                slot = g % 4
                return genBl(128, xr[:, slot, :], bxr[slot], g % 2, g % 3, yp[g * 128:(g + 1) * 128, :])
            drive([mkA(0)])
            A_next = mkA(1)
            if A_next is not None:
                next(A_next)
                next(A_next)
            for g in range(NCH + 1):
                A_after = mkA(g + 2)
                if g == NCH:
                    A_next = genA(NS, x1s_t[:, :], bx1s, 33, 0, 0)
                drive([mkBl(g - 1), mkBe(g), A_next, A_after],
                      order=[1, 0, 2, 0, 2, 1, 0, 2, 1, 0, 2, 3, 2, 1, 0, 2, 1, 3, 2, 1, 2, 0, 2, 2, 0, 2, 2, 2, 2], nodrain=(3,))
                A_next = A_after
            state_out(o_C_p, o_n_p, o_m_p)
            for hh in range(4):
                c0 = xr[:, hh, 0:512].rearrange("p (vb k) -> p vb k", vb=2)
                S.dma("sp", c0, sC[hh].rearrange("(vb p) k -> p vb k", p=128), writes=[bxr[hh]], owner=bxr[hh])
            for hh in range(4):
                c0 = xr[:, hh, 0:512].rearrange("p (vb k) -> p vb k", vb=2)
                for c in range(2):
                    bank, bb = nextpb()
                    for vb in range(2):
                        tp(bank[:, vb * 128:(vb + 1) * 128], c0[:, vb, c * 128:(c + 1) * 128], identf[:, :], [bxr[hh], bidf], [bb], vb == 1)
                    if c == 0:
                        acopy(Cf[:, hh, c, 0:256], bank[:, 0:256], [bb], [bCfh[hh]])
                    else:
                        vcopy(Cf[:, hh, c, 0:256], bank[:, 0:256], [bb], [bCfh[hh]])
                    S.dma("sp", Cf[:, hh, c, 256:257], sn[hh:hh + 1, c * 128:(c + 1) * 128].rearrange("a k -> k a"), writes=[bCfh[hh]], owner=bysb[1])
            S.dma("sp", mprev[:, :], sm, writes=[bmprev], owner=bysb[1])
            drive([genBe(NS, 0, 0)])
            drive([genBl(NS, x1s_t[:, :], bx1s, 0, 0, ys)])
            state_out(o_C_s, o_n_s, o_m_s)
            S.barrier()
            S.emit()

        S.barrier()
        S.emit()
    return nc


_CACHE = {}


def _host_consts(rel_bias):
    tab = np.asarray(rel_bias[0], np.float32)
    k = np.arange(128)[:, None, None]
    d = np.arange(5)[None, :, None]
    q = np.arange(128)[None, None, :]
    idx = np.clip(128 * d + q - k, -128, 128) + 128
    biasT = np.ascontiguousarray(tab[:, idx].transpose(1, 0, 2, 3))
    mask5 = np.ones((128, 5, 128), np.float32)
    mask5[64:, 0, :64] = 0.0
    mask5[:64, 4, 64:] = 0.0
    qs = np.arange(NS)[None, None, :]
    blk = np.arange(5)[None, :, None]
    kk = np.arange(128)[:, None, None]
    rel = np.where(blk < 4, 512 + qs - (128 * blk + kk), qs - kk)
    idxs = np.clip(rel, -128, 128) + 128
    biasS = np.ascontiguousarray(tab[:, idxs].transpose(1, 0, 2, 3))
    corr = np.ones((128, 4, 16), np.float32)
    for g, w in enumerate(POOLW):
        t = np.arange(16)
        corr[:, g, :] = w / np.minimum(t + 1, w)
    return biasT, mask5, biasS, corr


def _relayout_w(w):
    k, n = w.shape
    return np.ascontiguousarray(w.reshape(8, 128, n).transpose(1, 0, 2))


def kernel(x_prompt, x_sample, cache_pool, cache_k, cache_v, state_C, state_n, state_m,
           norm_pre, norm_post, w_in_even, w_pool_mix, pool_scale, rel_bias, w_out_even,
           w_in_odd, b_gate_odd, mlstm_norm, w_out_odd, _stage=2):
    f = lambda a: np.ascontiguousarray(np.asarray(a, np.float32))
    if ("nc", _stage) not in _CACHE:
        _CACHE[("nc", _stage)] = build_nc(_stage)
    nc = _CACHE[("nc", _stage)]
    biasT, mask5, biasS, corr = _host_consts(f(rel_bias))
    tri = np.triu(np.ones((128, 128), np.float32))
    sel = np.zeros((4, 4, 128), np.float32)
    for hh in range(4):
        sel[hh, hh, :] = 1.0
    shared = {
        "gpre": np.ascontiguousarray(f(norm_pre).reshape(2, 8, 128).transpose(2, 0, 1)),
        "gpost": np.ascontiguousarray(np.broadcast_to(f(norm_post)[None], (128, 2, D))),
        "w0in": _relayout_w(f(w_in_even)[0]), "w0out": _relayout_w(f(w_out_even)[0]),
        "wmix": np.ascontiguousarray(f(w_pool_mix)[0].transpose(1, 0, 2)),
        "pscale": np.ascontiguousarray(f(pool_scale)[0].reshape(4, 128).T),
        "biasT": biasT, "mask5": mask5, "biasS": biasS, "corr": corr,
        "ident": np.eye(128, dtype=np.float32),
        "w1in": _relayout_w(f(w_in_odd)[0]), "w1out": _relayout_w(f(w_out_odd)[0]),
        "bgate": np.ascontiguousarray(np.broadcast_to(f(b_gate_odd)[0][None, :], (128, 8))),
        "gml": np.ascontiguousarray(np.broadcast_to(f(mlstm_norm)[0][None], (128, D))),
        "tri": tri, "sel": sel,
    }
    in_maps = []
    for c in range(8):
        m = dict(shared)
        m.update({
            "xp": f(x_prompt[c]), "xs": f(x_sample[c]),
            "cpool": f(cache_pool[0, c]), "ck": f(cache_k[0, c]).reshape(512, 512), "cv": f(cache_v[0, c]).reshape(512, 512),
            "sC": f(state_C[0, c]), "sn": f(state_n[0, c]), "sm": f(state_m[0, c]).reshape(4, 1),
        })
        in_maps.append(m)
    res = run_bass_kernel_spmd(nc, in_maps, core_ids=list(range(8)))
    R = res.results

    def gather(name, shape):
        return np.stack([np.asarray(r[name], np.float32).reshape(shape) for r in R], 0)
    y_p = gather("yp", (T, D)); y_s = gather("ys", (NS, D))
    pool_p = gather("pool_p", (15, 512))[None]
    k_p = gather("k_p", (512, 8, 64))[None]; v_p = gather("v_p", (512, 8, 64))[None]
    C_p = gather("C_p", (4, 256, 256))[None]; n_p = gather("n_p", (4, 256))[None]; m_p = gather("m_p", (4,))[None]
    pool_s = gather("pool_s", (15, 512))[None]
    k_s = gather("k_s", (NS, 8, 64))[None]; v_s = gather("v_s", (NS, 8, 64))[None]
    C_s = gather("C_s", (4, 256, 256))[None]; n_s = gather("n_s", (4, 256))[None]; m_s = gather("m_s", (4,))[None]
    return (y_p, y_s, pool_p, k_p, v_p, C_p, n_p, m_p, pool_s, k_s, v_s, C_s, n_s, m_s)
```

```python
import numpy as np
from contextlib import ExitStack
import concourse.bass as bass
import concourse.mybir as mybir
from concourse.bass_utils import run_bass_kernel_spmd

F32 = mybir.dt.float32
BF16 = mybir.dt.bfloat16
ALU = mybir.AluOpType
AF = mybir.ActivationFunctionType

D = 1024
T = 4096
NS = 32
NMT = 8
POOLW = (2, 4, 8, 16)
EPS = 1e-6


class Buf:
    __slots__ = ("name", "lw", "readers", "dsem", "excl")

    def __init__(self, name="", excl=False):
        self.name = name
        self.excl = excl
        self.lw = None
        self.readers = {}
        self.dsem = None


class Sched:
    ENG = ("pe", "act", "dve", "pool", "sp")

    def __init__(self, nc, stack):
        self.nc = nc
        self.stack = stack
        self.prog = {e: [] for e in self.ENG}
        self.sems = {}
        self.issued = {}
        self.isdma = {}
        self.seen = {e: {} for e in self.ENG}
        for e in ("pe", "act", "dve", "pool"):
            self._mksem(e, False)
        self.ndma = 0

    def _mksem(self, key, isdma):
        self.sems[key] = self.stack.enter_context(self.nc.semaphore("s_" + key))
        self.issued[key] = 0
        self.isdma[key] = isdma

    def _waits(self, eng, reads, writes):
        need = {}

        def add(k, v):
            if self.isdma[k]:
                v = self.issued[k]
            if v > need.get(k, 0):
                need[k] = v
        for b in reads:
            if b.lw is not None:
                add(*b.lw)
            if b.excl:
                for k, v in b.readers.items():
                    if k != eng:
                        add(k, v)
        for b in writes:
            if b.lw is not None:
                add(*b.lw)
            for k, v in b.readers.items():
                add(k, v)
        out = []
        for k, v in need.items():
            if k == "pe" and eng == "pe":
                continue
            if self.seen[eng].get(k, 0) >= v:
                continue
            self.seen[eng][k] = v
            out.append((k, v))
        return out

    def _record(self, ev, reads, writes):
        k, v = ev
        for b in reads:
            if b.readers.get(k, 0) < v:
                b.readers[k] = v
        for b in writes:
            b.lw = ev
            b.readers = {}

    def op(self, eng, fn, reads=(), writes=(), inc=True):
        waits = self._waits(eng, reads, writes)
        if inc:
            self.issued[eng] += 1
            ev = (eng, self.issued[eng])
        else:
            ev = (eng, self.issued[eng] + 1)
        self.prog[eng].append((waits, fn, eng if inc else None))
        self._record(ev, reads, writes)
        return ev

    def dma(self, q, out_ap, in_ap, reads=(), writes=(), owner=None, **kw):
        waits = self._waits(q, reads, writes)
        if owner is None:
            owner = (list(writes) + list(reads))[0]
        if owner.dsem is None:
            self.ndma += 1
            owner.dsem = "d%d" % self.ndma
            self._mksem(owner.dsem, True)
        semkey = owner.dsem
        self.issued[semkey] += 16
        ev = (semkey, self.issued[semkey])

        def fn(e, out_ap=out_ap, in_ap=in_ap, kw=kw):
            return e.dma_start(out=out_ap, in_=in_ap, **kw)
        self.prog[q].append((waits, fn, semkey))
        self._record(ev, reads, writes)
        return ev

    def barrier(self):
        for e in self.ENG:
            waits = []
            for k, v in self.issued.items():
                if v > self.seen[e].get(k, 0):
                    self.seen[e][k] = v
                    waits.append((k, v))
            self.prog[e].append((waits, None, None))

    def emit(self):
        nc = self.nc
        prog = self.prog
        self.prog = {e: [] for e in self.ENG}
        with nc.Block() as block:
            def run(e, items):
                for waits, fn, inck in items:
                    for k, v in waits:
                        e.wait_ge(self.sems[k], v)
                    if fn is None:
                        continue
                    ins = fn(e)
                    if inck is not None:
                        ins.then_inc(self.sems[inck], 16 if self.isdma[inck] else 1)

            @block.tensor
            def _(e):
                run(e, prog["pe"])

            @block.scalar
            def _(e):
                run(e, prog["act"])

            @block.vector
            def _(e):
                run(e, prog["dve"])

            @block.gpsimd
            def _(e):
                run(e, prog["pool"])

            @block.sync
            def _(e):
                run(e, prog["sp"])


def build_nc(stage=2):
    nc = bass.Bass("TRN2", target_bir_lowering=False)

    def din(name, shape):
        return nc.dram_tensor(name, shape, F32, kind="ExternalInput").ap()

    def dout(name, shape):
        return nc.dram_tensor(name, shape, F32, kind="ExternalOutput").ap()
    xp = din("xp", [T, D]); xs = din("xs", [NS, D])
    cpool = din("cpool", [15, 512]); ck = din("ck", [512, 512]); cv = din("cv", [512, 512])
    sC = din("sC", [4, 256, 256]); sn = din("sn", [4, 256]); sm = din("sm", [4, 1])
    gpre = din("gpre", [128, 2, 8]); gpost = din("gpost", [128, 2, D])
    w0in = din("w0in", [128, 8, 3072]); w0out = din("w0out", [128, 8, D])
    wmix = din("wmix", [128, 4, 128]); pscale = din("pscale", [128, 4])
    biasT = din("biasT", [128, 8, 5, 128]); mask5 = din("mask5", [128, 5, 128]); biasS = din("biasS", [128, 8, 5, NS])
    corr = din("corr", [128, 4, 16]); ident = din("ident", [128, 128])
    w1in = din("w1in", [128, 8, 5128]); w1out = din("w1out", [128, 8, D])
    bgate = din("bgate", [128, 8]); gml = din("gml", [128, D])
    tri = din("tri", [128, 128]); sel = din("sel", [4, 4, 128])
    yp = dout("yp", [T, D]); ys = dout("ys", [NS, D])
    o_pool_p = dout("pool_p", [15, 512]); o_k_p = dout("k_p", [512, 512]); o_v_p = dout("v_p", [512, 512])
    o_C_p = dout("C_p", [4, 256, 256]); o_n_p = dout("n_p", [4, 256]); o_m_p = dout("m_p", [4, 1])
    o_pool_s = dout("pool_s", [15, 512]); o_k_s = dout("k_s", [NS, 512]); o_v_s = dout("v_s", [NS, 512])
    o_C_s = dout("C_s", [4, 256, 256]); o_n_s = dout("n_s", [4, 256]); o_m_s = dout("m_s", [4, 1])
    x1d = nc.dram_tensor("x1d", [T, D], F32, kind="Internal").ap()
    bx1d = [Buf("x1d%d" % i) for i in range(4 * NMT)]

    with ExitStack() as st:
        S = Sched(nc, st)

        def mk(stack):
            def sb(name, shape, dt=F32):
                return stack.enter_context(nc.sbuf_tensor(name, shape, dt))

            def ps(name, shape, dt=F32):
                return stack.enter_context(nc.psum_tensor(name, shape, dt))
            return sb, ps
        sb, ps = mk(st)

        bWs = [Buf("W%d" % i) for i in range(13)]

        def wb(c0, c1):
            return bWs[c0 // 512:(c1 - 1) // 512 + 1]
        stg = sb("stg", [128, 4, 1536]); bstg = [Buf("stg%d" % i) for i in range(4)]
        identf = sb("identf", [128, 128]); bidf = Buf("idf")
        identb = sb("identb", [128, 128], BF16); bidb = Buf("idb")
        gpre_t = sb("gpre_t", [128, 2, 8]); bgpre = Buf("gpre")
        gpost_t = sb("gpost_t", [128, 1, D]); bgpost = Buf("gpost")
        x1s_t = sb("x1s_t", [NS, D]); bx1s = Buf("x1s")
        junk = sb("junk", [128, D], BF16); bjunk = Buf("junk")
        ysb = sb("ysb", [128, 2, D]); bysb = [Buf("ysb%d" % i) for i in range(2)]
        h = sb("h", [128, 2, D], BF16); bh = [Buf("h%d" % i) for i in range(2)]
        ms = sb("ms", [128, 40]); rs = sb("rs", [128, 40])
        bms = [Buf("ms%d" % i) for i in range(10)]; brs = [Buf("rs%d" % i) for i in range(10)]
        pT = ps("pT", [128, 8, 128], BF16); bpT = Buf("pT", True)
        pbb = ps("pbb", [128, 2, 512]); pb = [pbb[:, 0, :], pbb[:, 1, :]]; bpb = [Buf("pb%d" % i, True) for i in range(2)]
        pST = [ps("pST%d" % i, [128, 2, 512]) for i in range(2)]; bpST = [Buf("pST%d" % i, True) for i in range(2)]
        pPV = ps("pPV", [128, 512]); bpPV = Buf("pPV", True)
        pbi = [0]

        pbl = [(pb[0], bpb[0]), (pb[1], bpb[1])]

        def nextpb():
            i = pbi[0] % len(pbl)
            pbi[0] += 1
            return pbl[i]

        def mm(out, lhsT, rhs, start, stop, reads, writes, inc):
            S.op("pe", lambda e: e.matmul(out=out, lhsT=lhsT, rhs=rhs, start=start, stop=stop), reads, writes, inc)

        def tp(out, in_, idn, reads, writes, inc):
            S.op("pe", lambda e: e.transpose(out=out, in_=in_, identity=idn), reads, writes, inc)

        def acopy(out, in_, reads, writes):
            S.op("act", lambda e: e.copy(out=out, in_=in_), reads, writes)

        def aact(out, in_, func, reads, writes, scale=1.0, bias=None, accum=None):
            kw = {}
            if bias is not None:
                kw["bias"] = bias
            if accum is not None:
                kw["accum_out"] = accum
            S.op("act", lambda e: e.activation(out=out, in_=in_, func=func, scale=scale, **kw), reads, writes)

        def vcopy(out, in_, reads, writes, eng="dve"):
            S.op(eng, lambda e: e.tensor_copy(out=out, in_=in_), reads, writes)

        def vtt(out, in0, in1, op, reads, writes, eng="dve"):
            S.op(eng, lambda e: e.tensor_tensor(out=out, in0=in0, in1=in1, op=op), reads, writes)

        def vts(out, in0, s1, s2, op0, op1, reads, writes):
            if s2 is None:
                S.op("dve", lambda e: e.tensor_scalar(out=out, in0=in0, scalar1=s1, scalar2=None, op0=op0), reads, writes)
            else:
                S.op("dve", lambda e: e.tensor_scalar(out=out, in0=in0, scalar1=s1, scalar2=s2, op0=op0, op1=op1), reads, writes)

        def vstt(out, in0, scalar, in1, op0, op1, reads, writes):
            S.op("dve", lambda e: e.scalar_tensor_tensor(out=out, in0=in0, scalar=scalar, in1=in1, op0=op0, op1=op1), reads, writes)

        def vrecip(out, in_, reads, writes):
            S.op("dve", lambda e: e.reciprocal(out=out, in_=in_), reads, writes)

        def memset(eng, ap, val, writes):
            S.op(eng, lambda e: e.memset(ap, val), (), writes)

        S.dma("sp", identf[:], ident, writes=[bidf])
        S.dma("sp", gpre_t[:], gpre, writes=[bgpre])
        S.dma("sp", gpost_t[:, 0, :], gpost[:, 0, :], writes=[bgpost])
        vcopy(identb[:], identf[:], [bidf], [bidb])

        def load_weights(warena, src, ncols, dst_col0, layer, kscale_cols=None, chunks=None, on_pool=True):
            if chunks is None:
                chunks = [(c0, min(1024, ncols - c0)) for c0 in range(0, ncols, 1024)]
            for c0, cw in chunks:
                wr = wb(dst_col0 + c0, dst_col0 + c0 + cw)
                extra = None
                if kscale_cols is not None and kscale_cols[0] <= c0 < kscale_cols[1]:
                    extra = kscale_cols[2]
                for kc in range(8):
                    s = load_weights.si % 4
                    load_weights.si += 1
                    S.dma("sp", stg[:, s, 0:cw], src[:, kc, c0:c0 + cw], writes=[bstg[s]])
                    o = warena[:, kc, dst_col0 + c0:dst_col0 + c0 + cw]
                    i = stg[:, s, 0:cw]
                    gsrc = None if layer is None else (gpre16 if extra is not None else gpre_t[:, layer, :])
                    if on_pool:
                        if layer is None:
                            vcopy(o, i, [bstg[s]], wr, eng="pool")
                        else:
                            vtt(o, i, gsrc[:, kc:kc + 1].to_broadcast([128, cw]), ALU.mult, [bstg[s], bgpre], wr, eng="pool")
                    elif layer is None:
                        if load_weights.si % 2:
                            vcopy(o, i, [bstg[s]], wr)
                        else:
                            acopy(o, i, [bstg[s]], wr)
                    elif load_weights.si % 2:
                        vts(o, i, gsrc[:, kc:kc + 1], None, ALU.mult, None, [bstg[s], bgpre], wr)
                    else:
                        aact(o, i, AF.Copy, [bstg[s], bgpre], wr, scale=gsrc[:, kc:kc + 1])
        load_weights.si = 0
        gpre16 = sb("gpre16", [128, 8])
        vts(gpre16[:], gpre_t[:, 1, :], 1.0 / 16.0, None, ALU.mult, None, [bgpre], [bgpre])

        with ExitStack() as st0:
            sb0, _ = mk(st0)
            W0 = sb0("W0", [128, 8, 4096], BF16)
            xr = stg[:, :, 0:D]; bxr = bstg
            xq = sb0("xq", [128, 2, D]); bxq = [Buf("xq%d" % i) for i in range(2)]
            hT = sb0("hT", [128, 8, 512], BF16); bhT = Buf("hT")
            uT = sb0("uT", [128, 4, 528]); buT = Buf("uT")
            qT = sb0("qT", [128, 4, 512], BF16); bqT = Buf("qT")
            kT = sb0("kT", [128, 4, 1024], BF16); bkT = [Buf("kT%d" % i) for i in range(2)]
            gpT = sb0("gpT", [128, 4, 512], BF16); bgpT = Buf("gpT")
            Vr = sb0("Vr", [128, 8, 8, 65], BF16); bV = [Buf("V%d" % i) for i in range(8)]
            gatt = sb0("gatt", [128, 4, 512], BF16); bgatt = [Buf("gatt%d" % i) for i in range(4)]
            pooledT = sb0("pooledT", [128, 4, 512], BF16); bpooled = Buf("pooled")
            tA = sb0("tA", [128, 2, 528]); btA = [Buf("tA0"), Buf("tA1")]
            ostg = tA[:, :, 0:512]; bostg = btA
            tB = sb0("tB", [128, 2, 528]); btB = [Buf("tB0"), Buf("tB1")]
            mixedT = sb0("mixedT", [128, 8, 512], BF16); bmixed = Buf("mixedT")
            E5 = sb0("E5", [128, 8, 5, 128], BF16); bE5 = Buf("E5")
            expo2 = sb0("expo", [128, 2, 640], BF16); bexpo = [Buf("expo0"), Buf("expo1"), Buf("expo2")]
            PT2 = sb0("PT", [128, 2, 640], BF16); bPT = [Buf("PT0"), Buf("PT1"), Buf("PT2")]
            expoS = [expo2[:, 0, :], expo2[:, 1, :], stg[:, 0, 1024:1536].bitcast(BF16)[:, 0:640]]
            PTS = [PT2[:, 0, :], PT2[:, 1, :], stg[:, 1, 1024:1536].bitcast(BF16)[:, 0:640]]
            ST3 = [pST[0], pST[1], pbb]
            bST3 = [[bpST[0]], [bpST[1]], bpb]
            rc = sb0("rc", [128, 2, 4]); brc = [Buf("rc0"), Buf("rc1")]
            tmpn = sb0("tmpn", [128, 2, 256]); btmpn = [Buf("tn0"), Buf("tn1")]
            attg = sb0("attg", [128, 512], BF16); battg = Buf("attg")
            attgs = [attg, stg[:, 2, 1024:1536].bitcast(BF16)[:, 0:512]]; battgs = [battg, Buf("attg2")]
            wmixb = sb0("wmixb", [128, 4, 128], BF16); bwmix = Buf("wmix")
            pscale_t = sb0("pscale_t", [128, 4]); bpscale = Buf("pscale")
            corr_t = sb0("corr_t", [128, 4, 16]); bcorr = Buf("corr")
            hTs = sb0("hTs", [128, 8, NS], BF16); bhTs = Buf("hTs")
            uTs = sb0("uTs", [128, 4, 48]); buTs = Buf("uTs")
            qTs = sb0("qTs", [128, 4, NS], BF16); bqTs = Buf("qTs")
            kTs = sb0("kTs", [128, 4, NS], BF16); bkTs = Buf("kTs")
            cstage = stg[:, 0:2, :].rearrange("p a b -> p (a b)")[:, 0:2048].rearrange("p (b f) -> p b f", b=4)

            wmixf = tB[:, 0, 0:512].rearrange("p (g d) -> p g d", g=4)
            S.dma("sp", wmixf, wmix, writes=[btB[0]])
            vcopy(wmixb[:], wmixf, [btB[0]], [bwmix])
            mask_f = tA[:].rearrange("p a b -> p (a b)")[:, 0:640]
            bmask = btA
            S.dma("sp", pscale_t[:], pscale, writes=[bpscale])
            S.dma("sp", corr_t[:], corr, writes=[bcorr])
            S.dma("sp", mask_f, mask5.rearrange("p d q -> p (d q)"), writes=bmask)
            for hh in range(8):
                s = hh % 4
                S.dma("sp", stg[:, s, 0:640], biasT[:, hh, :, :].rearrange("p d q -> p (d q)"), writes=[bstg[s]])
                aact(stg[:, s, 0:640], stg[:, s, 0:640], AF.Exp, [bstg[s]], [bstg[s]])
                vtt(E5[:, hh, :, :].rearrange("p d q -> p (d q)"), stg[:, s, 0:640], mask_f, ALU.mult,
                    [bstg[s]] + bmask, [bE5])
            memset("pool", Vr[:, :, :, 64:65], 1.0, bV)
            memset("pool", uT[:, :, 0:16], 0.0, [buT])

            def norm_stats(x_ap, nt, col, bx, bm_):
                aact(junk[0:nt, :], x_ap, AF.Square, [bx], [bjunk, bm_], scale=1.0 / 32.0, accum=ms[0:nt, col:col + 1])

            def norm_rstd(nt, c0, c1, bm_, br_):
                aact(rs[0:nt, c0:c1], ms[0:nt, c0:c1], AF.Ln, [bm_], [br_], bias=EPS)
                aact(rs[0:nt, c0:c1], rs[0:nt, c0:c1], AF.Exp, [br_], [br_], scale=-0.5)

            def make_hT(x_ap, nt, col, bx, br_, hslot, hT_out, bhT_out):
                hb = h[0:nt, hslot, :]
                vts(hb, x_ap, rs[0:nt, col:col + 1], None, ALU.mult, None, [bx, br_], [bh[hslot]])
                for kc in range(8):
                    tp(pT[:, kc, 0:nt], hb[:, kc * 128:(kc + 1) * 128], identb[0:nt, 0:nt], [bh[hslot], bidb], [bpT], kc == 7)
                acopy(hT_out, pT[:, :, 0:nt], [bpT], [bhT_out])

            def pool_branch(uTt, buT_, L, first, pooled_ap, bpooled_, gp_ap, bgp_, mixed_out, bmixed_):
                pool_sums(uTt, buT_, L, first, pooled_ap, bpooled_)
                pool_mix(L, pooled_ap, bpooled_, gp_ap, bgp_, mixed_out, bmixed_)

            def pool_sums(uTt, buT_, L, first, pooled_ap, bpooled_):
                for g, w in enumerate(POOLW):
                    tbufs = (tA, btA) if g < 2 else (tB, btB)
                    eng = "dve" if g < 2 else "pool"
                    cur = uTt[:, g, :]
                    curb = buT_
                    base = 16 - (w - 1)
                    lo_t = -(w - 1)
                    k = 1
                    step = 0
                    while k < w:
                        new_lo = lo_t + k
                        n = L - new_lo
                        c_hi = base + k
                        dst = tbufs[0][:, step % 2, 0:n]
                        dstb = tbufs[1][step % 2]
                        vtt(dst, cur[:, c_hi:c_hi + n], cur[:, c_hi - k:c_hi - k + n], ALU.add, [curb], [dstb], eng=eng)
                        cur = tbufs[0][:, step % 2, :]
                        curb = dstb
                        base = 0
                        lo_t = new_lo
                        k *= 2
                        step += 1
                    s_ap = cur[:, 0:L]
                    if first:
                        vtt(cur[:, 0:16], cur[:, 0:16], corr_t[:, g, :], ALU.mult, [curb, bcorr], [curb])
                    vstt(pooled_ap[:, g, 0:L], s_ap, 1.0 / w, uTt[:, g, 16:16 + L], ALU.mult, ALU.subtract, [curb, buT_], [bpooled_])

            def pool_mix(L, pooled_ap, bpooled_, gp_ap, bgp_, mixed_out, bmixed_):
                for g in range(4):
                    bank, bb = nextpb()
                    mm(bank[:, 0:L], wmixb[:, g, :], pooled_ap[:, g, 0:L], True, True, [bwmix, bpooled_], [bb], True)
                    vstt(mixed_out[:, g, 0:L], bank[:, 0:L], pscale_t[:, g:g + 1], gp_ap[:, g, 0:L], ALU.mult, ALU.mult,
                         [bb, bpscale, bgp_], [bmixed_])

            def post_norm_resid(nt, halves, bbs, col, x_ap, bx, layer, yslot, dst_dram, dst_sb=None, bdst=None, bdram=None):
                ya = ysb[0:nt, yslot, :]
                by = bysb[yslot]
                for hf in range(2):
                    aact(junk[0:nt, 0:512], halves[hf], AF.Square, [bbs[hf]], [bjunk, bms[8]], scale=1.0 / 32.0, accum=ms[0:nt, col + hf:col + hf + 1])
                    vcopy(ya[:, hf * 512:(hf + 1) * 512], halves[hf], [bbs[hf]], [by])
                vtt(ms[0:nt, col:col + 1], ms[0:nt, col:col + 1], ms[0:nt, col + 1:col + 2], ALU.add, [bms[8]], [bms[8]])
                aact(rs[0:nt, col:col + 1], ms[0:nt, col:col + 1], AF.Ln, [bms[8]], [brs[8]], bias=EPS)
                aact(rs[0:nt, col:col + 1], rs[0:nt, col:col + 1], AF.Exp, [brs[8]], [brs[8]], scale=-0.5)
                vstt(ya, ya, rs[0:nt, col:col + 1], gpost_t[0:nt, 0, :], ALU.mult, ALU.mult, [by, brs[8], bgpost], [by])
                if dst_sb is None:
                    vtt(ya, ya, x_ap, ALU.add, [by, bx], [by], eng="pool")
                    S.dma("sp", dst_dram, ya, reads=[by], writes=([bdram] if bdram is not None else []), owner=by)
                else:
                    vtt(dst_sb, ya, x_ap, ALU.add, [by, bx], [bdst], eng="pool")

            def pre_load(g):
                S.dma("sp", xr[:, g % 4, :], xp[g * 128:(g + 1) * 128, :], writes=[bxr[g % 4]])

            def pre_stats(m):
                for j in range(4):
                    g = 4 * m + j
                    norm_stats(xr[:, g % 4, :], 128, g, bxr[g % 4], bms[m])
                norm_rstd(128, 4 * m, 4 * m + 4, bms[m], brs[m])

            def pre_T(g):
                m, j = divmod(g, 4)
                make_hT(xr[:, g % 4, :], 128, g, bxr[g % 4], brs[m], g % 2, hT[:, :, j * 128:(j + 1) * 128], bhT)

            def pre_scale(g):
                m, j = divmod(g, 4)
                vts(h[:, g % 2, :], xr[:, g % 4, :], rs[:, g:g + 1], None, ALU.mult, None, [bxr[g % 4], brs[m]], [bh[g % 2]])

            def pre_tp(g):
                m, j = divmod(g, 4)
                hb = h[:, g % 2, :]
                for kc in range(8):
                    tp(pT[:, kc, :], hb[:, kc * 128:(kc + 1) * 128], identb[:, :], [bh[g % 2], bidb], [bpT], kc == 7)
                acopy(hT[:, :, j * 128:(j + 1) * 128], pT[:, :, :], [bpT], [bhT])

            def proj_feat(co, N, rhsT, brhs):
                bank, bb = nextpb()
                for kc in range(8):
                    mm(bank[:, 0:N], W0[:, kc, co:co + 128], rhsT[:, kc, 0:N], kc == 0, kc == 7, wb(co, co + 128) + [brhs], [bb], kc == 7)
                return bank, bb

            def proj_tok(co, nt, lhs, blhs):
                bank, bb = nextpb()
                for kc in range(8):
                    mm(bank[0:nt, :], lhs[:, kc, :], W0[:, kc, co:co + 512], kc == 0, kc == 7, wb(co, co + 512) + [blhs], [bb], kc == 7)
                return bank, bb

            def inf_stage(m):
                slot = m % 2
                for c in range(4):
                    bank, bb = proj_feat(c * 128, 512, hT, bhT)
                    acopy(uT[:, c, 16:528], bank[:, :], [bb], [buT])
                for c in range(4):
                    bank, bb = proj_feat(512 + c * 128, 512, hT, bhT)
                    vcopy(qT[:, c, :], bank[:, :], [bb], [bqT])
                for c in range(4):
                    bank, bb = proj_feat(1024 + c * 128, 512, hT, bhT)
                    if c % 2:
                        vcopy(kT[:, c, slot * 512:(slot + 1) * 512], bank[:, :], [bb], [bkT[slot]])
                    else:
                        acopy(kT[:, c, slot * 512:(slot + 1) * 512], bank[:, :], [bb], [bkT[slot]])
                for c in range(4):
                    bank, bb = proj_feat(2048 + c * 128, 512, hT, bhT)
                    aact(gpT[:, c, :], bank[:, :], AF.Silu, [bb], [bgpT])
                pool_sums(uT, buT, 512, m == 0, pooledT, bpooled)
                pool_hist(m)

            def int_stage(m):
                for j in range(4):
                    g = 4 * m + j
                    lhs = hT[:, :, j * 128:(j + 1) * 128]
                    bank, bb = proj_tok(1536, 128, lhs, bhT)
                    chk(130)
                    vcopy(Vr[:, g % 8, :, 0:64], bank[:, :].rearrange("p (h d) -> p h d", h=8), [bb], [bV[g % 8]])
                    chk(131)
                    if m == NMT - 1:
                        acopy(ostg[:, 0, :], bank[:, :], [bb], [bostg[0]])
                        chk(1315)
                        S.dma("sp", o_v_p[j * 128:(j + 1) * 128, :], ostg[:, 0, :], reads=[bostg[0]])
                    chk(132)
                    bank, bb = proj_tok(2560, 128, lhs, bhT)
                    aact(gatt[:, j, :], bank[:, :], AF.Silu, [bb], [bgatt[j]])
                    chk(133)
                    if m == NMT - 1:
                        bank, bb = proj_tok(1024, 128, lhs, bhT)
                        acopy(ostg[:, 1, :], bank[:, :], [bb], [bostg[1]])
                        S.dma("sp", o_k_p[j * 128:(j + 1) * 128, :], ostg[:, 1, :], reads=[bostg[1]])

            def pool_stage(m):
                pool_mix(512, pooledT, bpooled, gpT, bgpT, mixedT, bmixed)

            def pool_hist(m):
                if m == NMT - 1:
                    bank, bb = nextpb()
                    for g in range(4):
                        tp(bank[0:15, g * 128:(g + 1) * 128], uT[:, g, 513:528], identf[:, :], [buT, bidf], [bb], g == 3)
                    acopy(ostg[0:15, 0, :], bank[0:15, :], [bb], [bostg[0]])
                    S.dma("sp", o_pool_p, ostg[0:15, 0, :], reads=[bostg[0]])
                else:
                    S.op("pool", lambda e: e.tensor_copy(out=uT[:, :, 1:16], in_=uT[:, :, 513:528]), [buT], [buT])

            def att_qk(m, qb, hh, buf):
                gq = 4 * m + qb
                nkb = min(gq, 4) + 1
                r0 = (hh % 2) * 64
                c = hh // 2
                for d in range(nkb):
                    gk = gq - d
                    slot = (gk // 4) % 2
                    off = slot * 512 + (gk % 4) * 128
                    mm(ST3[buf][:, d // 4, (d % 4) * 128:(d % 4) * 128 + 128], kT[r0:r0 + 64, c, off:off + 128], qT[r0:r0 + 64, c, qb * 128:(qb + 1) * 128],
                       True, True, [bkT[slot], bqT], bST3[buf], d == nkb - 1)
                n = nkb * 128
                src = ST3[buf][:].rearrange("p a b -> p (a b)")[:, 0:n]
                aact(expoS[buf][:, 0:n], src, AF.Exp, bST3[buf], [bexpo[buf]], scale=0.125)
                vtt(PTS[buf][:, 0:n], expoS[buf][:, 0:n], E5[:, hh, 0:nkb, :].rearrange("p d q -> p (d q)"), ALU.mult, [bexpo[buf], bE5], [bPT[buf]])

            def att_pv(m, qb, hh, buf):
                gq = 4 * m + qb
                nkb = min(gq, 4) + 1
                hq = hh % 4
                for d in range(nkb):
                    gk = gq - d
                    mm(pPV[:, hq * 65:hq * 65 + 65], PTS[buf][:, d * 128:(d + 1) * 128], Vr[:, gk % 8, hh, :], d == 0, d == nkb - 1,
                       [bPT[buf], bV[gk % 8]], [bpPV], d == nkb - 1)

            def att_norm(qb, hg, gatt_ap, bg_, nt=128, asl=0):
                pv = pPV[0:nt, 0:260].rearrange("p (h d) -> p h d", h=4)
                r = hg % 2
                vrecip(rc[0:nt, r, :], pv[:, :, 64], [bpPV], [brc[r]])
                tn = tmpn[0:nt, r, :].rearrange("p (h d) -> p h d", h=4)
                vtt(tn, pv[:, :, 0:64], rc[0:nt, r, :].unsqueeze(2).to_broadcast([nt, 4, 64]), ALU.mult, [bpPV, brc[r]], [btmpn[r]])
                vtt(attgs[asl][0:nt, hg * 256:(hg + 1) * 256], tmpn[0:nt, r, :], gatt_ap[:, hg * 256:(hg + 1) * 256], ALU.mult, [btmpn[r], bg_], [battgs[asl]])

            def att_stage(m, between=None):
                def flush(qb):
                    a = attgs[qb % 2]
                    for c in range(4):
                        tp(pT[:, c, :], a[:, c * 128:(c + 1) * 128], identb[:, :], [battgs[qb % 2], bidb], [bpT], c == 3)
                    acopy(mixedT[:, 4:8, qb * 128:(qb + 1) * 128], pT[:, 0:4, :], [bpT], [bmixed])
                pending = None
                for qb in range(4):
                    if between is not None:
                        between(qb)
                    seq = list(range(8))
                    att_qk(m, qb, 0, 0)
                    att_qk(m, qb, 1, 1)
                    if pending is not None:
                        flush(pending)
                    for hh in seq:
                        if hh + 2 < 8:
                            att_qk(m, qb, hh + 2, (hh + 2) % 3)
                        att_pv(m, qb, hh, hh % 3)
                        if hh % 4 == 3:
                            att_norm(qb, hh // 4, gatt[:, qb, :], bgatt[qb], asl=qb % 2)
                    pending = qb
                return lambda: flush(pending)

            def out_stage(m, last_flush=None):
                for j in range(4):
                    g = 4 * m + j
                    if j == 3 and last_flush is not None:
                        last_flush()
                    S.dma("sp", xq[:, g % 2, :], xp[g * 128:(g + 1) * 128, :], writes=[bxq[g % 2]])
                    halves = []
                    bbs = []
                    for hf in range(2):
                        bank, bb = nextpb()
                        for kc in range(8):
                            mm(bank[:, :], mixedT[:, kc, j * 128:(j + 1) * 128], W0[:, kc, 3072 + hf * 512:3072 + (hf + 1) * 512], kc == 0, kc == 7,
                               [bmixed] + wb(3072 + hf * 512, 3584 + hf * 512), [bb], kc == 7)
                        halves.append(bank[:, :])
                        bbs.append(bb)
                    post_norm_resid(128, halves, bbs, 36, xq[:, g % 2, :], bxq[g % 2], 0, g % 2, x1d[g * 128:(g + 1) * 128, :], bdram=bx1d[g])

            def chk(n):
                if stage == n:
                    raise StopIteration
            def l0_all():
              chk(10)
              for g in range(4):
                pre_load(g)
              pre_stats(0)
              for g in range(4):
                pre_T(g)
              load_weights(W0, w0in, 3072, 0, 0)
              load_weights(W0, w0out, 1024, 3072, None)
              chk(11)
              for m in range(NMT):
                inf_stage(m)
                chk(12)
                int_stage(m)
                chk(13)
                if m + 1 < NMT:
                    for g in range(4 * (m + 1), 4 * (m + 2)):
                        pre_load(g)
                    pre_stats(m + 1)
                pool_stage(m)
                chk(14)
                if m + 1 < NMT:
                    def btw(qb, m=m):
                        g = 4 * (m + 1) + qb
                        pre_tp(g)
                        if qb < 3:
                            pre_scale(g + 1)
                    pre_scale(4 * (m + 1))
                    lf = att_stage(m, between=btw)
                else:
                    lf = att_stage(m)
                chk(15)
                out_stage(m, lf)
                chk(16)
              sample_l0()

            def sample_l0():
                bxs = bxr[3]
                xs_t = xr[0:NS, 3, :]
                S.dma("sp", xs_t, xs, writes=[bxs])
                norm_stats(xs_t, NS, 32, bxs, bms[9])
                norm_rstd(NS, 32, 33, bms[9], brs[9])
                make_hT(xs_t, NS, 32, bxs, brs[9], 0, hTs[:, :, :], bhTs)
                kTc = qT; bkTc = bqT
                Vc = Vr[:, 0:5, :, :]
                cstb = pooledT; bcstb = bpooled
                ES = E5[:].rearrange("p h d q -> p (h d q)")[:, 0:1280].rearrange("p (h d q) -> p h d q", h=8, d=5); bES = bE5
                S.dma("sp", stg[:, 2, 0:1280], biasS.rearrange("p h d q -> p (h d q)"), writes=[bstg[2], battgs[1]])
                aact(ES.rearrange("p h d q -> p (h d q)"), stg[:, 2, 0:1280], AF.Exp, [bstg[2]], [bES])
                S.dma("sp", cstage[:, :, :], ck.rearrange("(b p) f -> p b f", p=128), writes=[bstg[0], bstg[1], bexpo[2], bPT[2]])
                vcopy(cstb[:], cstage[:], [bstg[0], bstg[1]], [bcstb])
                for blk in range(4):
                    for c in range(4):
                        tp(pT[:, c, :], cstb[:, blk, c * 128:(c + 1) * 128], identb[:, :], [bcstb, bidb], [bpT], c == 3)
                    acopy(kTc[:, :, blk * 128:(blk + 1) * 128], pT[:, 0:4, :], [bpT], [bkTc])
                S.dma("sp", cstage[:, :, :], cv.rearrange("(b p) f -> p b f", p=128), writes=[bstg[0], bstg[1], bexpo[2], bPT[2]])
                vcopy(Vc[:, 0:4, :, 0:64], cstage[:].rearrange("p b (h d) -> p b h d", h=8), [bstg[0], bstg[1]], bV[0:5])
                S.dma("sp", ostg[0:15, 0, :], cpool, writes=[bostg[0]])
                bank, bb = nextpb()
                for g in range(4):
                    tp(bank[:, g * 16:g * 16 + 15], ostg[0:15, 0, g * 128:(g + 1) * 128], identf[0:15, 0:15], [bostg[0], bidf], [bb], g == 3)
                acopy(uTs[:, :, 1:16], bank[:, 0:64].rearrange("p (g t) -> p g t", g=4)[:, :, 0:15], [bb], [buTs])
                for c in range(4):
                    bank, bb = proj_feat(c * 128, NS, hTs, bhTs)
                    acopy(uTs[:, c, 16:48], bank[:, 0:NS], [bb], [buTs])
                for c in range(4):
                    bank, bb = proj_feat(512 + c * 128, NS, hTs, bhTs)
                    vcopy(qTs[:, c, :], bank[:, 0:NS], [bb], [bqTs])
                for c in range(4):
                    bank, bb = proj_feat(1024 + c * 128, NS, hTs, bhTs)
                    vcopy(kTs[:, c, :], bank[:, 0:NS], [bb], [bkTs])
                for c in range(4):
                    bank, bb = proj_feat(2048 + c * 128, NS, hTs, bhTs)
                    aact(gpT[:, c, 0:NS], bank[:, 0:NS], AF.Silu, [bb], [bgpT])
                bank, bb = proj_tok(0, NS, hTs, bhTs)
                acopy(ostg[0:NS, 0, :], bank[0:NS, :], [bb], [bostg[0]])
                S.dma("sp", o_pool_s, ostg[17:32, 0, :], reads=[bostg[0]])
                bank, bb = proj_tok(1024, NS, hTs, bhTs)
                acopy(ostg[0:NS, 1, :], bank[0:NS, :], [bb], [bostg[1]])
                S.dma("sp", o_k_s, ostg[0:NS, 1, :], reads=[bostg[1]])
                bank, bb = proj_tok(1536, NS, hTs, bhTs)
                acopy(ostg[0:NS, 0, :], bank[0:NS, :], [bb], [bostg[0]])
                S.dma("sp", o_v_s, ostg[0:NS, 0, :], reads=[bostg[0]])
                vcopy(Vc[0:NS, 4, :, 0:64], bank[0:NS, :].rearrange("p (h d) -> p h d", h=8), [bb], bV[0:5])
                bank, bb = proj_tok(2560, NS, hTs, bhTs)
                aact(gatt[0:NS, 0, :], bank[0:NS, :], AF.Silu, [bb], [bgatt[0]])
                pool_branch(uTs, buTs, NS, False, pooledT, bpooled, gpT, bgpT, mixedT, bmixed)
                for hh in range(8):
                    buf = hh % 2
                    r0 = (hh % 2) * 64
                    c = hh // 2
                    for blk in range(4):
                        mm(pST[buf][:, 0, blk * NS:(blk + 1) * NS], kTc[r0:r0 + 64, c, blk * 128:(blk + 1) * 128], qTs[r0:r0 + 64, c, :], True, True,
                           [bkTc, bqTs], [bpST[buf]], False)
                    mm(pST[buf][0:NS, 0, 4 * NS:5 * NS], kTs[r0:r0 + 64, c, :], qTs[r0:r0 + 64, c, :], True, True, [bkTs, bqTs], [bpST[buf]], True)
                    aact(expoS[buf][:, 0:4 * NS], pST[buf][:, 0, 0:4 * NS], AF.Exp, [bpST[buf]], [bexpo[buf]], scale=0.125)
                    aact(expoS[buf][0:NS, 4 * NS:5 * NS], pST[buf][0:NS, 0, 4 * NS:5 * NS], AF.Exp, [bpST[buf]], [bexpo[buf]], scale=0.125)
                    vtt(PTS[buf][:, 0:4 * NS], expoS[buf][:, 0:4 * NS], ES[:, hh, 0:4, :].rearrange("p d q -> p (d q)"), ALU.mult, [bexpo[buf], bES], [bPT[buf]])
                    vtt(PTS[buf][0:NS, 4 * NS:5 * NS], expoS[buf][0:NS, 4 * NS:5 * NS], ES[0:NS, hh, 4, :], ALU.mult, [bexpo[buf], bES], [bPT[buf]])
                    hq = hh % 4
                    for blk in range(4):
                        mm(pPV[0:NS, hq * 65:hq * 65 + 65], PTS[buf][:, blk * NS:(blk + 1) * NS], Vc[:, blk, hh, :], blk == 0, False, [bPT[buf]] + bV[0:5], [bpPV], False)
                    mm(pPV[0:NS, hq * 65:hq * 65 + 65], PTS[buf][0:NS, 4 * NS:5 * NS], Vc[0:NS, 4, hh, :], False, True, [bPT[buf]] + bV[0:5], [bpPV], True)
                    if hh % 4 == 3:
                        att_norm(0, hh // 4, gatt[0:NS, 0, :], bgatt[0], nt=NS)
                for c in range(4):
                    tp(pT[:, c, 0:NS], attg[0:NS, c * 128:(c + 1) * 128], identb[0:NS, 0:NS], [battg, bidb], [bpT], c == 3)
                acopy(mixedT[:, 4:8, 0:NS], pT[:, 0:4, 0:NS], [bpT], [bmixed])
                halves = []
                bbs = []
                for hf in range(2):
                    bank, bb = nextpb()
                    for kc in range(8):
                        mm(bank[0:NS, :], mixedT[:, kc, 0:NS], W0[:, kc, 3072 + hf * 512:3072 + (hf + 1) * 512], kc == 0, kc == 7, [bmixed] + wb(3072 + hf * 512, 3584 + hf * 512), [bb], kc == 7)
                    halves.append(bank[0:NS, :])
                    bbs.append(bb)
                post_norm_resid(NS, halves, bbs, 38, xs_t, bxs, 0, 0, None, dst_sb=x1s_t[:, :], bdst=bx1s)

            try:
                l0_all()
            except StopIteration:
                pass

            if stage == 1:
                for g in range(4 * NMT):
                    S.dma("sp", xq[:, g % 2, :], x1d[g * 128:(g + 1) * 128, :], reads=[bx1d[g]], writes=[bxq[g % 2]], owner=bxq[g % 2])
                    S.dma("sp", yp[g * 128:(g + 1) * 128, :], xq[:, g % 2, :], reads=[bxq[g % 2]])
                S.dma("sp", ys, x1s_t[:, :], reads=[bx1s])
            S.barrier()
            S.emit()


        if stage != 1:
          with ExitStack() as st1:
            sb1, _ = mk(st1)
            W1 = sb1("W1", [128, 8, 6152], BF16)
            S.dma("sp", gpost_t[:, 0, :], gpost[:, 1, :], writes=[bgpost])
            load_weights(W1, w1in, 5128, 0, 1, kscale_cols=(1024, 2048, 1.0 / 16.0),
                         chunks=[(5120, 8), (0, 1024), (1024, 1024), (2048, 1024), (3072, 1024), (4096, 1024)], on_pool=False)
            load_weights(W1, w1out, 1024, 5128, None, on_pool=False)
            xr = stg[:, :, 0:D]; bxr = bstg
            hT1a = sb1("hT1", [128, 2, 8, 128], BF16); bhT1a = [Buf("hT1_0"), Buf("hT1_1")]
            qT1 = sb1("qT1", [128, 3, 8, 128], BF16); bqT1 = [Buf("qT1_%d" % i) for i in range(3)]
            kT1 = sb1("kT1", [128, 2, 8, 128], BF16); bkT1 = [Buf("kT1a"), Buf("kT1b")]
            ktok = sb1("ktok", [128, 2, D], BF16); bktok = [Buf("ktoka"), Buf("ktokb")]
            vx = sb1("vx", [128, 3, 4, 257], BF16); bvx = [Buf("vx_%d" % i) for i in range(3)]
            wv = sb1("wv", [128, 4, 257], BF16); bwv = Buf("wv")
            St = stg[:, 3, 1024:1536].bitcast(BF16).rearrange("p (s h t) -> p s h t", s=2, h=4); bSt = [Buf("St0"), Buf("St1")]
            Cf = sb1("Cf", [128, 4, 2, 257]); bCfh = [Buf("Cf%d" % i) for i in range(4)]
            Cb = sb1("Cb", [128, 2, 4, 2, 257], BF16); bCb = [Buf("Cb0"), Buf("Cb1")]
            hc = sb1("hc", [128, D]); bhch = [Buf("hc%d" % i) for i in range(4)]
            junk2 = stg[:, 2, 1024:1280].bitcast(BF16).rearrange("p (a b) -> p a b", a=2); bjunk2 = [Buf("j2a"), Buf("j2b")]
            sg = sb1("sg", [128, 3, D], BF16); bsg = [Buf("sg_%d" % i) for i in range(3)]
            sz = sb1("sz", [128, D], BF16); bsz = Buf("sz")
            outm = stg[:, 1, 1024:1536].bitcast(BF16); boutm = Buf("outm")
            outT = stg[:, 0, 1024:1536].bitcast(BF16).rearrange("p (k t) -> p k t", k=8); boutT = Buf("outT")
            gml_t = sb1("gml_t", [128, D]); bgml = Buf("gml")
            tri_t = sb1("tri_t", [128, 128]); btri = Buf("tri")
            bg_t = sb1("bg_t", [128, 8]); bbg = Buf("bg")
            ones4 = sb1("ones4", [4, 128]); bones4 = Buf("ones4")
            gb = sb1("gb", [128, 2, 8]); bgb = [Buf("gba"), Buf("gbb")]
            g4 = sb1("g4", [128, 2, 16]); bg4 = [Buf("g4a"), Buf("g4b")]
            big4 = sb1("big4", [4, 256]); bbig4 = Buf("big4")
            cbt = sb1("cbt", [4, 2, 256]); bcbt = [Buf("cbta"), Buf("cbtb")]
            sm4 = sb1("sm4", [4, 16]); bsm4 = Buf("sm4")
            mprev = sb1("mprev", [4, 1]); bmprev = Buf("mprev")
            gsb = sb1("gsb", [128, 2, 12]); bgsb = [Buf("gsb0"), Buf("gsb1")]
            st8 = sb1("st8", [128, 4, 8]); bst8 = [Buf("st8_%d" % i) for i in range(4)]
            ctr = hc[:, 0:512].rearrange("p (a b) -> p a b", a=2); bctr = [bhch[0], bhch[1]]

            S.dma("sp", gml_t[:], gml, writes=[bgml])
            S.dma("sp", tri_t[:], tri, writes=[btri])
            S.dma("sp", bg_t[:], bgate, writes=[bbg])
            memset("pool", ones4[:], 1.0, [bones4])
            memset("pool", vx[:, :, :, 256:257], 1.0, bvx)

            bro = [Buf("ro0", True), Buf("ro1", True)]
            pbl.extend([(pST[0][:, 0, :], Buf("pq0", True)), (pST[0][:, 1, :], Buf("pq1", True))])

            def genA(nt, x_ap, bx, col, sl, s3):
                hT1 = hT1a[:, sl, :, :]
                bhT1 = bhT1a[sl]
                norm_stats(x_ap, nt, col, bx, bms[9])
                norm_rstd(nt, col, col + 1, bms[9], brs[9])
                hb = h[0:nt, sl, :]
                vts(hb, x_ap, rs[0:nt, col:col + 1], None, ALU.mult, None, [bx, brs[9]], [bh[sl]])
                yield
                for kc in range(8):
                    tp(pT[:, kc, 0:nt], hb[:, kc * 128:(kc + 1) * 128], identb[0:nt, 0:nt], [bh[sl], bidb], [bpT], kc == 7)
                acopy(hT1[:, :, 0:nt], pT[:, :, 0:nt], [bpT], [bhT1])
                yield
                bank, bb = nextpb()
                for kc in range(8):
                    mm(bank[0:nt, 0:8], hT1[:, kc, 0:nt], W1[:, kc, 5120:5128], kc == 0, kc == 7, wb(5120, 5128) + [bhT1], [bb], kc == 7)
                vtt(gb[0:nt, sl, :], bank[0:nt, 0:8], bg_t[0:nt, :], ALU.add, [bb, bbg], [bgb[sl]])
                aact(g4[0:nt, sl, 0:4], gb[0:nt, sl, 4:8], AF.Exp, [bgb[sl]], [bg4[sl]], scale=-1.0)
                aact(g4[0:nt, sl, 4:8], g4[0:nt, sl, 0:4], AF.Ln, [bg4[sl]], [bg4[sl]], bias=1.0)
                yield
                def tokproj(co):
                    bank, bb = nextpb()
                    for kc in range(8):
                        mm(bank[0:nt, :], hT1[:, kc, 0:nt], W1[:, kc, co:co + 512], kc == 0, kc == 7, wb(co, co + 512) + [bhT1], [bb], kc == 7)
                    return bank, bb
                bank, bb = tokproj(0)
                acopy(sz[0:nt, 0:512], bank[0:nt, :], [bb], [bsz])
                yield
                bank, bb = nextpb()
                mm(bank[0:nt, 0:4], tri_t[0:nt, 0:nt], g4[0:nt, sl, 4:8], True, True, [btri, bg4[sl]], [bb], True)
                vtt(g4[0:nt, sl, 8:12], gb[0:nt, sl, 0:4], bank[0:nt, 0:4], ALU.add, [bgb[sl], bb], [bg4[sl]])
                vcopy(g4[0:nt, sl, 12:16], bank[0:nt, 0:4], [bb], [bg4[sl]])
                yield
                bank, bb = tokproj(512)
                vcopy(sz[0:nt, 512:1024], bank[0:nt, :], [bb], [bsz])
                yield
                bank, bb = nextpb()
                tp(bank[0:4, 0:nt], g4[0:nt, sl, 8:12], identf[0:nt, 0:nt], [bg4[sl], bidf], [bb], False)
                tp(bank[0:4, 128:128 + nt], g4[0:nt, sl, 12:16], identf[0:nt, 0:nt], [bg4[sl], bidf], [bb], True)
                vcopy(cbt[:, sl, 0:256], bank[0:4, 0:256], [bb], [bcbt[sl]])
                for kc in range(8):
                    tp(pT[:, kc, 0:nt], sz[0:nt, kc * 128:(kc + 1) * 128], identb[0:nt, 0:nt], [bsz, bidb], [bpT], kc == 7)
                vcopy(qT1[:, s3, :, 0:nt], pT[:, :, 0:nt], [bpT], [bqT1[s3]])
                yield
                for half in range(2):
                    bank, bb = tokproj(1024 + half * 512)
                    acopy(ktok[0:nt, sl, half * 512:(half + 1) * 512], bank[0:nt, :], [bb], [bktok[sl]])
                    yield
                for half in range(2):
                    bank, bb = tokproj(2048 + half * 512)
                    vcopy(vx[0:nt, s3, half * 2:half * 2 + 2, 0:256], bank[0:nt, :].rearrange("p (h d) -> p h d", h=2), [bb], [bvx[s3]])
                    if half == 0:
                        for kc in range(8):
                            tp(pT[:, kc, 0:nt], ktok[0:nt, sl, kc * 128:(kc + 1) * 128], identb[0:nt, 0:nt], [bktok[sl], bidb], [bpT], kc == 7)
                        acopy(kT1[:, sl, :, 0:nt], pT[:, :, 0:nt], [bpT], [bkT1[sl]])
                    yield
                for half in range(2):
                    bank, bb = tokproj(3072 + half * 512)
                    aact(sg[0:nt, s3, half * 512:(half + 1) * 512], bank[0:nt, :], AF.Sigmoid, [bb], [bsg[s3]])
                    yield
                for half in range(2):
                    bank, bb = tokproj(4096 + half * 512)
                    aact(sz[0:nt, half * 512:(half + 1) * 512], bank[0:nt, :], AF.Silu, [bb], [bsz])
                    yield
                vtt(sg[0:nt, s3, :], sg[0:nt, s3, :], sz[0:nt, :], ALU.mult, [bsg[s3], bsz], [bsg[s3]], eng="pool")
                vtt(sg[0:nt, s3, :], sg[0:nt, s3, :], gml_t[0:nt, :], ALU.mult, [bsg[s3], bgml], [bsg[s3]], eng="pool")
                yield

            def genBe(nt, sl, s3):
                G = gsb[:, sl, :]
                S.op("dve", lambda e: e.tensor_reduce(out=sm4[:, 0:1], in_=cbt[:, sl, 0:nt], axis=mybir.AxisListType.X, op=ALU.max), [bcbt[sl]], [bsm4])
                vtt(sm4[:, 1:2], sm4[:, 0:1], mprev[:, :], ALU.max, [bsm4, bmprev], [bsm4])
                vts(sm4[:, 2:3], sm4[:, 1:2], -1.0, None, ALU.mult, None, [bsm4], [bsm4])
                aact(sm4[:, 3:4], mprev[:, :], AF.Exp, [bmprev, bsm4], [bsm4], bias=sm4[:, 2:3])
                aact(big4[:, 0:nt], cbt[:, sl, 0:nt], AF.Exp, [bcbt[sl], bsm4], [bbig4], bias=sm4[:, 2:3])
                aact(big4[:, 128:128 + nt], cbt[:, sl, 128:128 + nt], AF.Exp, [bcbt[sl], bsm4], [bbig4], bias=sm4[:, 2:3])
                vtt(mprev[:, :], sm4[:, 1:2], cbt[:, sl, 128 + nt - 1:128 + nt], ALU.subtract, [bsm4, bcbt[sl]], [bmprev])
                vts(sm4[:, 4:8], identf[0:4, 0:4], sm4[:, 3:4], None, ALU.mult, None, [bidf, bsm4], [bsm4])
                yield
                bank, bb = nextpb()
                mm(bank[0:nt, 0:4], big4[:, 0:nt], identf[0:4, 0:4], True, True, [bbig4, bidf], [bb], False)
                mm(bank[0:nt, 4:8], big4[:, 128:128 + nt], identf[0:4, 0:4], True, True, [bbig4, bidf], [bb], False)
                mm(bank[:, 8:12], ones4[:, :], sm4[:, 4:8], True, True, [bones4, bsm4], [bb], True)
                vcopy(G[0:nt, 0:8], bank[0:nt, 0:8], [bb], [bgsb[sl]])
                vcopy(G[:, 8:12], bank[:, 8:12], [bb], [bgsb[sl]])
                yield
                for hh in range(4):
                    aact(Cf[:, hh, :, :], Cf[:, hh, :, :], AF.Copy, [bCfh[hh], bgsb[sl]], [bCfh[hh]], scale=G[:, 8 + hh:9 + hh])
                    vcopy(Cb[:, sl, hh, :, :], Cf[:, hh, :, :], [bCfh[hh]], [bCb[sl]], eng="pool")
                S.op("dve", lambda e: e.tensor_tensor(out=wv[0:nt, :, :], in0=vx[0:nt, s3, :, :], in1=G[0:nt, 0:4].unsqueeze(2).to_broadcast([nt, 4, 257]), op=ALU.mult),
                     [bvx[s3], bgsb[sl]], [bwv])
                yield
                for hh in range(4):
                    for c in range(2):
                        bank, bb = nextpb()
                        mm(bank[:, 0:257], ktok[0:nt, sl, hh * 256 + c * 128:hh * 256 + (c + 1) * 128], wv[0:nt, hh, :], True, True, [bktok[sl], bwv], [bb], True)
                        vtt(Cf[:, hh, c, :], bank[:, 0:257], Cf[:, hh, c, :], ALU.add, [bCfh[hh], bb], [bCfh[hh]])
                    if hh % 2 == 1:
                        yield
                for hh in range(4):
                    for c in range(2):
                        mm(pPV[0:nt, hh * 128:hh * 128 + nt], kT1[:, sl, 2 * hh + c, 0:nt], qT1[:, s3, 2 * hh + c, 0:nt], c == 0, c == 1, [bkT1[sl], bqT1[s3]], [bpPV], hh == 3 and c == 1)
                for hh in range(4):
                    vstt(St[0:nt, sl, hh, 0:nt], pPV[0:nt, hh * 128:hh * 128 + nt], G[0:nt, hh:hh + 1], tri_t[0:nt, 0:nt], ALU.mult, ALU.mult, [bpPV, bgsb[sl], btri], [bSt[sl]])
                yield

            def genBl(nt, x_ap, bx, sl, s3, dst_dram):
                G = gsb[:, sl, :]
                for hh in range(4):
                    ro = pST[1][0:nt, hh % 2, 0:257]
                    for c in range(2):
                        mm(ro, qT1[:, s3, 2 * hh + c, 0:nt], Cb[:, sl, hh, c, :], c == 0, False, [bqT1[s3], bCb[sl]], [bro[hh % 2]], False)
                    mm(ro, St[0:nt, sl, hh, 0:nt], vx[0:nt, s3, hh, :], False, True, [bSt[sl], bvx[s3]], [bro[hh % 2]], True)
                    b8 = [bst8[hh]]
                    vcopy(st8[0:nt, hh, 2:3], ro[:, 256:257], [bro[hh % 2]], b8)
                    vstt(st8[0:nt, hh, 0:1], st8[0:nt, hh, 2:3], -1.0, st8[0:nt, hh, 2:3], ALU.mult, ALU.max, b8, b8)
                    vtt(st8[0:nt, hh, 0:1], st8[0:nt, hh, 0:1], G[0:nt, 4 + hh:5 + hh], ALU.max, b8 + [bgsb[sl]], b8)
                    vrecip(st8[0:nt, hh, 1:2], st8[0:nt, hh, 0:1], b8, b8)
                    aact(hc[0:nt, hh * 256:(hh + 1) * 256], ro[:, 0:256], AF.Copy, [bro[hh % 2]] + b8, [bhch[hh]] + b8, scale=st8[0:nt, hh, 1:2], accum=st8[0:nt, hh, 3:4])
                    aact(junk2[0:nt, hh % 2, :], hc[0:nt, hh * 256:(hh + 1) * 256], AF.Square, [bhch[hh]], [bjunk2[hh % 2]] + b8, accum=st8[0:nt, hh, 4:5])
                    yield
                vts(st8[0:nt, :, 5], st8[0:nt, :, 3], 1.0 / 256.0, None, ALU.mult, None, bst8, bst8)
                vtt(st8[0:nt, :, 6], st8[0:nt, :, 5], st8[0:nt, :, 5], ALU.mult, bst8, bst8)
                vts(st8[0:nt, :, 4], st8[0:nt, :, 4], 1.0 / 256.0, None, ALU.mult, None, bst8, bst8)
                vtt(st8[0:nt, :, 7], st8[0:nt, :, 4], st8[0:nt, :, 6], ALU.subtract, bst8, bst8)
                aact(st8[0:nt, :, 7], st8[0:nt, :, 7], AF.Ln, bst8, bst8, bias=EPS)
                aact(st8[0:nt, :, 7], st8[0:nt, :, 7], AF.Exp, bst8, bst8, scale=-0.5)
                for hh in range(4):
                    vts(hc[0:nt, hh * 256:(hh + 1) * 256], hc[0:nt, hh * 256:(hh + 1) * 256], st8[0:nt, hh, 5:6], st8[0:nt, hh, 7:8],
                        ALU.subtract, ALU.mult, [bhch[hh], bst8[hh]], [bhch[hh]])
                vtt(outm[0:nt, :], hc[0:nt, :], sg[0:nt, s3, :], ALU.mult, bhch + [bsg[s3]], [boutm])
                yield
                for kc in range(8):
                    tp(pT[:, kc, 0:nt], outm[0:nt, kc * 128:(kc + 1) * 128], identb[0:nt, 0:nt], [boutm, bidb], [bpT], kc == 7)
                acopy(outT[:, :, 0:nt], pT[:, :, 0:nt], [bpT], [boutT])
                yield
                halves = []
                bbs = []
                for hf in range(2):
                    bank, bb = nextpb()
                    for kc in range(8):
                        mm(bank[0:nt, :], outT[:, kc, 0:nt], W1[:, kc, 5128 + hf * 512:5128 + (hf + 1) * 512], kc == 0, kc == 7, [boutT] + wb(5128 + hf * 512, 5640 + hf * 512), [bb], kc == 7)
                    halves.append(bank[0:nt, :])
                    bbs.append(bb)
                post_norm_resid(nt, halves, bbs, 36, x_ap, bx, 1, 0, dst_dram)
                yield

            def drive(gens, order=None, nodrain=()):
                live = {i: g for i, g in enumerate(gens) if g is not None}
                for i in (order or ()):
                    if i in live:
                        try:
                            next(live[i])
                        except StopIteration:
                            del live[i]
                while [i for i in live if i not in nodrain]:
                    for i in sorted(live):
                        if i in nodrain:
                            continue
                        try:
                            next(live[i])
                        except StopIteration:
                            del live[i]

            def state_out(oC, on, om):
                stage = [hc[:, :].rearrange("p (h vb k) -> p h vb k", h=2, vb=2), ysb[:, 1, :].rearrange("p (h vb k) -> p h vb k", h=2, vb=2)]
                bst = [bhch, [bysb[1]]]
                for hh in range(4):
                    sg_ = stage[hh // 2]
                    for vb in range(2):
                        bank, bb = nextpb()
                        for c in range(2):
                            tp(bank[:, c * 128:(c + 1) * 128], Cf[:, hh, c, vb * 128:(vb + 1) * 128], identf[:, :], bCfh + [bidf], [bb], c == 1)
                        if vb == 0:
                            acopy(sg_[:, hh % 2, vb, :], bank[:, 0:256], [bb], bst[hh // 2])
                        else:
                            vcopy(sg_[:, hh % 2, vb, :], bank[:, 0:256], [bb], bst[hh // 2])
                    S.dma("sp", oC[hh].rearrange("(vb p) k -> p vb k", p=128), sg_[:, hh % 2, :, :], reads=bst[hh // 2], owner=bst[hh // 2][0])
                    for c in range(2):
                        S.dma("sp", on[hh:hh + 1, c * 128:(c + 1) * 128].rearrange("a k -> k a"), Cf[:, hh, c, 256:257], reads=bCfh, owner=bst[hh // 2][0])
                S.dma("sp", om, mprev[:, :], reads=[bmprev], owner=bysb[1])

            memset("pool", Cf[:], 0.0, bCfh)
            memset("pool", mprev[:], 0.0, [bmprev])
            NCH = 4 * NMT

            def mkA(g):
                if g >= NCH:
                    return None
                slot = g % 4
                S.dma("sp", xr[:, slot, :], x1d[g * 128:(g + 1) * 128, :], reads=[bx1d[g]], writes=[bxr[slot]], owner=bxr[slot])
                return genA(128, xr[:, slot, :], bxr[slot], 32, g % 2, g % 3)

            def mkBe(g):
                return genBe(128, g % 2, g % 3) if g < NCH else None

            def mkBl(g):
                if g < 0:
                    return None
                slot = g % 4
                return genBl(128, xr[:, slot, :], bxr[slot], g % 2, g % 3, yp[g * 128:(g + 1) * 128, :])
            drive([mkA(0)])
            A_next = mkA(1)
            if A_next is not None:
                next(A_next)
                next(A_next)
            for g in range(NCH + 1):
                A_after = mkA(g + 2)
                if g == NCH:
                    A_next = genA(NS, x1s_t[:, :], bx1s, 33, 0, 0)
                drive([mkBl(g - 1), mkBe(g), A_next, A_after],
                      order=[1, 0, 2, 0, 2, 1, 0, 2, 1, 0, 2, 3, 2, 1, 0, 2, 1, 3, 2, 1, 2, 0, 2, 2, 0, 2, 2, 2, 2], nodrain=(3,))
                A_next = A_after
            state_out(o_C_p, o_n_p, o_m_p)
            for hh in range(4):
                c0 = xr[:, hh, 0:512].rearrange("p (vb k) -> p vb k", vb=2)
                S.dma("sp", c0, sC[hh].rearrange("(vb p) k -> p vb k", p=128), writes=[bxr[hh]], owner=bxr[hh])
            for hh in range(4):
                c0 = xr[:, hh, 0:512].rearrange("p (vb k) -> p vb k", vb=2)
                for c in range(2):
                    bank, bb = nextpb()
                    for vb in range(2):
                        tp(bank[:, vb * 128:(vb + 1) * 128], c0[:, vb, c * 128:(c + 1) * 128], identf[:, :], [bxr[hh], bidf], [bb], vb == 1)
                    if c == 0:
                        acopy(Cf[:, hh, c, 0:256], bank[:, 0:256], [bb], [bCfh[hh]])
                    else:
                        vcopy(Cf[:, hh, c, 0:256], bank[:, 0:256], [bb], [bCfh[hh]])
                    S.dma("sp", Cf[:, hh, c, 256:257], sn[hh:hh + 1, c * 128:(c + 1) * 128].rearrange("a k -> k a"), writes=[bCfh[hh]], owner=bysb[1])
            S.dma("sp", mprev[:, :], sm, writes=[bmprev], owner=bysb[1])
            drive([genBe(NS, 0, 0)])
            drive([genBl(NS, x1s_t[:, :], bx1s, 0, 0, ys)])
            state_out(o_C_s, o_n_s, o_m_s)
            S.barrier()
            S.emit()

        S.barrier()
        S.emit()
    return nc


_CACHE = {}


def _host_consts(rel_bias):
    tab = np.asarray(rel_bias[0], np.float32)
    k = np.arange(128)[:, None, None]
    d = np.arange(5)[None, :, None]
    q = np.arange(128)[None, None, :]
    idx = np.clip(128 * d + q - k, -128, 128) + 128
    biasT = np.ascontiguousarray(tab[:, idx].transpose(1, 0, 2, 3))
    mask5 = np.ones((128, 5, 128), np.float32)
    mask5[64:, 0, :64] = 0.0
    mask5[:64, 4, 64:] = 0.0
    qs = np.arange(NS)[None, None, :]
    blk = np.arange(5)[None, :, None]
    kk = np.arange(128)[:, None, None]
    rel = np.where(blk < 4, 512 + qs - (128 * blk + kk), qs - kk)
    idxs = np.clip(rel, -128, 128) + 128
    biasS = np.ascontiguousarray(tab[:, idxs].transpose(1, 0, 2, 3))
    corr = np.ones((128, 4, 16), np.float32)
    for g, w in enumerate(POOLW):
        t = np.arange(16)
        corr[:, g, :] = w / np.minimum(t + 1, w)
    return biasT, mask5, biasS, corr


def _relayout_w(w):
    k, n = w.shape
    return np.ascontiguousarray(w.reshape(8, 128, n).transpose(1, 0, 2))


def kernel(x_prompt, x_sample, cache_pool, cache_k, cache_v, state_C, state_n, state_m,
           norm_pre, norm_post, w_in_even, w_pool_mix, pool_scale, rel_bias, w_out_even,
           w_in_odd, b_gate_odd, mlstm_norm, w_out_odd, _stage=2):
    f = lambda a: np.ascontiguousarray(np.asarray(a, np.float32))
    if ("nc", _stage) not in _CACHE:
        _CACHE[("nc", _stage)] = build_nc(_stage)
    nc = _CACHE[("nc", _stage)]
    biasT, mask5, biasS, corr = _host_consts(f(rel_bias))
    tri = np.triu(np.ones((128, 128), np.float32))
    sel = np.zeros((4, 4, 128), np.float32)
    for hh in range(4):
        sel[hh, hh, :] = 1.0
    shared = {
        "gpre": np.ascontiguousarray(f(norm_pre).reshape(2, 8, 128).transpose(2, 0, 1)),
        "gpost": np.ascontiguousarray(np.broadcast_to(f(norm_post)[None], (128, 2, D))),
        "w0in": _relayout_w(f(w_in_even)[0]), "w0out": _relayout_w(f(w_out_even)[0]),
        "wmix": np.ascontiguousarray(f(w_pool_mix)[0].transpose(1, 0, 2)),
        "pscale": np.ascontiguousarray(f(pool_scale)[0].reshape(4, 128).T),
        "biasT": biasT, "mask5": mask5, "biasS": biasS, "corr": corr,
        "ident": np.eye(128, dtype=np.float32),
        "w1in": _relayout_w(f(w_in_odd)[0]), "w1out": _relayout_w(f(w_out_odd)[0]),
        "bgate": np.ascontiguousarray(np.broadcast_to(f(b_gate_odd)[0][None, :], (128, 8))),
        "gml": np.ascontiguousarray(np.broadcast_to(f(mlstm_norm)[0][None], (128, D))),
        "tri": tri, "sel": sel,
    }
    in_maps = []
    for c in range(8):
        m = dict(shared)
        m.update({
            "xp": f(x_prompt[c]), "xs": f(x_sample[c]),
            "cpool": f(cache_pool[0, c]), "ck": f(cache_k[0, c]).reshape(512, 512), "cv": f(cache_v[0, c]).reshape(512, 512),
            "sC": f(state_C[0, c]), "sn": f(state_n[0, c]), "sm": f(state_m[0, c]).reshape(4, 1),
        })
        in_maps.append(m)
    res = run_bass_kernel_spmd(nc, in_maps, core_ids=list(range(8)))
    R = res.results

    def gather(name, shape):
        return np.stack([np.asarray(r[name], np.float32).reshape(shape) for r in R], 0)
    y_p = gather("yp", (T, D)); y_s = gather("ys", (NS, D))
    pool_p = gather("pool_p", (15, 512))[None]
    k_p = gather("k_p", (512, 8, 64))[None]; v_p = gather("v_p", (512, 8, 64))[None]
    C_p = gather("C_p", (4, 256, 256))[None]; n_p = gather("n_p", (4, 256))[None]; m_p = gather("m_p", (4,))[None]
    pool_s = gather("pool_s", (15, 512))[None]
    k_s = gather("k_s", (NS, 8, 64))[None]; v_s = gather("v_s", (NS, 8, 64))[None]
    C_s = gather("C_s", (4, 256, 256))[None]; n_s = gather("n_s", (4, 256))[None]; m_s = gather("m_s", (4,))[None]
    return (y_p, y_s, pool_p, k_p, v_p, C_p, n_p, m_p, pool_s, k_s, v_s, C_s, n_s, m_s)
```

```python
import numpy as np
from contextlib import ExitStack
import concourse.bass as bass
import concourse.mybir as mybir
from concourse.bass_utils import run_bass_kernel_spmd

F32 = mybir.dt.float32
BF16 = mybir.dt.bfloat16
ALU = mybir.AluOpType
AF = mybir.ActivationFunctionType

D = 1024
T = 4096
NS = 32
NMT = 8
POOLW = (2, 4, 8, 16)
EPS = 1e-6


class Buf:
    __slots__ = ("name", "lw", "readers", "dsem", "excl")

    def __init__(self, name="", excl=False):
        self.name = name
        self.excl = excl
        self.lw = None
        self.readers = {}
        self.dsem = None


class Sched:
    ENG = ("pe", "act", "dve", "pool", "sp")

    def __init__(self, nc, stack):
        self.nc = nc
        self.stack = stack
        self.prog = {e: [] for e in self.ENG}
        self.sems = {}
        self.issued = {}
        self.isdma = {}
        self.seen = {e: {} for e in self.ENG}
        for e in ("pe", "act", "dve", "pool"):
            self._mksem(e, False)
        self.ndma = 0

    def _mksem(self, key, isdma):
        self.sems[key] = self.stack.enter_context(self.nc.semaphore("s_" + key))
        self.issued[key] = 0
        self.isdma[key] = isdma

    def _waits(self, eng, reads, writes):
        need = {}

        def add(k, v):
            if self.isdma[k]:
                v = self.issued[k]
            if v > need.get(k, 0):
                need[k] = v
        for b in reads:
            if b.lw is not None:
                add(*b.lw)
            if b.excl:
                for k, v in b.readers.items():
                    if k != eng:
                        add(k, v)
        for b in writes:
            if b.lw is not None:
                add(*b.lw)
            for k, v in b.readers.items():
                add(k, v)
        out = []
        for k, v in need.items():
            if k == "pe" and eng == "pe":
                continue
            if self.seen[eng].get(k, 0) >= v:
                continue
            self.seen[eng][k] = v
            out.append((k, v))
        return out

    def _record(self, ev, reads, writes):
        k, v = ev
        for b in reads:
            if b.readers.get(k, 0) < v:
                b.readers[k] = v
        for b in writes:
            b.lw = ev
            b.readers = {}

    def op(self, eng, fn, reads=(), writes=(), inc=True):
        waits = self._waits(eng, reads, writes)
        if inc:
            self.issued[eng] += 1
            ev = (eng, self.issued[eng])
        else:
            ev = (eng, self.issued[eng] + 1)
        self.prog[eng].append((waits, fn, eng if inc else None))
        self._record(ev, reads, writes)
        return ev

    def dma(self, q, out_ap, in_ap, reads=(), writes=(), owner=None, **kw):
        waits = self._waits(q, reads, writes)
        if owner is None:
            owner = (list(writes) + list(reads))[0]
        if owner.dsem is None:
            self.ndma += 1
            owner.dsem = "d%d" % self.ndma
            self._mksem(owner.dsem, True)
        semkey = owner.dsem
        self.issued[semkey] += 16
        ev = (semkey, self.issued[semkey])

        def fn(e, out_ap=out_ap, in_ap=in_ap, kw=kw):
            return e.dma_start(out=out_ap, in_=in_ap, **kw)
        self.prog[q].append((waits, fn, semkey))
        self._record(ev, reads, writes)
        return ev

    def barrier(self):
        for e in self.ENG:
            waits = []
            for k, v in self.issued.items():
                if v > self.seen[e].get(k, 0):
                    self.seen[e][k] = v
                    waits.append((k, v))
            self.prog[e].append((waits, None, None))

    def emit(self):
        nc = self.nc
        prog = self.prog
        self.prog = {e: [] for e in self.ENG}
        with nc.Block() as block:
            def run(e, items):
                for waits, fn, inck in items:
                    for k, v in waits:
                        e.wait_ge(self.sems[k], v)
                    if fn is None:
                        continue
                    ins = fn(e)
                    if inck is not None:
                        ins.then_inc(self.sems[inck], 16 if self.isdma[inck] else 1)

            @block.tensor
            def _(e):
                run(e, prog["pe"])

            @block.scalar
            def _(e):
                run(e, prog["act"])

            @block.vector
            def _(e):
                run(e, prog["dve"])

            @block.gpsimd
            def _(e):
                run(e, prog["pool"])

            @block.sync
            def _(e):
                run(e, prog["sp"])


def build_nc(stage=2):
    nc = bass.Bass("TRN2", target_bir_lowering=False)

    def din(name, shape):
        return nc.dram_tensor(name, shape, F32, kind="ExternalInput").ap()

    def dout(name, shape):
        return nc.dram_tensor(name, shape, F32, kind="ExternalOutput").ap()
    xp = din("xp", [T, D]); xs = din("xs", [NS, D])
    cpool = din("cpool", [15, 512]); ck = din("ck", [512, 512]); cv = din("cv", [512, 512])
    sC = din("sC", [4, 256, 256]); sn = din("sn", [4, 256]); sm = din("sm", [4, 1])
    gpre = din("gpre", [128, 2, 8]); gpost = din("gpost", [128, 2, D])
    w0in = din("w0in", [128, 8, 3072]); w0out = din("w0out", [128, 8, D])
    wmix = din("wmix", [128, 4, 128]); pscale = din("pscale", [128, 4])
    biasT = din("biasT", [128, 8, 5, 128]); mask5 = din("mask5", [128, 5, 128]); biasS = din("biasS", [128, 8, 5, NS])
    corr = din("corr", [128, 4, 16]); ident = din("ident", [128, 128])
    w1in = din("w1in", [128, 8, 5128]); w1out = din("w1out", [128, 8, D])
    bgate = din("bgate", [128, 8]); gml = din("gml", [128, D])
    tri = din("tri", [128, 128]); sel = din("sel", [4, 4, 128])
    yp = dout("yp", [T, D]); ys = dout("ys", [NS, D])
    o_pool_p = dout("pool_p", [15, 512]); o_k_p = dout("k_p", [512, 512]); o_v_p = dout("v_p", [512, 512])
    o_C_p = dout("C_p", [4, 256, 256]); o_n_p = dout("n_p", [4, 256]); o_m_p = dout("m_p", [4, 1])
    o_pool_s = dout("pool_s", [15, 512]); o_k_s = dout("k_s", [NS, 512]); o_v_s = dout("v_s", [NS, 512])
    o_C_s = dout("C_s", [4, 256, 256]); o_n_s = dout("n_s", [4, 256]); o_m_s = dout("m_s", [4, 1])
    x1d = nc.dram_tensor("x1d", [T, D], F32, kind="Internal").ap()
    bx1d = [Buf("x1d%d" % i) for i in range(4 * NMT)]

    with ExitStack() as st:
        S = Sched(nc, st)

        def mk(stack):
            def sb(name, shape, dt=F32):
                return stack.enter_context(nc.sbuf_tensor(name, shape, dt))

            def ps(name, shape, dt=F32):
                return stack.enter_context(nc.psum_tensor(name, shape, dt))
            return sb, ps
        sb, ps = mk(st)

        bWs = [Buf("W%d" % i) for i in range(13)]

        def wb(c0, c1):
            return bWs[c0 // 512:(c1 - 1) // 512 + 1]
        stg = sb("stg", [128, 4, 1536]); bstg = [Buf("stg%d" % i) for i in range(4)]
        identf = sb("identf", [128, 128]); bidf = Buf("idf")
        identb = sb("identb", [128, 128], BF16); bidb = Buf("idb")
        gpre_t = sb("gpre_t", [128, 2, 8]); bgpre = Buf("gpre")
        gpost_t = sb("gpost_t", [128, 1, D]); bgpost = Buf("gpost")
        x1s_t = sb("x1s_t", [NS, D]); bx1s = Buf("x1s")
        junk = sb("junk", [128, D], BF16); bjunk = Buf("junk")
        ysb = sb("ysb", [128, 2, D]); bysb = [Buf("ysb%d" % i) for i in range(2)]
        h = sb("h", [128, 2, D], BF16); bh = [Buf("h%d" % i) for i in range(2)]
        ms = sb("ms", [128, 40]); rs = sb("rs", [128, 40])
        bms = [Buf("ms%d" % i) for i in range(10)]; brs = [Buf("rs%d" % i) for i in range(10)]
        pT = ps("pT", [128, 8, 128], BF16); bpT = Buf("pT", True)
        pbb = ps("pbb", [128, 2, 512]); pb = [pbb[:, 0, :], pbb[:, 1, :]]; bpb = [Buf("pb%d" % i, True) for i in range(2)]
        pST = [ps("pST%d" % i, [128, 2, 512]) for i in range(2)]; bpST = [Buf("pST%d" % i, True) for i in range(2)]
        pPV = ps("pPV", [128, 512]); bpPV = Buf("pPV", True)
        pbi = [0]

        pbl = [(pb[0], bpb[0]), (pb[1], bpb[1])]

        def nextpb():
            i = pbi[0] % len(pbl)
            pbi[0] += 1
            return pbl[i]

        def mm(out, lhsT, rhs, start, stop, reads, writes, inc):
            S.op("pe", lambda e: e.matmul(out=out, lhsT=lhsT, rhs=rhs, start=start, stop=stop), reads, writes, inc)

        def tp(out, in_, idn, reads, writes, inc):
            S.op("pe", lambda e: e.transpose(out=out, in_=in_, identity=idn), reads, writes, inc)

        def acopy(out, in_, reads, writes):
            S.op("act", lambda e: e.copy(out=out, in_=in_), reads, writes)

        def aact(out, in_, func, reads, writes, scale=1.0, bias=None, accum=None):
            kw = {}
            if bias is not None:
                kw["bias"] = bias
            if accum is not None:
                kw["accum_out"] = accum
            S.op("act", lambda e: e.activation(out=out, in_=in_, func=func, scale=scale, **kw), reads, writes)

        def vcopy(out, in_, reads, writes, eng="dve"):
            S.op(eng, lambda e: e.tensor_copy(out=out, in_=in_), reads, writes)

        def vtt(out, in0, in1, op, reads, writes, eng="dve"):
            S.op(eng, lambda e: e.tensor_tensor(out=out, in0=in0, in1=in1, op=op), reads, writes)

        def vts(out, in0, s1, s2, op0, op1, reads, writes):
            if s2 is None:
                S.op("dve", lambda e: e.tensor_scalar(out=out, in0=in0, scalar1=s1, scalar2=None, op0=op0), reads, writes)
            else:
                S.op("dve", lambda e: e.tensor_scalar(out=out, in0=in0, scalar1=s1, scalar2=s2, op0=op0, op1=op1), reads, writes)

        def vstt(out, in0, scalar, in1, op0, op1, reads, writes):
            S.op("dve", lambda e: e.scalar_tensor_tensor(out=out, in0=in0, scalar=scalar, in1=in1, op0=op0, op1=op1), reads, writes)

        def vrecip(out, in_, reads, writes):
            S.op("dve", lambda e: e.reciprocal(out=out, in_=in_), reads, writes)

        def memset(eng, ap, val, writes):
            S.op(eng, lambda e: e.memset(ap, val), (), writes)

        S.dma("sp", identf[:], ident, writes=[bidf])
        S.dma("sp", gpre_t[:], gpre, writes=[bgpre])
        S.dma("sp", gpost_t[:, 0, :], gpost[:, 0, :], writes=[bgpost])
        vcopy(identb[:], identf[:], [bidf], [bidb])

        def load_weights(warena, src, ncols, dst_col0, layer, kscale_cols=None, chunks=None, on_pool=True):
            if chunks is None:
                chunks = [(c0, min(1024, ncols - c0)) for c0 in range(0, ncols, 1024)]
            for c0, cw in chunks:
                wr = wb(dst_col0 + c0, dst_col0 + c0 + cw)
                extra = None
                if kscale_cols is not None and kscale_cols[0] <= c0 < kscale_cols[1]:
                    extra = kscale_cols[2]
                for kc in range(8):
                    s = load_weights.si % 4
                    load_weights.si += 1
                    S.dma("sp", stg[:, s, 0:cw], src[:, kc, c0:c0 + cw], writes=[bstg[s]])
                    o = warena[:, kc, dst_col0 + c0:dst_col0 + c0 + cw]
                    i = stg[:, s, 0:cw]
                    gsrc = None if layer is None else (gpre16 if extra is not None else gpre_t[:, layer, :])
                    if on_pool:
                        if layer is None:
                            vcopy(o, i, [bstg[s]], wr, eng="pool")
                        else:
                            vtt(o, i, gsrc[:, kc:kc + 1].to_broadcast([128, cw]), ALU.mult, [bstg[s], bgpre], wr, eng="pool")
                    elif layer is None:
                        if load_weights.si % 2:
                            vcopy(o, i, [bstg[s]], wr)
                        else:
                            acopy(o, i, [bstg[s]], wr)
                    elif load_weights.si % 2:
                        vts(o, i, gsrc[:, kc:kc + 1], None, ALU.mult, None, [bstg[s], bgpre], wr)
                    else:
                        aact(o, i, AF.Copy, [bstg[s], bgpre], wr, scale=gsrc[:, kc:kc + 1])
        load_weights.si = 0
        gpre16 = sb("gpre16", [128, 8])
        vts(gpre16[:], gpre_t[:, 1, :], 1.0 / 16.0, None, ALU.mult, None, [bgpre], [bgpre])

        with ExitStack() as st0:
            sb0, _ = mk(st0)
            W0 = sb0("W0", [128, 8, 4096], BF16)
            xr = stg[:, :, 0:D]; bxr = bstg
            xq = sb0("xq", [128, 2, D]); bxq = [Buf("xq%d" % i) for i in range(2)]
            hT = sb0("hT", [128, 8, 512], BF16); bhT = Buf("hT")
            uT = sb0("uT", [128, 4, 528]); buT = Buf("uT")
            qT = sb0("qT", [128, 4, 512], BF16); bqT = Buf("qT")
            kT = sb0("kT", [128, 4, 1024], BF16); bkT = [Buf("kT%d" % i) for i in range(2)]
            gpT = sb0("gpT", [128, 4, 512], BF16); bgpT = Buf("gpT")
            Vr = sb0("Vr", [128, 8, 8, 65], BF16); bV = [Buf("V%d" % i) for i in range(8)]
            gatt = sb0("gatt", [128, 4, 512], BF16); bgatt = [Buf("gatt%d" % i) for i in range(4)]
            pooledT = sb0("pooledT", [128, 4, 512], BF16); bpooled = Buf("pooled")
            tA = sb0("tA", [128, 2, 528]); btA = [Buf("tA0"), Buf("tA1")]
            ostg = tA[:, :, 0:512]; bostg = btA
            tB = sb0("tB", [128, 2, 528]); btB = [Buf("tB0"), Buf("tB1")]
            mixedT = sb0("mixedT", [128, 8, 512], BF16); bmixed = Buf("mixedT")
            E5 = sb0("E5", [128, 8, 5, 128], BF16); bE5 = Buf("E5")
            expo2 = sb0("expo", [128, 2, 640], BF16); bexpo = [Buf("expo0"), Buf("expo1"), Buf("expo2")]
            PT2 = sb0("PT", [128, 2, 640], BF16); bPT = [Buf("PT0"), Buf("PT1"), Buf("PT2")]
            expoS = [expo2[:, 0, :], expo2[:, 1, :], stg[:, 0, 1024:1536].bitcast(BF16)[:, 0:640]]
            PTS = [PT2[:, 0, :], PT2[:, 1, :], stg[:, 1, 1024:1536].bitcast(BF16)[:, 0:640]]
            ST3 = [pST[0], pST[1], pbb]
            bST3 = [[bpST[0]], [bpST[1]], bpb]
            rc = sb0("rc", [128, 2, 4]); brc = [Buf("rc0"), Buf("rc1")]
            tmpn = sb0("tmpn", [128, 2, 256]); btmpn = [Buf("tn0"), Buf("tn1")]
            attg = sb0("attg", [128, 512], BF16); battg = Buf("attg")
            attgs = [attg, stg[:, 2, 1024:1536].bitcast(BF16)[:, 0:512]]; battgs = [battg, Buf("attg2")]
            wmixb = sb0("wmixb", [128, 4, 128], BF16); bwmix = Buf("wmix")
            pscale_t = sb0("pscale_t", [128, 4]); bpscale = Buf("pscale")
            corr_t = sb0("corr_t", [128, 4, 16]); bcorr = Buf("corr")
            hTs = sb0("hTs", [128, 8, NS], BF16); bhTs = Buf("hTs")
            uTs = sb0("uTs", [128, 4, 48]); buTs = Buf("uTs")
            qTs = sb0("qTs", [128, 4, NS], BF16); bqTs = Buf("qTs")
            kTs = sb0("kTs", [128, 4, NS], BF16); bkTs = Buf("kTs")
            cstage = stg[:, 0:2, :].rearrange("p a b -> p (a b)")[:, 0:2048].rearrange("p (b f) -> p b f", b=4)

            wmixf = tB[:, 0, 0:512].rearrange("p (g d) -> p g d", g=4)
            S.dma("sp", wmixf, wmix, writes=[btB[0]])
            vcopy(wmixb[:], wmixf, [btB[0]], [bwmix])
            mask_f = tA[:].rearrange("p a b -> p (a b)")[:, 0:640]
            bmask = btA
            S.dma("sp", pscale_t[:], pscale, writes=[bpscale])
            S.dma("sp", corr_t[:], corr, writes=[bcorr])
            S.dma("sp", mask_f, mask5.rearrange("p d q -> p (d q)"), writes=bmask)
            for hh in range(8):
                s = hh % 4
                S.dma("sp", stg[:, s, 0:640], biasT[:, hh, :, :].rearrange("p d q -> p (d q)"), writes=[bstg[s]])
                aact(stg[:, s, 0:640], stg[:, s, 0:640], AF.Exp, [bstg[s]], [bstg[s]])
                vtt(E5[:, hh, :, :].rearrange("p d q -> p (d q)"), stg[:, s, 0:640], mask_f, ALU.mult,
                    [bstg[s]] + bmask, [bE5])
            memset("pool", Vr[:, :, :, 64:65], 1.0, bV)
            memset("pool", uT[:, :, 0:16], 0.0, [buT])

            def norm_stats(x_ap, nt, col, bx, bm_):
                aact(junk[0:nt, :], x_ap, AF.Square, [bx], [bjunk, bm_], scale=1.0 / 32.0, accum=ms[0:nt, col:col + 1])

            def norm_rstd(nt, c0, c1, bm_, br_):
                aact(rs[0:nt, c0:c1], ms[0:nt, c0:c1], AF.Ln, [bm_], [br_], bias=EPS)
                aact(rs[0:nt, c0:c1], rs[0:nt, c0:c1], AF.Exp, [br_], [br_], scale=-0.5)

            def make_hT(x_ap, nt, col, bx, br_, hslot, hT_out, bhT_out):
                hb = h[0:nt, hslot, :]
                vts(hb, x_ap, rs[0:nt, col:col + 1], None, ALU.mult, None, [bx, br_], [bh[hslot]])
                for kc in range(8):
                    tp(pT[:, kc, 0:nt], hb[:, kc * 128:(kc + 1) * 128], identb[0:nt, 0:nt], [bh[hslot], bidb], [bpT], kc == 7)
                acopy(hT_out, pT[:, :, 0:nt], [bpT], [bhT_out])

            def pool_branch(uTt, buT_, L, first, pooled_ap, bpooled_, gp_ap, bgp_, mixed_out, bmixed_):
                pool_sums(uTt, buT_, L, first, pooled_ap, bpooled_)
                pool_mix(L, pooled_ap, bpooled_, gp_ap, bgp_, mixed_out, bmixed_)

            def pool_sums(uTt, buT_, L, first, pooled_ap, bpooled_):
                for g, w in enumerate(POOLW):
                    tbufs = (tA, btA) if g < 2 else (tB, btB)
                    eng = "dve" if g < 2 else "pool"
                    cur = uTt[:, g, :]
                    curb = buT_
                    base = 16 - (w - 1)
                    lo_t = -(w - 1)
                    k = 1
                    step = 0
                    while k < w:
                        new_lo = lo_t + k
                        n = L - new_lo
                        c_hi = base + k
                        dst = tbufs[0][:, step % 2, 0:n]
                        dstb = tbufs[1][step % 2]
                        vtt(dst, cur[:, c_hi:c_hi + n], cur[:, c_hi - k:c_hi - k + n], ALU.add, [curb], [dstb], eng=eng)
                        cur = tbufs[0][:, step % 2, :]
                        curb = dstb
                        base = 0
                        lo_t = new_lo
                        k *= 2
                        step += 1
                    s_ap = cur[:, 0:L]
                    if first:
                        vtt(cur[:, 0:16], cur[:, 0:16], corr_t[:, g, :], ALU.mult, [curb, bcorr], [curb])
                    vstt(pooled_ap[:, g, 0:L], s_ap, 1.0 / w, uTt[:, g, 16:16 + L], ALU.mult, ALU.subtract, [curb, buT_], [bpooled_])

            def pool_mix(L, pooled_ap, bpooled_, gp_ap, bgp_, mixed_out, bmixed_):
                for g in range(4):
                    bank, bb = nextpb()
                    mm(bank[:, 0:L], wmixb[:, g, :], pooled_ap[:, g, 0:L], True, True, [bwmix, bpooled_], [bb], True)
                    vstt(mixed_out[:, g, 0:L], bank[:, 0:L], pscale_t[:, g:g + 1], gp_ap[:, g, 0:L], ALU.mult, ALU.mult,
                         [bb, bpscale, bgp_], [bmixed_])

            def post_norm_resid(nt, halves, bbs, col, x_ap, bx, layer, yslot, dst_dram, dst_sb=None, bdst=None, bdram=None):
                ya = ysb[0:nt, yslot, :]
                by = bysb[yslot]
                for hf in range(2):
                    aact(junk[0:nt, 0:512], halves[hf], AF.Square, [bbs[hf]], [bjunk, bms[8]], scale=1.0 / 32.0, accum=ms[0:nt, col + hf:col + hf + 1])
                    vcopy(ya[:, hf * 512:(hf + 1) * 512], halves[hf], [bbs[hf]], [by])
                vtt(ms[0:nt, col:col + 1], ms[0:nt, col:col + 1], ms[0:nt, col + 1:col + 2], ALU.add, [bms[8]], [bms[8]])
                aact(rs[0:nt, col:col + 1], ms[0:nt, col:col + 1], AF.Ln, [bms[8]], [brs[8]], bias=EPS)
                aact(rs[0:nt, col:col + 1], rs[0:nt, col:col + 1], AF.Exp, [brs[8]], [brs[8]], scale=-0.5)
                vstt(ya, ya, rs[0:nt, col:col + 1], gpost_t[0:nt, 0, :], ALU.mult, ALU.mult, [by, brs[8], bgpost], [by])
                if dst_sb is None:
                    vtt(ya, ya, x_ap, ALU.add, [by, bx], [by], eng="pool")
                    S.dma("sp", dst_dram, ya, reads=[by], writes=([bdram] if bdram is not None else []), owner=by)
                else:
                    vtt(dst_sb, ya, x_ap, ALU.add, [by, bx], [bdst], eng="pool")

            def pre_load(g):
                S.dma("sp", xr[:, g % 4, :], xp[g * 128:(g + 1) * 128, :], writes=[bxr[g % 4]])

            def pre_stats(m):
                for j in range(4):
                    g = 4 * m + j
                    norm_stats(xr[:, g % 4, :], 128, g, bxr[g % 4], bms[m])
                norm_rstd(128, 4 * m, 4 * m + 4, bms[m], brs[m])

            def pre_T(g):
                m, j = divmod(g, 4)
                make_hT(xr[:, g % 4, :], 128, g, bxr[g % 4], brs[m], g % 2, hT[:, :, j * 128:(j + 1) * 128], bhT)

            def pre_scale(g):
                m, j = divmod(g, 4)
                vts(h[:, g % 2, :], xr[:, g % 4, :], rs[:, g:g + 1], None, ALU.mult, None, [bxr[g % 4], brs[m]], [bh[g % 2]])

            def pre_tp(g):
                m, j = divmod(g, 4)
                hb = h[:, g % 2, :]
                for kc in range(8):
                    tp(pT[:, kc, :], hb[:, kc * 128:(kc + 1) * 128], identb[:, :], [bh[g % 2], bidb], [bpT], kc == 7)
                acopy(hT[:, :, j * 128:(j + 1) * 128], pT[:, :, :], [bpT], [bhT])

            def proj_feat(co, N, rhsT, brhs):
                bank, bb = nextpb()
                for kc in range(8):
                    mm(bank[:, 0:N], W0[:, kc, co:co + 128], rhsT[:, kc, 0:N], kc == 0, kc == 7, wb(co, co + 128) + [brhs], [bb], kc == 7)
                return bank, bb

            def proj_tok(co, nt, lhs, blhs):
                bank, bb = nextpb()
                for kc in range(8):
                    mm(bank[0:nt, :], lhs[:, kc, :], W0[:, kc, co:co + 512], kc == 0, kc == 7, wb(co, co + 512) + [blhs], [bb], kc == 7)
                return bank, bb

            def inf_stage(m):
                slot = m % 2
                for c in range(4):
                    bank, bb = proj_feat(c * 128, 512, hT, bhT)
                    acopy(uT[:, c, 16:528], bank[:, :], [bb], [buT])
                for c in range(4):
                    bank, bb = proj_feat(512 + c * 128, 512, hT, bhT)
                    vcopy(qT[:, c, :], bank[:, :], [bb], [bqT])
                for c in range(4):
                    bank, bb = proj_feat(1024 + c * 128, 512, hT, bhT)
                    if c % 2:
                        vcopy(kT[:, c, slot * 512:(slot + 1) * 512], bank[:, :], [bb], [bkT[slot]])
                    else:
                        acopy(kT[:, c, slot * 512:(slot + 1) * 512], bank[:, :], [bb], [bkT[slot]])
                for c in range(4):
                    bank, bb = proj_feat(2048 + c * 128, 512, hT, bhT)
                    aact(gpT[:, c, :], bank[:, :], AF.Silu, [bb], [bgpT])
                pool_sums(uT, buT, 512, m == 0, pooledT, bpooled)
                pool_hist(m)

            def int_stage(m):
                for j in range(4):
                    g = 4 * m + j
                    lhs = hT[:, :, j * 128:(j + 1) * 128]
                    bank, bb = proj_tok(1536, 128, lhs, bhT)
                    chk(130)
                    vcopy(Vr[:, g % 8, :, 0:64], bank[:, :].rearrange("p (h d) -> p h d", h=8), [bb], [bV[g % 8]])
                    chk(131)
                    if m == NMT - 1:
                        acopy(ostg[:, 0, :], bank[:, :], [bb], [bostg[0]])
                        chk(1315)
                        S.dma("sp", o_v_p[j * 128:(j + 1) * 128, :], ostg[:, 0, :], reads=[bostg[0]])
                    chk(132)
                    bank, bb = proj_tok(2560, 128, lhs, bhT)
                    aact(gatt[:, j, :], bank[:, :], AF.Silu, [bb], [bgatt[j]])
                    chk(133)
                    if m == NMT - 1:
                        bank, bb = proj_tok(1024, 128, lhs, bhT)
                        acopy(ostg[:, 1, :], bank[:, :], [bb], [bostg[1]])
                        S.dma("sp", o_k_p[j * 128:(j + 1) * 128, :], ostg[:, 1, :], reads=[bostg[1]])

            def pool_stage(m):
                pool_mix(512, pooledT, bpooled, gpT, bgpT, mixedT, bmixed)

            def pool_hist(m):
                if m == NMT - 1:
                    bank, bb = nextpb()
                    for g in range(4):
                        tp(bank[0:15, g * 128:(g + 1) * 128], uT[:, g, 513:528], identf[:, :], [buT, bidf], [bb], g == 3)
                    acopy(ostg[0:15, 0, :], bank[0:15, :], [bb], [bostg[0]])
                    S.dma("sp", o_pool_p, ostg[0:15, 0, :], reads=[bostg[0]])
                else:
                    S.op("pool", lambda e: e.tensor_copy(out=uT[:, :, 1:16], in_=uT[:, :, 513:528]), [buT], [buT])

            def att_qk(m, qb, hh, buf):
                gq = 4 * m + qb
                nkb = min(gq, 4) + 1
                r0 = (hh % 2) * 64
                c = hh // 2
                for d in range(nkb):
                    gk = gq - d
                    slot = (gk // 4) % 2
                    off = slot * 512 + (gk % 4) * 128
                    mm(ST3[buf][:, d // 4, (d % 4) * 128:(d % 4) * 128 + 128], kT[r0:r0 + 64, c, off:off + 128], qT[r0:r0 + 64, c, qb * 128:(qb + 1) * 128],
                       True, True, [bkT[slot], bqT], bST3[buf], d == nkb - 1)
                n = nkb * 128
                src = ST3[buf][:].rearrange("p a b -> p (a b)")[:, 0:n]
                aact(expoS[buf][:, 0:n], src, AF.Exp, bST3[buf], [bexpo[buf]], scale=0.125)
                vtt(PTS[buf][:, 0:n], expoS[buf][:, 0:n], E5[:, hh, 0:nkb, :].rearrange("p d q -> p (d q)"), ALU.mult, [bexpo[buf], bE5], [bPT[buf]])

            def att_pv(m, qb, hh, buf):
                gq = 4 * m + qb
                nkb = min(gq, 4) + 1
                hq = hh % 4
                for d in range(nkb):
                    gk = gq - d
                    mm(pPV[:, hq * 65:hq * 65 + 65], PTS[buf][:, d * 128:(d + 1) * 128], Vr[:, gk % 8, hh, :], d == 0, d == nkb - 1,
                       [bPT[buf], bV[gk % 8]], [bpPV], d == nkb - 1)

            def att_norm(qb, hg, gatt_ap, bg_, nt=128, asl=0):
                pv = pPV[0:nt, 0:260].rearrange("p (h d) -> p h d", h=4)
                r = hg % 2
                vrecip(rc[0:nt, r, :], pv[:, :, 64], [bpPV], [brc[r]])
                tn = tmpn[0:nt, r, :].rearrange("p (h d) -> p h d", h=4)
                vtt(tn, pv[:, :, 0:64], rc[0:nt, r, :].unsqueeze(2).to_broadcast([nt, 4, 64]), ALU.mult, [bpPV, brc[r]], [btmpn[r]])
                vtt(attgs[asl][0:nt, hg * 256:(hg + 1) * 256], tmpn[0:nt, r, :], gatt_ap[:, hg * 256:(hg + 1) * 256], ALU.mult, [btmpn[r], bg_], [battgs[asl]])

            def att_stage(m, between=None):
                def flush(qb):
                    a = attgs[qb % 2]
                    for c in range(4):
                        tp(pT[:, c, :], a[:, c * 128:(c + 1) * 128], identb[:, :], [battgs[qb % 2], bidb], [bpT], c == 3)
                    acopy(mixedT[:, 4:8, qb * 128:(qb + 1) * 128], pT[:, 0:4, :], [bpT], [bmixed])
                pending = None
                for qb in range(4):
                    if between is not None:
                        between(qb)
                    seq = list(range(8))
                    att_qk(m, qb, 0, 0)
                    att_qk(m, qb, 1, 1)
                    for hh in seq:
                        if hh + 2 < 8:
                            att_qk(m, qb, hh + 2, (hh + 2) % 3)
                        if hh == 0 and pending is not None:
                            flush(pending)
                        att_pv(m, qb, hh, hh % 3)
                        if hh % 4 == 3:
                            att_norm(qb, hh // 4, gatt[:, qb, :], bgatt[qb], asl=qb % 2)
                    pending = qb
                return lambda: flush(pending)

            def out_stage(m, last_flush=None):
                for j in range(4):
                    g = 4 * m + j
                    if j == 3 and last_flush is not None:
                        last_flush()
                    S.dma("sp", xq[:, g % 2, :], xp[g * 128:(g + 1) * 128, :], writes=[bxq[g % 2]])
                    halves = []
                    bbs = []
                    for hf in range(2):
                        bank, bb = nextpb()
                        for kc in range(8):
                            mm(bank[:, :], mixedT[:, kc, j * 128:(j + 1) * 128], W0[:, kc, 3072 + hf * 512:3072 + (hf + 1) * 512], kc == 0, kc == 7,
                               [bmixed] + wb(3072 + hf * 512, 3584 + hf * 512), [bb], kc == 7)
                        halves.append(bank[:, :])
                        bbs.append(bb)
                    post_norm_resid(128, halves, bbs, 36, xq[:, g % 2, :], bxq[g % 2], 0, g % 2, x1d[g * 128:(g + 1) * 128, :], bdram=bx1d[g])

            def chk(n):
                if stage == n:
                    raise StopIteration
            def l0_all():
              chk(10)
              for g in range(4):
                pre_load(g)
              pre_stats(0)
              for g in range(4):
                pre_T(g)
              load_weights(W0, w0in, 3072, 0, 0)
              load_weights(W0, w0out, 1024, 3072, None)
              chk(11)
              for m in range(NMT):
                inf_stage(m)
                chk(12)
                int_stage(m)
                chk(13)
                if m + 1 < NMT:
                    for g in range(4 * (m + 1), 4 * (m + 2)):
                        pre_load(g)
                    pre_stats(m + 1)
                pool_stage(m)
                chk(14)
                if m + 1 < NMT:
                    def btw(qb, m=m):
                        g = 4 * (m + 1) + qb
                        pre_tp(g)
                        if qb < 3:
                            pre_scale(g + 1)
                    pre_scale(4 * (m + 1))
                    lf = att_stage(m, between=btw)
                else:
                    lf = att_stage(m)
                chk(15)
                out_stage(m, lf)
                chk(16)
              sample_l0()

            def sample_l0():
                bxs = bxr[3]
                xs_t = xr[0:NS, 3, :]
                S.dma("sp", xs_t, xs, writes=[bxs])
                norm_stats(xs_t, NS, 32, bxs, bms[9])
                norm_rstd(NS, 32, 33, bms[9], brs[9])
                make_hT(xs_t, NS, 32, bxs, brs[9], 0, hTs[:, :, :], bhTs)
                kTc = qT; bkTc = bqT
                Vc = Vr[:, 0:5, :, :]
                cstb = pooledT; bcstb = bpooled
                ES = E5[:].rearrange("p h d q -> p (h d q)")[:, 0:1280].rearrange("p (h d q) -> p h d q", h=8, d=5); bES = bE5
                S.dma("sp", stg[:, 2, 0:1280], biasS.rearrange("p h d q -> p (h d q)"), writes=[bstg[2], battgs[1]])
                aact(ES.rearrange("p h d q -> p (h d q)"), stg[:, 2, 0:1280], AF.Exp, [bstg[2]], [bES])
                S.dma("sp", cstage[:, :, :], ck.rearrange("(b p) f -> p b f", p=128), writes=[bstg[0], bstg[1], bexpo[2], bPT[2]])
                vcopy(cstb[:], cstage[:], [bstg[0], bstg[1]], [bcstb])
                for blk in range(4):
                    for c in range(4):
                        tp(pT[:, c, :], cstb[:, blk, c * 128:(c + 1) * 128], identb[:, :], [bcstb, bidb], [bpT], c == 3)
                    acopy(kTc[:, :, blk * 128:(blk + 1) * 128], pT[:, 0:4, :], [bpT], [bkTc])
                S.dma("sp", cstage[:, :, :], cv.rearrange("(b p) f -> p b f", p=128), writes=[bstg[0], bstg[1], bexpo[2], bPT[2]])
                vcopy(Vc[:, 0:4, :, 0:64], cstage[:].rearrange("p b (h d) -> p b h d", h=8), [bstg[0], bstg[1]], bV[0:5])
                S.dma("sp", ostg[0:15, 0, :], cpool, writes=[bostg[0]])
                bank, bb = nextpb()
                for g in range(4):
                    tp(bank[:, g * 16:g * 16 + 15], ostg[0:15, 0, g * 128:(g + 1) * 128], identf[0:15, 0:15], [bostg[0], bidf], [bb], g == 3)
                acopy(uTs[:, :, 1:16], bank[:, 0:64].rearrange("p (g t) -> p g t", g=4)[:, :, 0:15], [bb], [buTs])
                for c in range(4):
                    bank, bb = proj_feat(c * 128, NS, hTs, bhTs)
                    acopy(uTs[:, c, 16:48], bank[:, 0:NS], [bb], [buTs])
                for c in range(4):
                    bank, bb = proj_feat(512 + c * 128, NS, hTs, bhTs)
                    vcopy(qTs[:, c, :], bank[:, 0:NS], [bb], [bqTs])
                for c in range(4):
                    bank, bb = proj_feat(1024 + c * 128, NS, hTs, bhTs)
                    vcopy(kTs[:, c, :], bank[:, 0:NS], [bb], [bkTs])
                for c in range(4):
                    bank, bb = proj_feat(2048 + c * 128, NS, hTs, bhTs)
                    aact(gpT[:, c, 0:NS], bank[:, 0:NS], AF.Silu, [bb], [bgpT])
                bank, bb = proj_tok(0, NS, hTs, bhTs)
                acopy(ostg[0:NS, 0, :], bank[0:NS, :], [bb], [bostg[0]])
                S.dma("sp", o_pool_s, ostg[17:32, 0, :], reads=[bostg[0]])
                bank, bb = proj_tok(1024, NS, hTs, bhTs)
                acopy(ostg[0:NS, 1, :], bank[0:NS, :], [bb], [bostg[1]])
                S.dma("sp", o_k_s, ostg[0:NS, 1, :], reads=[bostg[1]])
                bank, bb = proj_tok(1536, NS, hTs, bhTs)
                acopy(ostg[0:NS, 0, :], bank[0:NS, :], [bb], [bostg[0]])
                S.dma("sp", o_v_s, ostg[0:NS, 0, :], reads=[bostg[0]])
                vcopy(Vc[0:NS, 4, :, 0:64], bank[0:NS, :].rearrange("p (h d) -> p h d", h=8), [bb], bV[0:5])
                bank, bb = proj_tok(2560, NS, hTs, bhTs)
                aact(gatt[0:NS, 0, :], bank[0:NS, :], AF.Silu, [bb], [bgatt[0]])
                pool_branch(uTs, buTs, NS, False, pooledT, bpooled, gpT, bgpT, mixedT, bmixed)
                for hh in range(8):
                    buf = hh % 2
                    r0 = (hh % 2) * 64
                    c = hh // 2
                    for blk in range(4):
                        mm(pST[buf][:, 0, blk * NS:(blk + 1) * NS], kTc[r0:r0 + 64, c, blk * 128:(blk + 1) * 128], qTs[r0:r0 + 64, c, :], True, True,
                           [bkTc, bqTs], [bpST[buf]], False)
                    mm(pST[buf][0:NS, 0, 4 * NS:5 * NS], kTs[r0:r0 + 64, c, :], qTs[r0:r0 + 64, c, :], True, True, [bkTs, bqTs], [bpST[buf]], True)
                    aact(expoS[buf][:, 0:4 * NS], pST[buf][:, 0, 0:4 * NS], AF.Exp, [bpST[buf]], [bexpo[buf]], scale=0.125)
                    aact(expoS[buf][0:NS, 4 * NS:5 * NS], pST[buf][0:NS, 0, 4 * NS:5 * NS], AF.Exp, [bpST[buf]], [bexpo[buf]], scale=0.125)
                    vtt(PTS[buf][:, 0:4 * NS], expoS[buf][:, 0:4 * NS], ES[:, hh, 0:4, :].rearrange("p d q -> p (d q)"), ALU.mult, [bexpo[buf], bES], [bPT[buf]])
                    vtt(PTS[buf][0:NS, 4 * NS:5 * NS], expoS[buf][0:NS, 4 * NS:5 * NS], ES[0:NS, hh, 4, :], ALU.mult, [bexpo[buf], bES], [bPT[buf]])
                    hq = hh % 4
                    for blk in range(4):
                        mm(pPV[0:NS, hq * 65:hq * 65 + 65], PTS[buf][:, blk * NS:(blk + 1) * NS], Vc[:, blk, hh, :], blk == 0, False, [bPT[buf]] + bV[0:5], [bpPV], False)
                    mm(pPV[0:NS, hq * 65:hq * 65 + 65], PTS[buf][0:NS, 4 * NS:5 * NS], Vc[0:NS, 4, hh, :], False, True, [bPT[buf]] + bV[0:5], [bpPV], True)
                    if hh % 4 == 3:
                        att_norm(0, hh // 4, gatt[0:NS, 0, :], bgatt[0], nt=NS)
                for c in range(4):
                    tp(pT[:, c, 0:NS], attg[0:NS, c * 128:(c + 1) * 128], identb[0:NS, 0:NS], [battg, bidb], [bpT], c == 3)
                acopy(mixedT[:, 4:8, 0:NS], pT[:, 0:4, 0:NS], [bpT], [bmixed])
                halves = []
                bbs = []
                for hf in range(2):
                    bank, bb = nextpb()
                    for kc in range(8):
                        mm(bank[0:NS, :], mixedT[:, kc, 0:NS], W0[:, kc, 3072 + hf * 512:3072 + (hf + 1) * 512], kc == 0, kc == 7, [bmixed] + wb(3072 + hf * 512, 3584 + hf * 512), [bb], kc == 7)
                    halves.append(bank[0:NS, :])
                    bbs.append(bb)
                post_norm_resid(NS, halves, bbs, 38, xs_t, bxs, 0, 0, None, dst_sb=x1s_t[:, :], bdst=bx1s)

            try:
                l0_all()
            except StopIteration:
                pass

            if stage == 1:
                for g in range(4 * NMT):
                    S.dma("sp", xq[:, g % 2, :], x1d[g * 128:(g + 1) * 128, :], reads=[bx1d[g]], writes=[bxq[g % 2]], owner=bxq[g % 2])
                    S.dma("sp", yp[g * 128:(g + 1) * 128, :], xq[:, g % 2, :], reads=[bxq[g % 2]])
                S.dma("sp", ys, x1s_t[:, :], reads=[bx1s])
            S.barrier()
            S.emit()


        if stage != 1:
          with ExitStack() as st1:
            sb1, _ = mk(st1)
            W1 = sb1("W1", [128, 8, 6152], BF16)
            S.dma("sp", gpost_t[:, 0, :], gpost[:, 1, :], writes=[bgpost])
            load_weights(W1, w1in, 5128, 0, 1, kscale_cols=(1024, 2048, 1.0 / 16.0),
                         chunks=[(5120, 8), (0, 1024), (1024, 1024), (2048, 1024), (3072, 1024), (4096, 1024)], on_pool=False)
            load_weights(W1, w1out, 1024, 5128, None, on_pool=False)
            xr = stg[:, :, 0:D]; bxr = bstg
            hT1a = sb1("hT1", [128, 2, 8, 128], BF16); bhT1a = [Buf("hT1_0"), Buf("hT1_1")]
            qT1 = sb1("qT1", [128, 3, 8, 128], BF16); bqT1 = [Buf("qT1_%d" % i) for i in range(3)]
            kT1 = sb1("kT1", [128, 2, 8, 128], BF16); bkT1 = [Buf("kT1a"), Buf("kT1b")]
            ktok = sb1("ktok", [128, 2, D], BF16); bktok = [Buf("ktoka"), Buf("ktokb")]
            vx = sb1("vx", [128, 3, 4, 257], BF16); bvx = [Buf("vx_%d" % i) for i in range(3)]
            wv = sb1("wv", [128, 4, 257], BF16); bwv = Buf("wv")
            St = stg[:, 3, 1024:1536].bitcast(BF16).rearrange("p (s h t) -> p s h t", s=2, h=4); bSt = [Buf("St0"), Buf("St1")]
            Cf = sb1("Cf", [128, 4, 2, 257]); bCfh = [Buf("Cf%d" % i) for i in range(4)]
            Cb = sb1("Cb", [128, 2, 4, 2, 257], BF16); bCb = [Buf("Cb0"), Buf("Cb1")]
            hc = sb1("hc", [128, D]); bhch = [Buf("hc%d" % i) for i in range(4)]
            junk2 = stg[:, 2, 1024:1280].bitcast(BF16).rearrange("p (a b) -> p a b", a=2); bjunk2 = [Buf("j2a"), Buf("j2b")]
            sg = sb1("sg", [128, 3, D], BF16); bsg = [Buf("sg_%d" % i) for i in range(3)]
            sz = sb1("sz", [128, D], BF16); bsz = Buf("sz")
            outm = stg[:, 1, 1024:1536].bitcast(BF16); boutm = Buf("outm")
            outT = stg[:, 0, 1024:1536].bitcast(BF16).rearrange("p (k t) -> p k t", k=8); boutT = Buf("outT")
            gml_t = sb1("gml_t", [128, D]); bgml = Buf("gml")
            tri_t = sb1("tri_t", [128, 128]); btri = Buf("tri")
            bg_t = sb1("bg_t", [128, 8]); bbg = Buf("bg")
            ones4 = sb1("ones4", [4, 128]); bones4 = Buf("ones4")
            gb = sb1("gb", [128, 2, 8]); bgb = [Buf("gba"), Buf("gbb")]
            g4 = sb1("g4", [128, 2, 16]); bg4 = [Buf("g4a"), Buf("g4b")]
            big4 = sb1("big4", [4, 256]); bbig4 = Buf("big4")
            cbt = sb1("cbt", [4, 2, 256]); bcbt = [Buf("cbta"), Buf("cbtb")]
            sm4 = sb1("sm4", [4, 16]); bsm4 = Buf("sm4")
            mprev = sb1("mprev", [4, 1]); bmprev = Buf("mprev")
            gsb = sb1("gsb", [128, 2, 12]); bgsb = [Buf("gsb0"), Buf("gsb1")]
            st8 = sb1("st8", [128, 4, 8]); bst8 = [Buf("st8_%d" % i) for i in range(4)]
            ctr = hc[:, 0:512].rearrange("p (a b) -> p a b", a=2); bctr = [bhch[0], bhch[1]]

            S.dma("sp", gml_t[:], gml, writes=[bgml])
            S.dma("sp", tri_t[:], tri, writes=[btri])
            S.dma("sp", bg_t[:], bgate, writes=[bbg])
            memset("pool", ones4[:], 1.0, [bones4])
            memset("pool", vx[:, :, :, 256:257], 1.0, bvx)

            bro = [Buf("ro0", True), Buf("ro1", True)]
            pbl.extend([(pST[0][:, 0, :], Buf("pq0", True)), (pST[0][:, 1, :], Buf("pq1", True))])

            def genA(nt, x_ap, bx, col, sl, s3):
                hT1 = hT1a[:, sl, :, :]
                bhT1 = bhT1a[sl]
                norm_stats(x_ap, nt, col, bx, bms[9])
                norm_rstd(nt, col, col + 1, bms[9], brs[9])
                hb = h[0:nt, sl, :]
                vts(hb, x_ap, rs[0:nt, col:col + 1], None, ALU.mult, None, [bx, brs[9]], [bh[sl]])
                yield
                for kc in range(8):
                    tp(pT[:, kc, 0:nt], hb[:, kc * 128:(kc + 1) * 128], identb[0:nt, 0:nt], [bh[sl], bidb], [bpT], kc == 7)
                acopy(hT1[:, :, 0:nt], pT[:, :, 0:nt], [bpT], [bhT1])
                yield
                bank, bb = nextpb()
                for kc in range(8):
                    mm(bank[0:nt, 0:8], hT1[:, kc, 0:nt], W1[:, kc, 5120:5128], kc == 0, kc == 7, wb(5120, 5128) + [bhT1], [bb], kc == 7)
                vtt(gb[0:nt, sl, :], bank[0:nt, 0:8], bg_t[0:nt, :], ALU.add, [bb, bbg], [bgb[sl]])
                aact(g4[0:nt, sl, 0:4], gb[0:nt, sl, 4:8], AF.Exp, [bgb[sl]], [bg4[sl]], scale=-1.0)
                aact(g4[0:nt, sl, 4:8], g4[0:nt, sl, 0:4], AF.Ln, [bg4[sl]], [bg4[sl]], bias=1.0)
                yield
                def tokproj(co):
                    bank, bb = nextpb()
                    for kc in range(8):
                        mm(bank[0:nt, :], hT1[:, kc, 0:nt], W1[:, kc, co:co + 512], kc == 0, kc == 7, wb(co, co + 512) + [bhT1], [bb], kc == 7)
                    return bank, bb
                bank, bb = tokproj(0)
                acopy(sz[0:nt, 0:512], bank[0:nt, :], [bb], [bsz])
                yield
                bank, bb = nextpb()
                mm(bank[0:nt, 0:4], tri_t[0:nt, 0:nt], g4[0:nt, sl, 4:8], True, True, [btri, bg4[sl]], [bb], True)
                vtt(g4[0:nt, sl, 8:12], gb[0:nt, sl, 0:4], bank[0:nt, 0:4], ALU.add, [bgb[sl], bb], [bg4[sl]])
                vcopy(g4[0:nt, sl, 12:16], bank[0:nt, 0:4], [bb], [bg4[sl]])
                yield
                bank, bb = tokproj(512)
                vcopy(sz[0:nt, 512:1024], bank[0:nt, :], [bb], [bsz])
                yield
                bank, bb = nextpb()
                tp(bank[0:4, 0:nt], g4[0:nt, sl, 8:12], identf[0:nt, 0:nt], [bg4[sl], bidf], [bb], False)
                tp(bank[0:4, 128:128 + nt], g4[0:nt, sl, 12:16], identf[0:nt, 0:nt], [bg4[sl], bidf], [bb], True)
                vcopy(cbt[:, sl, 0:256], bank[0:4, 0:256], [bb], [bcbt[sl]])
                for kc in range(8):
                    tp(pT[:, kc, 0:nt], sz[0:nt, kc * 128:(kc + 1) * 128], identb[0:nt, 0:nt], [bsz, bidb], [bpT], kc == 7)
                vcopy(qT1[:, s3, :, 0:nt], pT[:, :, 0:nt], [bpT], [bqT1[s3]])
                yield
                for half in range(2):
                    bank, bb = tokproj(1024 + half * 512)
                    acopy(ktok[0:nt, sl, half * 512:(half + 1) * 512], bank[0:nt, :], [bb], [bktok[sl]])
                    yield
                for half in range(2):
                    bank, bb = tokproj(2048 + half * 512)
                    vcopy(vx[0:nt, s3, half * 2:half * 2 + 2, 0:256], bank[0:nt, :].rearrange("p (h d) -> p h d", h=2), [bb], [bvx[s3]])
                    if half == 0:
                        for kc in range(8):
                            tp(pT[:, kc, 0:nt], ktok[0:nt, sl, kc * 128:(kc + 1) * 128], identb[0:nt, 0:nt], [bktok[sl], bidb], [bpT], kc == 7)
                        acopy(kT1[:, sl, :, 0:nt], pT[:, :, 0:nt], [bpT], [bkT1[sl]])
                    yield
                for half in range(2):
                    bank, bb = tokproj(3072 + half * 512)
                    aact(sg[0:nt, s3, half * 512:(half + 1) * 512], bank[0:nt, :], AF.Sigmoid, [bb], [bsg[s3]])
                    yield
                for half in range(2):
                    bank, bb = tokproj(4096 + half * 512)
                    aact(sz[0:nt, half * 512:(half + 1) * 512], bank[0:nt, :], AF.Silu, [bb], [bsz])
                    yield
                vtt(sg[0:nt, s3, :], sg[0:nt, s3, :], sz[0:nt, :], ALU.mult, [bsg[s3], bsz], [bsg[s3]], eng="pool")
                vtt(sg[0:nt, s3, :], sg[0:nt, s3, :], gml_t[0:nt, :], ALU.mult, [bsg[s3], bgml], [bsg[s3]], eng="pool")
                yield

            def genBe(nt, sl, s3):
                G = gsb[:, sl, :]
                S.op("dve", lambda e: e.tensor_reduce(out=sm4[:, 0:1], in_=cbt[:, sl, 0:nt], axis=mybir.AxisListType.X, op=ALU.max), [bcbt[sl]], [bsm4])
                vtt(sm4[:, 1:2], sm4[:, 0:1], mprev[:, :], ALU.max, [bsm4, bmprev], [bsm4])
                vts(sm4[:, 2:3], sm4[:, 1:2], -1.0, None, ALU.mult, None, [bsm4], [bsm4])
                aact(sm4[:, 3:4], mprev[:, :], AF.Exp, [bmprev, bsm4], [bsm4], bias=sm4[:, 2:3])
                aact(big4[:, 0:nt], cbt[:, sl, 0:nt], AF.Exp, [bcbt[sl], bsm4], [bbig4], bias=sm4[:, 2:3])
                aact(big4[:, 128:128 + nt], cbt[:, sl, 128:128 + nt], AF.Exp, [bcbt[sl], bsm4], [bbig4], bias=sm4[:, 2:3])
                vtt(mprev[:, :], sm4[:, 1:2], cbt[:, sl, 128 + nt - 1:128 + nt], ALU.subtract, [bsm4, bcbt[sl]], [bmprev])
                vts(sm4[:, 4:8], identf[0:4, 0:4], sm4[:, 3:4], None, ALU.mult, None, [bidf, bsm4], [bsm4])
                yield
                bank, bb = nextpb()
                mm(bank[0:nt, 0:4], big4[:, 0:nt], identf[0:4, 0:4], True, True, [bbig4, bidf], [bb], False)
                mm(bank[0:nt, 4:8], big4[:, 128:128 + nt], identf[0:4, 0:4], True, True, [bbig4, bidf], [bb], False)
                mm(bank[:, 8:12], ones4[:, :], sm4[:, 4:8], True, True, [bones4, bsm4], [bb], True)
                vcopy(G[0:nt, 0:8], bank[0:nt, 0:8], [bb], [bgsb[sl]])
                vcopy(G[:, 8:12], bank[:, 8:12], [bb], [bgsb[sl]])
                yield
                for hh in range(4):
                    aact(Cf[:, hh, :, :], Cf[:, hh, :, :], AF.Copy, [bCfh[hh], bgsb[sl]], [bCfh[hh]], scale=G[:, 8 + hh:9 + hh])
                    vcopy(Cb[:, sl, hh, :, :], Cf[:, hh, :, :], [bCfh[hh]], [bCb[sl]], eng="pool")
                S.op("dve", lambda e: e.tensor_tensor(out=wv[0:nt, :, :], in0=vx[0:nt, s3, :, :], in1=G[0:nt, 0:4].unsqueeze(2).to_broadcast([nt, 4, 257]), op=ALU.mult),
                     [bvx[s3], bgsb[sl]], [bwv])
                yield
                for hh in range(4):
                    for c in range(2):
                        bank, bb = nextpb()
                        mm(bank[:, 0:257], ktok[0:nt, sl, hh * 256 + c * 128:hh * 256 + (c + 1) * 128], wv[0:nt, hh, :], True, True, [bktok[sl], bwv], [bb], True)
                        vtt(Cf[:, hh, c, :], bank[:, 0:257], Cf[:, hh, c, :], ALU.add, [bCfh[hh], bb], [bCfh[hh]])
                    if hh % 2 == 1:
                        yield
                for hh in range(4):
                    for c in range(2):
                        mm(pPV[0:nt, hh * 128:hh * 128 + nt], kT1[:, sl, 2 * hh + c, 0:nt], qT1[:, s3, 2 * hh + c, 0:nt], c == 0, c == 1, [bkT1[sl], bqT1[s3]], [bpPV], hh == 3 and c == 1)
                for hh in range(4):
                    vstt(St[0:nt, sl, hh, 0:nt], pPV[0:nt, hh * 128:hh * 128 + nt], G[0:nt, hh:hh + 1], tri_t[0:nt, 0:nt], ALU.mult, ALU.mult, [bpPV, bgsb[sl], btri], [bSt[sl]])
                yield

            def genBl(nt, x_ap, bx, sl, s3, dst_dram):
                G = gsb[:, sl, :]
                for hh in range(4):
                    ro = pST[1][0:nt, hh % 2, 0:257]
                    for c in range(2):
                        mm(ro, qT1[:, s3, 2 * hh + c, 0:nt], Cb[:, sl, hh, c, :], c == 0, False, [bqT1[s3], bCb[sl]], [bro[hh % 2]], False)
                    mm(ro, St[0:nt, sl, hh, 0:nt], vx[0:nt, s3, hh, :], False, True, [bSt[sl], bvx[s3]], [bro[hh % 2]], True)
                    b8 = [bst8[hh]]
                    vcopy(st8[0:nt, hh, 2:3], ro[:, 256:257], [bro[hh % 2]], b8)
                    vstt(st8[0:nt, hh, 0:1], st8[0:nt, hh, 2:3], -1.0, st8[0:nt, hh, 2:3], ALU.mult, ALU.max, b8, b8)
                    vtt(st8[0:nt, hh, 0:1], st8[0:nt, hh, 0:1], G[0:nt, 4 + hh:5 + hh], ALU.max, b8 + [bgsb[sl]], b8)
                    vrecip(st8[0:nt, hh, 1:2], st8[0:nt, hh, 0:1], b8, b8)
                    aact(hc[0:nt, hh * 256:(hh + 1) * 256], ro[:, 0:256], AF.Copy, [bro[hh % 2]] + b8, [bhch[hh]] + b8, scale=st8[0:nt, hh, 1:2], accum=st8[0:nt, hh, 3:4])
                    aact(junk2[0:nt, hh % 2, :], hc[0:nt, hh * 256:(hh + 1) * 256], AF.Square, [bhch[hh]], [bjunk2[hh % 2]] + b8, accum=st8[0:nt, hh, 4:5])
                    yield
                vts(st8[0:nt, :, 5], st8[0:nt, :, 3], 1.0 / 256.0, None, ALU.mult, None, bst8, bst8)
                vtt(st8[0:nt, :, 6], st8[0:nt, :, 5], st8[0:nt, :, 5], ALU.mult, bst8, bst8)
                vts(st8[0:nt, :, 4], st8[0:nt, :, 4], 1.0 / 256.0, None, ALU.mult, None, bst8, bst8)
                vtt(st8[0:nt, :, 7], st8[0:nt, :, 4], st8[0:nt, :, 6], ALU.subtract, bst8, bst8)
                aact(st8[0:nt, :, 7], st8[0:nt, :, 7], AF.Ln, bst8, bst8, bias=EPS)
                aact(st8[0:nt, :, 7], st8[0:nt, :, 7], AF.Exp, bst8, bst8, scale=-0.5)
                for hh in range(4):
                    vts(hc[0:nt, hh * 256:(hh + 1) * 256], hc[0:nt, hh * 256:(hh + 1) * 256], st8[0:nt, hh, 5:6], st8[0:nt, hh, 7:8],
                        ALU.subtract, ALU.mult, [bhch[hh], bst8[hh]], [bhch[hh]])
                vtt(outm[0:nt, :], hc[0:nt, :], sg[0:nt, s3, :], ALU.mult, bhch + [bsg[s3]], [boutm])
                yield
                for kc in range(8):
                    tp(pT[:, kc, 0:nt], outm[0:nt, kc * 128:(kc + 1) * 128], identb[0:nt, 0:nt], [boutm, bidb], [bpT], kc == 7)
                acopy(outT[:, :, 0:nt], pT[:, :, 0:nt], [bpT], [boutT])
                yield
                halves = []
                bbs = []
                for hf in range(2):
                    bank, bb = nextpb()
                    for kc in range(8):
                        mm(bank[0:nt, :], outT[:, kc, 0:nt], W1[:, kc, 5128 + hf * 512:5128 + (hf + 1) * 512], kc == 0, kc == 7, [boutT] + wb(5128 + hf * 512, 5640 + hf * 512), [bb], kc == 7)
                    halves.append(bank[0:nt, :])
                    bbs.append(bb)
                post_norm_resid(nt, halves, bbs, 36, x_ap, bx, 1, 0, dst_dram)
                yield

            def drive(gens, order=None, nodrain=()):
                live = {i: g for i, g in enumerate(gens) if g is not None}
                for i in (order or ()):
                    if i in live:
                        try:
                            next(live[i])
                        except StopIteration:
                            del live[i]
                while [i for i in live if i not in nodrain]:
                    for i in sorted(live):
                        if i in nodrain:
                            continue
                        try:
                            next(live[i])
                        except StopIteration:
                            del live[i]

            def state_out(oC, on, om):
                stage = [hc[:, :].rearrange("p (h vb k) -> p h vb k", h=2, vb=2), ysb[:, 1, :].rearrange("p (h vb k) -> p h vb k", h=2, vb=2)]
                bst = [bhch, [bysb[1]]]
                for hh in range(4):
                    sg_ = stage[hh // 2]
                    for vb in range(2):
                        bank, bb = nextpb()
                        for c in range(2):
                            tp(bank[:, c * 128:(c + 1) * 128], Cf[:, hh, c, vb * 128:(vb + 1) * 128], identf[:, :], bCfh + [bidf], [bb], c == 1)
                        if vb == 0:
                            acopy(sg_[:, hh % 2, vb, :], bank[:, 0:256], [bb], bst[hh // 2])
                        else:
                            vcopy(sg_[:, hh % 2, vb, :], bank[:, 0:256], [bb], bst[hh // 2])
                    S.dma("sp", oC[hh].rearrange("(vb p) k -> p vb k", p=128), sg_[:, hh % 2, :, :], reads=bst[hh // 2], owner=bst[hh // 2][0])
                    for c in range(2):
                        S.dma("sp", on[hh:hh + 1, c * 128:(c + 1) * 128].rearrange("a k -> k a"), Cf[:, hh, c, 256:257], reads=bCfh, owner=bst[hh // 2][0])
                S.dma("sp", om, mprev[:, :], reads=[bmprev], owner=bysb[1])

            memset("pool", Cf[:], 0.0, bCfh)
            memset("pool", mprev[:], 0.0, [bmprev])
            NCH = 4 * NMT

            def mkA(g):
                if g >= NCH:
                    return None
                slot = g % 4
                S.dma("sp", xr[:, slot, :], x1d[g * 128:(g + 1) * 128, :], reads=[bx1d[g]], writes=[bxr[slot]], owner=bxr[slot])
                return genA(128, xr[:, slot, :], bxr[slot], 32, g % 2, g % 3)

            def mkBe(g):
                return genBe(128, g % 2, g % 3) if g < NCH else None

            def mkBl(g):
                if g < 0:
                    return None
                slot = g % 4
                return genBl(128, xr[:, slot, :], bxr[slot], g % 2, g % 3, yp[g * 128:(g + 1) * 128, :])
            drive([mkA(0)])
            A_next = mkA(1)
            if A_next is not None:
                next(A_next)
                next(A_next)
            for g in range(NCH + 1):
                A_after = mkA(g + 2)
                if g == NCH:
                    A_next = genA(NS, x1s_t[:, :], bx1s, 33, 0, 0)
                drive([mkBl(g - 1), mkBe(g), A_next, A_after],
                      order=[1, 0, 2, 0, 2, 1, 0, 2, 1, 0, 2, 3, 2, 1, 0, 2, 1, 3, 2, 1, 2, 0, 2, 2, 0, 2, 2, 2, 2], nodrain=(3,))
                A_next = A_after
            state_out(o_C_p, o_n_p, o_m_p)
            for hh in range(4):
                c0 = xr[:, hh, 0:512].rearrange("p (vb k) -> p vb k", vb=2)
                S.dma("sp", c0, sC[hh].rearrange("(vb p) k -> p vb k", p=128), writes=[bxr[hh]], owner=bxr[hh])
            for hh in range(4):
                c0 = xr[:, hh, 0:512].rearrange("p (vb k) -> p vb k", vb=2)
                for c in range(2):
                    bank, bb = nextpb()
                    for vb in range(2):
                        tp(bank[:, vb * 128:(vb + 1) * 128], c0[:, vb, c * 128:(c + 1) * 128], identf[:, :], [bxr[hh], bidf], [bb], vb == 1)
                    if c == 0:
                        acopy(Cf[:, hh, c, 0:256], bank[:, 0:256], [bb], [bCfh[hh]])
                    else:
                        vcopy(Cf[:, hh, c, 0:256], bank[:, 0:256], [bb], [bCfh[hh]])
                    S.dma("sp", Cf[:, hh, c, 256:257], sn[hh:hh + 1, c * 128:(c + 1) * 128].rearrange("a k -> k a"), writes=[bCfh[hh]], owner=bysb[1])
            S.dma("sp", mprev[:, :], sm, writes=[bmprev], owner=bysb[1])
            drive([genBe(NS, 0, 0)])
            drive([genBl(NS, x1s_t[:, :], bx1s, 0, 0, ys)])
            state_out(o_C_s, o_n_s, o_m_s)
            S.barrier()
            S.emit()

        S.barrier()
        S.emit()
    return nc


_CACHE = {}


def _host_consts(rel_bias):
    tab = np.asarray(rel_bias[0], np.float32)
    k = np.arange(128)[:, None, None]
    d = np.arange(5)[None, :, None]
    q = np.arange(128)[None, None, :]
    idx = np.clip(128 * d + q - k, -128, 128) + 128
    biasT = np.ascontiguousarray(tab[:, idx].transpose(1, 0, 2, 3))
    mask5 = np.ones((128, 5, 128), np.float32)
    mask5[64:, 0, :64] = 0.0
    mask5[:64, 4, 64:] = 0.0
    qs = np.arange(NS)[None, None, :]
    blk = np.arange(5)[None, :, None]
    kk = np.arange(128)[:, None, None]
    rel = np.where(blk < 4, 512 + qs - (128 * blk + kk), qs - kk)
    idxs = np.clip(rel, -128, 128) + 128
    biasS = np.ascontiguousarray(tab[:, idxs].transpose(1, 0, 2, 3))
    corr = np.ones((128, 4, 16), np.float32)
    for g, w in enumerate(POOLW):
        t = np.arange(16)
        corr[:, g, :] = w / np.minimum(t + 1, w)
    return biasT, mask5, biasS, corr


def _relayout_w(w):
    k, n = w.shape
    return np.ascontiguousarray(w.reshape(8, 128, n).transpose(1, 0, 2))


def kernel(x_prompt, x_sample, cache_pool, cache_k, cache_v, state_C, state_n, state_m,
           norm_pre, norm_post, w_in_even, w_pool_mix, pool_scale, rel_bias, w_out_even,
           w_in_odd, b_gate_odd, mlstm_norm, w_out_odd, _stage=2):
    f = lambda a: np.ascontiguousarray(np.asarray(a, np.float32))
    if ("nc", _stage) not in _CACHE:
        _CACHE[("nc", _stage)] = build_nc(_stage)
    nc = _CACHE[("nc", _stage)]
    biasT, mask5, biasS, corr = _host_consts(f(rel_bias))
    tri = np.triu(np.ones((128, 128), np.float32))
    sel = np.zeros((4, 4, 128), np.float32)
    for hh in range(4):
        sel[hh, hh, :] = 1.0
    shared = {
        "gpre": np.ascontiguousarray(f(norm_pre).reshape(2, 8, 128).transpose(2, 0, 1)),
        "gpost": np.ascontiguousarray(np.broadcast_to(f(norm_post)[None], (128, 2, D))),
        "w0in": _relayout_w(f(w_in_even)[0]), "w0out": _relayout_w(f(w_out_even)[0]),
        "wmix": np.ascontiguousarray(f(w_pool_mix)[0].transpose(1, 0, 2)),
        "pscale": np.ascontiguousarray(f(pool_scale)[0].reshape(4, 128).T),
        "biasT": biasT, "mask5": mask5, "biasS": biasS, "corr": corr,
        "ident": np.eye(128, dtype=np.float32),
        "w1in": _relayout_w(f(w_in_odd)[0]), "w1out": _relayout_w(f(w_out_odd)[0]),
        "bgate": np.ascontiguousarray(np.broadcast_to(f(b_gate_odd)[0][None, :], (128, 8))),
        "gml": np.ascontiguousarray(np.broadcast_to(f(mlstm_norm)[0][None], (128, D))),
        "tri": tri, "sel": sel,
    }
    in_maps = []
    for c in range(8):
        m = dict(shared)
        m.update({
            "xp": f(x_prompt[c]), "xs": f(x_sample[c]),
            "cpool": f(cache_pool[0, c]), "ck": f(cache_k[0, c]).reshape(512, 512), "cv": f(cache_v[0, c]).reshape(512, 512),
            "sC": f(state_C[0, c]), "sn": f(state_n[0, c]), "sm": f(state_m[0, c]).reshape(4, 1),
        })
        in_maps.append(m)
    res = run_bass_kernel_spmd(nc, in_maps, core_ids=list(range(8)))
    R = res.results

    def gather(name, shape):
        return np.stack([np.asarray(r[name], np.float32).reshape(shape) for r in R], 0)
    y_p = gather("yp", (T, D)); y_s = gather("ys", (NS, D))
    pool_p = gather("pool_p", (15, 512))[None]
    k_p = gather("k_p", (512, 8, 64))[None]; v_p = gather("v_p", (512, 8, 64))[None]
    C_p = gather("C_p", (4, 256, 256))[None]; n_p = gather("n_p", (4, 256))[None]; m_p = gather("m_p", (4,))[None]
    pool_s = gather("pool_s", (15, 512))[None]
    k_s = gather("k_s", (NS, 8, 64))[None]; v_s = gather("v_s", (NS, 8, 64))[None]
    C_s = gather("C_s", (4, 256, 256))[None]; n_s = gather("n_s", (4, 256))[None]; m_s = gather("m_s", (4,))[None]
    return (y_p, y_s, pool_p, k_p, v_p, C_p, n_p, m_p, pool_s, k_s, v_s, C_s, n_s, m_s)
```
